# Optimizing a Trainium2 kernel written in Bass

```python
import jax, jax.numpy as jnp
from jax import lax
import numpy as np

D_MODEL = 1024
BATCH = 4
SEQ = 4096
DEPTH = 1
DEC_BATCH = 8
DEC_SEQ = 64
PAST_LEN = 2048

CHUNK = 64
HEAD_DIM = 64
D_MIX = D_MODEL
FOX_HEADS = 8
FOX_W = FOX_HEADS * HEAD_DIM
POOL_GROUPS = 4
POOL_W = D_MIX // 4
POOL_GC = POOL_W // POOL_GROUPS
POOL_WINDOWS = (2, 4, 8, 16)
POOL_STATE = max(POOL_WINDOWS) - 1
MEM_HEADS = 4
MEM_W = MEM_HEADS * HEAD_DIM
MEM_TOKENS = 256
QBLK = 128
EPS = 1e-6
NEG = -1e30
SCALE = HEAD_DIM ** -0.5
IN_SIZES = (FOX_W, FOX_W, FOX_W, FOX_W, FOX_HEADS, POOL_W, POOL_W, MEM_W, MEM_W)
IN_SPLITS = tuple(int(i) for i in np.cumsum(IN_SIZES)[:-1])
D_IN = int(sum(IN_SIZES))

kernel_name = "hymba_fox_pool_mem_stream_step"


def rmsnorm(x, g):
    xf = x.astype(jnp.float32)
    xf = xf * lax.rsqrt(jnp.mean(xf * xf, axis=-1, keepdims=True) + EPS)
    return (xf * g.astype(jnp.float32)).astype(x.dtype)


def project(x, g_norm, w_in, b_f):
    B, S, _ = x.shape
    h = rmsnorm(x, g_norm) @ w_in
    q, k, v, gf, zf, u, gp, qm, gm = jnp.split(h, IN_SPLITS, axis=-1)
    q = q.reshape(B, S, FOX_HEADS, HEAD_DIM)
    k = k.reshape(B, S, FOX_HEADS, HEAD_DIM)
    v = v.reshape(B, S, FOX_HEADS, HEAD_DIM)
    logf = jax.nn.log_sigmoid(zf.astype(jnp.float32) + b_f.astype(jnp.float32))
    qm = qm.reshape(B, S, MEM_HEADS, HEAD_DIM)
    return q, k, v, gf, logf, u, gp, qm, gm


def fox_attend(q, k, v, Fq, Fk, qpos, kpos):
    s = jnp.einsum('bqhd,bkhd->bhqk', q, k).astype(jnp.float32) * SCALE
    bias = jnp.transpose(Fq, (0, 2, 1))[..., :, None] - jnp.transpose(Fk, (0, 2, 1))[..., None, :]
    s = jnp.where(kpos[None, :] <= qpos[:, None], s + bias, NEG)
    p = jax.nn.softmax(s, axis=-1).astype(v.dtype)
    return jnp.einsum('bhqk,bkhd->bqhd', p, v)


def pool_mix(u_ext, n_prefix, pos0, w_pool, pool_scale):
    B, L, C = u_ext.shape
    S = L - n_prefix
    uf = u_ext.astype(jnp.float32)
    cs0 = jnp.concatenate([jnp.zeros((B, 1, C), jnp.float32), jnp.cumsum(uf, axis=1)], axis=1)
    hi = n_prefix + 1 + jnp.arange(S)
    pos = pos0 + jnp.arange(S)
    parts = []
    for gi, w in enumerate(POOL_WINDOWS):
        sl = slice(gi * POOL_GC, (gi + 1) * POOL_GC)
        lo = jnp.maximum(hi - w, 0)
        wsum = cs0[:, hi, sl] - cs0[:, lo, sl]
        cnt = jnp.minimum(w, pos + 1).astype(jnp.float32)
        parts.append(wsum / cnt[None, :, None])
    pooled = jnp.concatenate(parts, axis=-1) - uf[:, n_prefix:]
    pooled = pooled.astype(u_ext.dtype).reshape(B, S, POOL_GROUPS, POOL_GC)
    out = jnp.einsum('bsgc,gcd->bsgd', pooled, w_pool).reshape(B, S, POOL_W)
    return out * pool_scale


def mem_kv(mem, g_mem, w_mem_kv):
    B, M, _ = mem.shape
    kv = rmsnorm(mem, g_mem) @ w_mem_kv
    mk, mv = jnp.split(kv, 2, axis=-1)
    return mk.reshape(B, M, MEM_HEADS, HEAD_DIM), mv.reshape(B, M, MEM_HEADS, HEAD_DIM)


def mem_attend(qm, mk, mv):
    s = jnp.einsum('bqhd,bkhd->bhqk', qm, mk).astype(jnp.float32) * SCALE
    p = jax.nn.softmax(s, axis=-1).astype(mv.dtype)
    return jnp.einsum('bhqk,bkhd->bqhd', p, mv)


def merge(x, fox_o, gf, pool_o, gp, mem_o, gm, w_out):
    B, S, _ = x.shape
    cat = jnp.concatenate([
        jax.nn.silu(gf) * fox_o.reshape(B, S, FOX_W),
        jax.nn.silu(gp) * pool_o,
        jax.nn.silu(gm) * mem_o.reshape(B, S, MEM_W)], axis=-1)
    return x + cat @ w_out


def layer_prompt(x, mem, g_norm, w_in, b_f, w_pool, pool_scale, g_mem, w_mem_kv, w_out):
    B, S, _ = x.shape
    q, k, v, gf, logf, u, gp, qm, gm = project(x, g_norm, w_in, b_f)
    F = jnp.cumsum(logf, axis=1)
    kpos = jnp.arange(S)

    def block(i):
        st = i * QBLK
        qb = lax.dynamic_slice_in_dim(q, st, QBLK, axis=1)
        Fqb = lax.dynamic_slice_in_dim(F, st, QBLK, axis=1)
        return fox_attend(qb, k, v, Fqb, F, st + jnp.arange(QBLK), kpos)

    fox_o = lax.map(block, jnp.arange(S // QBLK))
    fox_o = jnp.transpose(fox_o, (1, 0, 2, 3, 4)).reshape(B, S, FOX_HEADS, HEAD_DIM)
    pool_o = pool_mix(u, 0, 0, w_pool, pool_scale)
    mk, mv = mem_kv(mem, g_mem, w_mem_kv)
    mem_o = mem_attend(qm, mk, mv)
    y = merge(x, fox_o, gf, pool_o, gp, mem_o, gm, w_out)
    return y, k, v, logf.astype(x.dtype), mk, mv, u[:, S - POOL_STATE:]


def layer_sample(x, ck, cv, clogf, cmk, cmv, cpool, g_norm, w_in, b_f, w_pool, pool_scale, w_out):
    B, S, _ = x.shape
    P = ck.shape[1]
    q, k, v, gf, logf, u, gp, qm, gm = project(x, g_norm, w_in, b_f)
    k_all = jnp.concatenate([ck, k], axis=1)
    v_all = jnp.concatenate([cv, v], axis=1)
    F_all = jnp.cumsum(jnp.concatenate([clogf.astype(jnp.float32), logf], axis=1), axis=1)
    fox_o = fox_attend(q, k_all, v_all, F_all[:, P:], F_all, P + jnp.arange(S), jnp.arange(P + S))
    u_ext = jnp.concatenate([cpool.astype(u.dtype), u], axis=1)
    pool_o = pool_mix(u_ext, POOL_STATE, P, w_pool, pool_scale)
    mem_o = mem_attend(qm, cmk, cmv)
    y = merge(x, fox_o, gf, pool_o, gp, mem_o, gm, w_out)
    return y, k, v, logf.astype(x.dtype), u_ext[:, S:]


def setup_inputs(seed: int = 0) -> dict:
    key = jax.random.key(seed)
    ks = jax.random.split(key, 20)
    nrm = jax.random.normal
    f32 = jnp.float32
    return {
        "x_prompt": nrm(ks[0], (BATCH, SEQ, D_MODEL), f32),
        "x_sample": nrm(ks[1], (DEC_BATCH, DEC_SEQ, D_MODEL), f32),
        "mem_prompt": nrm(ks[2], (BATCH, MEM_TOKENS, D_MODEL), f32),
        "cache_fox_k": nrm(ks[3], (DEPTH, DEC_BATCH, PAST_LEN, FOX_HEADS, HEAD_DIM), f32),
        "cache_fox_v": nrm(ks[4], (DEPTH, DEC_BATCH, PAST_LEN, FOX_HEADS, HEAD_DIM), f32),
        "cache_fox_logf": jax.nn.log_sigmoid(3.0 + nrm(ks[5], (DEPTH, DEC_BATCH, PAST_LEN, FOX_HEADS), f32)),
        "cache_mem_k": nrm(ks[6], (DEPTH, DEC_BATCH, MEM_TOKENS, MEM_HEADS, HEAD_DIM), f32),
        "cache_mem_v": nrm(ks[7], (DEPTH, DEC_BATCH, MEM_TOKENS, MEM_HEADS, HEAD_DIM), f32),
        "state_pool": nrm(ks[8], (DEPTH, DEC_BATCH, POOL_STATE, POOL_W), f32),
        "g_norm": 1.0 + 0.02 * nrm(ks[9], (DEPTH, D_MODEL), f32),
        "w_in": nrm(ks[10], (DEPTH, D_MODEL, D_IN), f32) * D_MODEL ** -0.5,
        "b_f": jax.random.uniform(ks[11], (DEPTH, FOX_HEADS), f32, 1.0, 5.0),
        "w_pool": nrm(ks[12], (DEPTH, POOL_GROUPS, POOL_GC, POOL_GC), f32) * POOL_GC ** -0.5,
        "pool_scale": 0.5 + 0.1 * nrm(ks[13], (DEPTH, POOL_W), f32),
        "g_mem": 1.0 + 0.02 * nrm(ks[14], (DEPTH, D_MODEL), f32),
        "w_mem_kv": nrm(ks[15], (DEPTH, D_MODEL, 2 * MEM_W), f32) * D_MODEL ** -0.5,
        "w_out": nrm(ks[16], (DEPTH, D_MIX, D_MODEL), f32) * D_MIX ** -0.5,
        "g_final": 1.0 + 0.02 * nrm(ks[17], (D_MODEL,), f32),
    }


def reference(x_prompt, x_sample, mem_prompt, cache_fox_k, cache_fox_v, cache_fox_logf,
              cache_mem_k, cache_mem_v, state_pool, g_norm, w_in, b_f, w_pool, pool_scale,
              g_mem, w_mem_kv, w_out, g_final):
    hp, hs = x_prompt, x_sample
    kp_l, vp_l, lfp_l, mkp_l, mvp_l, pp_l = [], [], [], [], [], []
    ks_l, vs_l, lfs_l, ps_l = [], [], [], []
    for l in range(DEPTH):
        hp, kp, vp, lfp, mkp, mvp, pp = layer_prompt(
            hp, mem_prompt, g_norm[l], w_in[l], b_f[l], w_pool[l], pool_scale[l],
            g_mem[l], w_mem_kv[l], w_out[l])
        hs, ksn, vsn, lfs, psn = layer_sample(
            hs, cache_fox_k[l], cache_fox_v[l], cache_fox_logf[l], cache_mem_k[l], cache_mem_v[l],
            state_pool[l], g_norm[l], w_in[l], b_f[l], w_pool[l], pool_scale[l], w_out[l])
        kp_l.append(kp); vp_l.append(vp); lfp_l.append(lfp)
        mkp_l.append(mkp); mvp_l.append(mvp); pp_l.append(pp)
        ks_l.append(ksn); vs_l.append(vsn); lfs_l.append(lfs); ps_l.append(psn)
    y_prompt = rmsnorm(hp, g_final)
    y_sample = rmsnorm(hs, g_final)
    return (y_prompt, y_sample,
            jnp.stack(kp_l), jnp.stack(vp_l), jnp.stack(lfp_l),
            jnp.stack(mkp_l), jnp.stack(mvp_l), jnp.stack(pp_l),
            jnp.stack(ks_l), jnp.stack(vs_l), jnp.stack(lfs_l), jnp.stack(ps_l))
```

```python
import contextlib
import numpy as np
import concourse.bass as bass
import concourse.mybir as mybir
from concourse.bass_utils import run_bass_kernel_spmd

F32 = mybir.dt.float32
BF16 = mybir.dt.bfloat16
AF = mybir.ActivationFunctionType
ALU = mybir.AluOpType

NB_OWN = 16
NB_ALL = 32
NT = 4
S_OWN = 2048
DEC = 64
NCB = 16
EPS = 1e-6
STOP_AFTER = 99
import os
STOP_AT = float(os.environ.get('KSTOP', '99'))


class StopBuild(Exception):
    pass


class Buf:
    __slots__ = ("w", "r", "excl")

    def __init__(self, excl=False):
        self.w = None
        self.r = {}
        self.excl = excl


def PB():
    return Buf(excl=True)


class DSem:
    def __init__(self, sem, key):
        self.sem = sem
        self.key = key
        self.n = 0


class KB:
    def __init__(self, nc, es):
        self.nc = nc
        self.es = es
        self.engs = {"pe": nc.tensor, "act": nc.scalar, "dve": nc.vector, "pool": nc.gpsimd, "sp": nc.sync}
        self.sem = {k: es.enter_context(nc.semaphore("s_" + k)) for k in self.engs}
        self.cnt = {k: 0 for k in self.engs}
        self.seen = {k: {} for k in self.engs}
        self.pend = {k: ([], []) for k in self.engs}
        self.dsems = []
        self.nd = 0
        self.dead = False
        self.nops = 0
        self.limit = int(os.environ.get('KLIMIT', '0'))
        self.trace = []

    def dsem(self):
        self.nd += 1
        d = DSem(self.es.enter_context(self.nc.semaphore("d%d" % self.nd)), "d%d" % self.nd)
        self.dsems.append(d)
        return d

    def _wait(self, eng, ev):
        if ev is None:
            return
        sem, val, key = ev
        if self.seen[eng].get(key, 0) >= val:
            return
        self.seen[eng][key] = val
        self.engs[eng].wait_ge(sem, val)

    def _deps(self, eng, reads, writes):
        for b in reads:
            self._wait(eng, b.w)
        for b in writes:
            self._wait(eng, b.w)
            for ev in list(b.r.values()):
                self._wait(eng, ev)

    @staticmethod
    def _commit(ev, reads, writes):
        for b in reads:
            b.r[ev[2]] = ev
        for b in writes:
            b.w = ev
            b.r = {}

    def _tick(self, what):
        self.nops += 1
        if self.limit:
            self.trace.append(what)
        if self.limit and self.nops >= self.limit and (what[0] != "pe" or what[1]):
            self.barrier()
            self.dead = True
            print("KLIMIT stop at", self.nops, self.trace[-6:])

    def op(self, eng, fn, reads=(), writes=(), mark=True):
        if self.dead:
            return None
        if any(b.excl for b in reads):
            writes = list(writes) + [b for b in reads if b.excl]
            reads = [b for b in reads if not b.excl]
        self._deps(eng, reads, writes)
        ins = fn(self.engs[eng])
        pr, pw = self.pend[eng]
        pr.extend(reads)
        pw.extend(writes)
        if not mark:
            return None
        self.cnt[eng] += 1
        ins.then_inc(self.sem[eng], 1)
        ev = (self.sem[eng], self.cnt[eng], eng)
        self._commit(ev, pr, pw)
        self.pend[eng] = ([], [])
        import traceback
        self._tick((eng, mark, traceback.extract_stack(limit=3)[0].lineno))
        return ev

    def dma(self, ds, fn, reads=(), writes=(), q="sp"):
        if self.dead:
            return None
        self._deps(q, reads, writes)
        ins = fn(self.engs[q])
        ds.n += 16
        ins.then_inc(ds.sem, 16)
        ev = (ds.sem, ds.n, ds.key)
        self._commit(ev, reads, writes)
        import traceback
        self._tick(("dma", True, traceback.extract_stack(limit=3)[0].lineno))
        return ev

    def barrier(self):
        if self.dead:
            return
        evs = [(self.sem[k], self.cnt[k], k) for k in self.engs if self.cnt[k] > 0]
        for d in self.dsems:
            if d.n > 0:
                evs.append((d.sem, d.n, d.key))
        for eng in self.engs:
            for ev in evs:
                if ev[2] != eng:
                    self._wait(eng, ev)


class _View2:
    def __init__(self, full, shape):
        self.full = full
        self.shape = shape

    def __getitem__(self, idx):
        if not isinstance(idx, tuple):
            idx = (idx,)
        idx = idx + (slice(None),) * (2 - len(idx))
        p, f = idx
        p = _norm(p, self.shape[0])
        f = _norm(f, self.shape[1])
        return self.full[p, f]


class _View3:
    def __init__(self, full, shape):
        self.full = full
        self.shape = shape

    def __getitem__(self, idx):
        if not isinstance(idx, tuple):
            idx = (idx,)
        idx = idx + (slice(None),) * (3 - len(idx))
        p, a, b = idx
        P_, A, B = self.shape
        p = _norm(p, P_)
        v = self.full[p, 0:A * B].rearrange("p (a b) -> p a b", b=B)
        return v[:, a, b]


def _norm(sl, n):
    if isinstance(sl, slice):
        a = 0 if sl.start is None else sl.start
        b = n if sl.stop is None else sl.stop
        return slice(a, b)
    return sl


def build_nc():
    nc = bass.Bass("TRN2", target_bir_lowering=False)

    def din(name, shape):
        return nc.dram_tensor(name, list(shape), F32, kind="ExternalInput").ap()

    def dout(name, shape):
        return nc.dram_tensor(name, list(shape), F32, kind="ExternalOutput").ap()

    xall = din("xall", [4096, 1024])
    xhalo = din("xhalo", [256, 1024])
    xmem = din("xmem", [256, 1024])
    xs = din("xs", [DEC, 1024])
    ck = din("ck", [2048, 512])
    cv = din("cv", [2048, 512])
    clf = din("clf", [2048, 8])
    cmk = din("cmk", [256, 256])
    cmv = din("cmv", [256, 256])
    spT = din("spT", [128, 2, 16])
    w_in = din("w_in", [1024, 3080])
    w_out = din("w_out", [1024, 1024])
    w_mkv = din("w_mkv", [1024, 512])
    gl_d = din("gl", [128, 8])
    gml_d = din("gml", [128, 8])
    gfin_d = din("gfin", [128, 1024])
    bf_d = din("bfb", [128, 8])
    pq_d = din("pq", [128, 3])
    rc_d = din("rc", [128, 2, 16])
    psc_d = din("psc", [128, 2])
    wp_d = din("wpl", [128, 2, 64])

    y_o = dout("y_o", [S_OWN, 1024])
    k_o = dout("k_o", [S_OWN, 512])
    v_o = dout("v_o", [S_OWN, 512])
    lf_o = dout("lf_o", [S_OWN, 8])
    mk_o = dout("mk_o", [256, 256])
    mv_o = dout("mv_o", [256, 256])
    pool_o = dout("pool_o", [128, 2, 16])
    ys_o = dout("ys_o", [DEC, 1024])
    ks_o = dout("ks_o", [DEC, 512])
    vs_o = dout("vs_o", [DEC, 512])
    lfs_o = dout("lfs_o", [DEC, 8])
    pools_o = dout("pools_o", [128, 2, 16])

    with contextlib.ExitStack() as es:
        kb = KB(nc, es)

        def stop(level):
            if os.environ.get('KVERB'):
                print('stop point', level, 'nops', kb.nops)
            if STOP_AT <= level:
                kb.barrier()
                kb.dead = True

        uid = [0]

        def sbuf(st, name, shape, dt):
            uid[0] += 1
            return st.enter_context(nc.sbuf_tensor("sb%d_%s" % (uid[0], name), list(shape), dt))

        def psum(st, name, shape, dt):
            uid[0] += 1
            n = int(np.prod(shape[1:])) * (4 if dt == F32 else 2)
            assert n <= 2048 or n == 4096, (name, n)
            full = st.enter_context(nc.psum_tensor("ps%d_%s" % (uid[0], name), [128, (max(n, 2048)) // (4 if dt == F32 else 2)], dt))
            if len(shape) == 2:
                return _View2(full, shape)
            return _View3(full, shape)

        ident_bf = sbuf(es, "ident_bf", [128, 128], BF16)
        ident_f = sbuf(es, "ident_f", [128, 128], F32)
        tri_f = sbuf(es, "tri_f", [128, 128], F32)
        ones_f = sbuf(es, "ones_f", [128, 128], F32)
        ones_bf = sbuf(es, "ones_bf", [128, 128], BF16)
        sel = sbuf(es, "selz", [128, 8, 128], BF16)
        negh = sbuf(es, "negh", [128, 1], F32)
        bfb = sbuf(es, "bfb", [128, 8], F32)
        pq = sbuf(es, "pq", [128, 3], F32)
        rc = sbuf(es, "rc", [128, 2, 16], F32)
        psc = sbuf(es, "psc", [128, 2], F32)
        gl = sbuf(es, "gl", [128, 8], F32)
        gml = sbuf(es, "gml", [128, 8], F32)
        wp_f = sbuf(es, "wp_f", [128, 2, 64], F32)
        wp_bf = sbuf(es, "wp_bf", [128, 2, 64], BF16)
        catT = sbuf(es, "catT", [128, 8, S_OWN], BF16)
        QT = sbuf(es, "QTz", [128, 4, 2, S_OWN], BF16)
        Eq = sbuf(es, "Eqz", [128, S_OWN], BF16)
        w_kv = sbuf(es, "w_kv", [128, 8, 1032], BF16)
        catT_s = sbuf(es, "catT_s", [128, 8, DEC], BF16)
        QT_s = sbuf(es, "QTz_s", [128, 4, 2, DEC], BF16)
        Eq_s = sbuf(es, "Eqz_s", [128, DEC], BF16)
        xnT_s = sbuf(es, "xnT_s", [128, 8, DEC], BF16)
        ss = sbuf(es, "ss", [128, 64], F32)
        rstd = sbuf(es, "rstd", [128, 64], F32)
        junk = sbuf(es, "junk", [128, 1024], BF16)

        B_const = Buf()
        B_catT = [[Buf() for _ in range(NT)] for _ in range(8)]
        B_QT = [[Buf() for _ in range(NT)] for _ in range(4)]
        B_Eq = [Buf() for _ in range(NT)]
        B_wkv = [(Buf(), Buf()) for _ in range(8)]
        B_catTs = [Buf() for _ in range(8)]
        B_QTs = Buf()
        B_Eqs = Buf()
        B_xnTs = Buf()
        B_ss = [Buf() for _ in range(64)]
        B_junk = Buf()

        d_const = kb.dsem()
        B_cdma = Buf()
        nconst = 0
        for dst, src in ((gl, gl_d), (gml, gml_d), (bfb, bf_d), (pq, pq_d), (rc, rc_d), (psc, psc_d), (wp_f, wp_d)):
            ev_c = kb.dma(d_const, lambda q, dst=dst, src=src: q.dma_start(out=dst[:], in_=src))
            nconst += 1
        B_cdma.w = ev_c
        P = "pool"
        kb.op(P, lambda e: e.memset(negh[:], -0.5), writes=[B_const])
        kb.op(P, lambda e: e.memset(ident_f[:], 0.0), writes=[B_const])
        kb.op(P, lambda e: e.affine_select(out=ident_f[:], in_=ident_f[:], pattern=[[-1, 128]],
                                           compare_op=ALU.not_equal, fill=1.0, base=0, channel_multiplier=1),
              writes=[B_const])
        kb.op(P, lambda e: e.tensor_copy(out=ident_bf[:], in_=ident_f[:]), writes=[B_const])
        kb.op(P, lambda e: e.memset(ones_f[:], 1.0), writes=[B_const])
        kb.op(P, lambda e: e.memset(ones_bf[:], 1.0), writes=[B_const])
        kb.op(P, lambda e: e.memset(tri_f[:], 1.0), writes=[B_const])
        kb.op(P, lambda e: e.affine_select(out=tri_f[:], in_=tri_f[:], pattern=[[1, 128]], compare_op=ALU.is_ge,
                                           fill=0.0, base=0, channel_multiplier=-1), writes=[B_const])
        kb.op(P, lambda e: e.memset(sel[:], 0.0), writes=[B_const])
        kb.op(P, lambda e: e.affine_select(out=sel[0:8], in_=sel[0:8], pattern=[[1, 8], [0, 128]],
                                           compare_op=ALU.not_equal, fill=1.0, base=0, channel_multiplier=-1),
              writes=[B_const])
        kb.op(P, lambda e: e.tensor_copy(out=wp_bf[:], in_=wp_f[:]), reads=[B_cdma], writes=[B_const])
        zq = [b_ for row in B_QT for b_ in row]
        kb.op(P, lambda e: e.memset(QT_s[:], 0.0), writes=[B_QTs])
        kb.op(P, lambda e: e.memset(Eq_s[:], 0.0), writes=[B_Eqs])
        kb.op(P, lambda e: e.memset(Eq[:], 0.0), writes=list(B_Eq))
        kb.op(P, lambda e: e.memset(QT[:, :, :, 0:1024], 0.0), writes=zq)
        kb.op(P, lambda e: e.memset(QT[:, :, :, 1024:2048], 0.0), writes=zq)
        stop0 = True

        ss_idx = [0]

        def prep_parts(st, src_ap, m, dstT, dcol, Bdst, xt_ring, xs_ring, tp_ring):
            i = st["i"]
            st["i"] += 1
            xt, Bxt, dxt = xt_ring[i % len(xt_ring)]
            xsb, Bxs = xs_ring[i % len(xs_ring)]
            tp, Btp = tp_ring[i % len(tp_ring)]
            si = ss_idx[0] % 64
            ss_idx[0] += 1

            def fa():
                kb.dma(dxt, lambda q: q.dma_start(out=xt[0:m, :], in_=src_ap), writes=[Bxt])
                kb.op("act", lambda e: e.activation(out=junk[0:m, :], in_=xt[0:m, :], func=AF.Square,
                                                    accum_out=ss[0:m, si:si + 1]),
                      reads=[Bxt], writes=[B_junk, B_ss[si]])
                kb.op("pool", lambda e: e.tensor_scalar(out=rstd[0:m, si:si + 1], in0=ss[0:m, si:si + 1],
                                                        scalar1=1.0 / 1024.0, scalar2=EPS, op0=ALU.mult, op1=ALU.add),
                      reads=[B_ss[si]], writes=[B_ss[si]])
                kb.op("pool", lambda e: e.tensor_tensor(out=rstd[0:m, si:si + 1], in0=rstd[0:m, si:si + 1],
                                                        in1=negh[0:m, 0:1], op=ALU.pow),
                      reads=[B_ss[si]], writes=[B_ss[si]])
                kb.op("dve", lambda e: e.tensor_scalar(out=xsb[0:m, :], in0=xt[0:m, :], scalar1=rstd[0:m, si:si + 1],
                                                       scalar2=None, op0=ALU.mult),
                      reads=[Bxt, B_ss[si]], writes=[Bxs])

            def fb():
                for kc in range(8):
                    kb.op("pe", lambda e, kc=kc: e.transpose(tp[:, kc, 0:m], xsb[0:m, kc * 128:(kc + 1) * 128],
                                                             ident_bf[0:m, 0:m]),
                          reads=[Bxs], writes=[Btp], mark=(kc == 7))
                kb.op("dve", lambda e: e.tensor_copy(out=dstT[:, :, dcol:dcol + m], in_=tp[:, :, 0:m]),
                      reads=[Btp], writes=[Bdst])
            return fa, fb

        def prep_block(st, src_ap, m, dstT, dcol, Bdst, xt_ring, xs_ring, tp_ring):
            fa, fb = prep_parts(st, src_ap, m, dstT, dcol, Bdst, xt_ring, xs_ring, tp_ring)
            fa()
            fb()

        try:
            with contextlib.ExitStack() as sa:
                w_own = sbuf(sa, "w_own", [128, 8, 2048], BF16)
                B_wown = [(Buf(), Buf()) for _ in range(8)]
                mkT = sbuf(sa, "mkT", [128, 2, 256], BF16)
                mvb = sbuf(sa, "mvb", [128, 2, 256], BF16)
                mkT_s = sbuf(sa, "mkT_s", [128, 2, 256], BF16)
                mvb_s = sbuf(sa, "mvb_s", [128, 2, 256], BF16)
                B_mk = Buf()
                B_mks = Buf()
                qmT = sbuf(sa, "qmT", [128, 2, S_OWN], BF16)
                B_qmT = [[Buf() for _ in range(NT)] for _ in range(2)]
                qmT_s = sbuf(sa, "qmT_s", [128, 2, DEC], BF16)
                B_qmTs = Buf()
                uext = sbuf(sa, "uext", [128, 2, NB_OWN, 144], F32)
                B_uext = Buf()
                uext_s = sbuf(sa, "uext_s", [128, 2, 16 + DEC], F32)
                B_uexts = Buf()
                w_mk = sbuf(sa, "w_mk", [128, 8, 512], BF16)
                B_wmk = [(Buf(), Buf()) for _ in range(8)]
                tp_ring = [(psum(sa, "tp%d" % i, [128, 8, 128], BF16), PB()) for i in range(2)]
                mm_ring = [(psum(sa, "mm%d" % i, [128, 512], F32), PB()) for i in range(4)]
                pp_ring = [(psum(sa, "pp%d" % i, [128, 512], F32), PB()) for i in range(2)]
                mmi = [0]

                with contextlib.ExitStack() as sw:
                    wst = [(sbuf(sw, "wst%d" % i, [128, 772], F32), Buf(), kb.dsem()) for i in range(10)]
                    segs = ((0, 512, w_own, 0, B_wown), (512, 1536, w_kv, 0, B_wkv), (1536, 2048, w_own, 512, B_wown),
                            (2048, 2056, w_kv, 1024, B_wkv), (2056, 3080, w_own, 1024, B_wown))
                    parts = ((0, 768), (768, 1536), (1536, 2308), (2308, 3080))
                    li = 0
                    ci = 0
                    for kc in range(8):
                        for (pa, pb) in parts:
                            w, Bw, dw = wst[li % 10]
                            li += 1
                            kb.dma(dw, lambda q, w=w, kc=kc, pa=pa, pb=pb: q.dma_start(
                                out=w[:, 0:pb - pa], in_=w_in[kc * 128:(kc + 1) * 128, pa:pb]), writes=[Bw])
                            for (sa_, sb_, dst, d0, Bd) in segs:
                                lo, hi = max(pa, sa_), min(pb, sb_)
                                if lo >= hi:
                                    continue
                                o_ap = dst[:, kc, d0 + lo - sa_:d0 + hi - sa_]
                                i_ap = w[:, lo - pa:hi - pa]
                                if ci % 2 == 0 and hi - lo > 16:
                                    kb.op("act", lambda e, o_ap=o_ap, i_ap=i_ap, kc=kc: e.activation(
                                        out=o_ap, in_=i_ap, func=AF.Identity, scale=gl[:, kc:kc + 1]),
                                        reads=[Bw, B_cdma], writes=[Bd[kc][0]])
                                else:
                                    kb.op("dve", lambda e, o_ap=o_ap, i_ap=i_ap, kc=kc: e.tensor_scalar(
                                        out=o_ap, in0=i_ap, scalar1=gl[:, kc:kc + 1], scalar2=None, op0=ALU.mult),
                                        reads=[Bw, B_cdma], writes=[Bd[kc][1]])
                                ci += 1
                    for kc in range(8):
                        w, Bw, dw = wst[li % 10]
                        li += 1
                        kb.dma(dw, lambda q, w=w, kc=kc: q.dma_start(out=w[:, 0:512], in_=w_mkv[kc * 128:(kc + 1) * 128, :]),
                               writes=[Bw])
                        if kc % 2 == 0:
                            kb.op("act", lambda e, w=w, kc=kc: e.activation(
                                out=w_mk[:, kc, :], in_=w[:, 0:512], func=AF.Identity, scale=gml[:, kc:kc + 1]),
                                reads=[Bw, B_cdma], writes=[B_wmk[kc][0]])
                        else:
                            kb.op("dve", lambda e, w=w, kc=kc: e.tensor_scalar(
                                out=w_mk[:, kc, :], in0=w[:, 0:512], scalar1=gml[:, kc:kc + 1], scalar2=None, op0=ALU.mult),
                                reads=[Bw, B_cdma], writes=[B_wmk[kc][1]])
                    kb.barrier()
                sa1 = sa.enter_context(contextlib.ExitStack())
                xt_ring = []
                for i in range(2):
                    xt_ring.append((sbuf(sa1, "xt%d" % i, [128, 1024], F32), Buf(), kb.dsem()))
                xs_ring = [(sbuf(sa1, "xsb%d" % i, [128, 1024], BF16), Buf()) for i in range(2)]
                xnT_ring = [(sbuf(sa1, "xnT%d" % i, [128, 8, 512], BF16), Buf()) for i in range(2)]
                stop(0.2)

                pst = {"i": 0}

                def proj_fm(xnT, Bx, n, ccs, evac):
                    for cc in ccs:
                        ps, Bp = mm_ring[mmi[0] % 4]
                        mmi[0] += 1
                        for kc in range(8):
                            kb.op("pe", lambda e, kc=kc, cc=cc, ps=ps: e.matmul(
                                ps[:, 0:n], lhsT=w_own[:, kc, cc * 128:(cc + 1) * 128], rhs=xnT[:, kc, 0:n],
                                start=(kc == 0), stop=(kc == 7)), reads=[Bx, *B_wown[kc]], writes=[Bp], mark=(kc == 7))
                        evac(cc, ps, Bp)

                def evac_own(t):
                    def f(cc, ps, Bp):
                        cols = slice(t * 512, (t + 1) * 512)
                        if cc < 4:
                            kb.op("dve", lambda e: e.tensor_copy(out=QT[0:64, cc, 0, cols], in_=ps[0:64, :]),
                                  reads=[Bp], writes=[B_QT[cc][t]])
                            kb.op("dve", lambda e: e.tensor_copy(out=QT[64:128, cc, 1, cols], in_=ps[64:128, :]),
                                  reads=[Bp], writes=[B_QT[cc][t]])
                        elif cc < 8:
                            kb.op("act", lambda e: e.activation(out=catT[:, cc - 4, cols], in_=ps[:, :], func=AF.Silu),
                                  reads=[Bp], writes=[B_catT[cc - 4][t]])
                        elif cc < 10:
                            kb.op("dve", lambda e: e.tensor_copy(
                                out=uext[:, cc - 8, t * 4:(t + 1) * 4, 16:144],
                                in_=ps[:, :].rearrange("p (b k) -> p b k", k=128)), reads=[Bp], writes=[B_uext])
                        elif cc < 12:
                            kb.op("act", lambda e: e.activation(out=catT[:, cc - 6, cols], in_=ps[:, :], func=AF.Silu),
                                  reads=[Bp], writes=[B_catT[cc - 6][t]])
                        elif cc < 14:
                            kb.op("dve", lambda e: e.tensor_copy(out=qmT[:, cc - 12, cols], in_=ps[:, :]),
                                  reads=[Bp], writes=[B_qmT[cc - 12][t]])
                        else:
                            kb.op("act", lambda e: e.activation(out=catT[:, cc - 8, cols], in_=ps[:, :], func=AF.Silu),
                                  reads=[Bp], writes=[B_catT[cc - 8][t]])
                    return f

                def evac_halo(cc, ps, Bp):
                    kb.op("dve", lambda e: e.tensor_copy(out=uext[:, cc - 8, :, 0:16],
                                                         in_=ps[:, 0:256].rearrange("p (b k) -> p b k", k=16)),
                          reads=[Bp], writes=[B_uext])

                def evac_s(cc, ps, Bp):
                    n = DEC
                    if cc < 4:
                        kb.op("dve", lambda e: e.tensor_copy(out=QT_s[0:64, cc, 0, :], in_=ps[0:64, 0:n]), reads=[Bp], writes=[B_QTs])
                        kb.op("dve", lambda e: e.tensor_copy(out=QT_s[64:128, cc, 1, :], in_=ps[64:128, 0:n]), reads=[Bp], writes=[B_QTs])
                    elif cc < 8:
                        kb.op("act", lambda e: e.activation(out=catT_s[:, cc - 4, :], in_=ps[:, 0:n], func=AF.Silu),
                              reads=[Bp], writes=[B_catTs[cc - 4]])
                    elif cc < 10:
                        kb.op("dve", lambda e: e.tensor_copy(out=uext_s[:, cc - 8, 16:16 + n], in_=ps[:, 0:n]),
                              reads=[Bp], writes=[B_uexts])
                    elif cc < 12:
                        kb.op("act", lambda e: e.activation(out=catT_s[:, cc - 6, :], in_=ps[:, 0:n], func=AF.Silu),
                              reads=[Bp], writes=[B_catTs[cc - 6]])
                    elif cc < 14:
                        kb.op("dve", lambda e: e.tensor_copy(out=qmT_s[:, cc - 12, :], in_=ps[:, 0:n]),
                              reads=[Bp], writes=[B_qmTs])
                    else:
                        kb.op("act", lambda e: e.activation(out=catT_s[:, cc - 8, :], in_=ps[:, 0:n], func=AF.Silu),
                              reads=[Bp], writes=[B_catTs[cc - 8]])

                def prep_tile(src, nblk, blkrows, xnT, Bx):
                    for b in range(nblk):
                        prep_block(pst, src[b * blkrows:(b + 1) * blkrows, :], blkrows, xnT, b * blkrows, Bx,
                                   xt_ring, xs_ring, tp_ring)

                smk = sa1.enter_context(contextlib.ExitStack())
                mst = [(sbuf(smk, "mst%d" % i, [128, 512], F32), Buf(), kb.dsem()) for i in range(2)]
                mkb = [(sbuf(smk, "mkb%d" % i, [128, 256], BF16), Buf()) for i in range(2)]
                xnTm, Bxm = xnT_ring[0]
                prep_tile(xmem, 2, 128, xnTm, Bxm)
                for blk in range(2):
                    ps, Bp = mm_ring[mmi[0] % 4]
                    mmi[0] += 1
                    for kc in range(8):
                        kb.op("pe", lambda e, kc=kc, ps=ps, blk=blk: e.matmul(
                            ps[:, :], lhsT=xnTm[:, kc, blk * 128:(blk + 1) * 128], rhs=w_mk[:, kc, :],
                            start=(kc == 0), stop=(kc == 7)), reads=[Bxm, *B_wmk[kc]], writes=[Bp], mark=(kc == 7))
                    m, Bm, dm = mst[blk]
                    kbf, Bk = mkb[blk]
                    kb.op("act", lambda e, m=m, ps=ps: e.activation(out=m[:], in_=ps[:, :], func=AF.Identity),
                          reads=[Bp], writes=[Bm])
                    kb.op("dve", lambda e, kbf=kbf, ps=ps: e.tensor_copy(out=kbf[:], in_=ps[:, 0:256]),
                          reads=[Bp], writes=[Bk])
                    kb.op("dve", lambda e, ps=ps, blk=blk: e.tensor_copy(out=mvb[:, blk, :], in_=ps[:, 256:512]),
                          reads=[Bp], writes=[B_mk])
                    kb.dma(dm, lambda q, m=m, blk=blk: q.dma_start(out=mk_o[blk * 128:(blk + 1) * 128, :], in_=m[:, 0:256]),
                           reads=[Bm])
                    kb.dma(dm, lambda q, m=m, blk=blk: q.dma_start(out=mv_o[blk * 128:(blk + 1) * 128, :], in_=m[:, 256:512]),
                           reads=[Bm])
                    tp, Btp = tp_ring[blk % 2]
                    for cc in range(2):
                        kb.op("pe", lambda e, cc=cc, tp=tp, kbf=kbf: e.transpose(
                            tp[:, cc, :], kbf[:, cc * 128:(cc + 1) * 128], ident_bf[:]),
                            reads=[Bk, B_const], writes=[Btp], mark=(cc == 1))
                    kb.op("dve", lambda e, tp=tp, blk=blk: e.tensor_copy(out=mkT[:, :, blk * 128:(blk + 1) * 128],
                                                                          in_=tp[:, 0:2, :]),
                          reads=[Btp], writes=[B_mk])
                for blk in range(2):
                    m, Bm, dm = mst[blk]
                    kbf, Bk = mkb[blk]
                    kb.dma(dm, lambda q, m=m, blk=blk: q.dma_start(out=m[:, 0:256], in_=cmk[blk * 128:(blk + 1) * 128, :]),
                           writes=[Bm])
                    kb.dma(dm, lambda q, m=m, blk=blk: q.dma_start(out=m[:, 256:512], in_=cmv[blk * 128:(blk + 1) * 128, :]),
                           writes=[Bm])
                    kb.op("dve", lambda e, kbf=kbf, m=m: e.tensor_copy(out=kbf[:], in_=m[:, 0:256]), reads=[Bm], writes=[Bk])
                    kb.op("dve", lambda e, m=m, blk=blk: e.tensor_copy(out=mvb_s[:, blk, :], in_=m[:, 256:512]),
                          reads=[Bm], writes=[B_mks])
                    tp, Btp = tp_ring[blk % 2]
                    for cc in range(2):
                        kb.op("pe", lambda e, cc=cc, tp=tp, kbf=kbf: e.transpose(
                            tp[:, cc, :], kbf[:, cc * 128:(cc + 1) * 128], ident_bf[:]),
                            reads=[Bk, B_const], writes=[Btp], mark=(cc == 1))
                    kb.op("dve", lambda e, tp=tp, blk=blk: e.tensor_copy(out=mkT_s[:, :, blk * 128:(blk + 1) * 128],
                                                                          in_=tp[:, 0:2, :]),
                          reads=[Btp], writes=[B_mks])


                kb.barrier()
                smk.close()
                stop(0.3)
                d_misc = kb.dsem()
                kb.dma(d_misc, lambda q: q.dma_start(out=uext_s[:, :, 0:16], in_=spT), writes=[B_uexts])
                tiles = [("own", t) for t in range(NT)] + [("halo", 0), ("s", 0)]

                def prep_list(kind, t, slot):
                    xnT, Bx = xnT_ring[slot]
                    if kind == "own":
                        return [prep_parts(pst, xall[t * 512 + b * 128:t * 512 + (b + 1) * 128, :], 128, xnT, b * 128, Bx,
                                           xt_ring, xs_ring, tp_ring) for b in range(4)]
                    elif kind == "halo":
                        return [prep_parts(pst, xhalo[b * 128:(b + 1) * 128, :], 128, xnT, b * 128, Bx,
                                           xt_ring, xs_ring, tp_ring) for b in range(2)]
                    return [prep_parts(pst, xs, DEC, xnT_s, 0, B_xnTs, xt_ring, xs_ring, tp_ring)]

                for fa, fb in prep_list(tiles[0][0], tiles[0][1], 0):
                    fa()
                    fb()
                for i, (kind, t) in enumerate(tiles):
                    nxt = prep_list(tiles[i + 1][0], tiles[i + 1][1], (i + 1) % 2) if i + 1 < len(tiles) else []
                    xnT, Bx = xnT_ring[i % 2]
                    if kind == "own":
                        for g in range(4):
                            if g < len(nxt):
                                nxt[g][0]()
                            proj_fm(xnT, Bx, 512, range(4 * g, 4 * g + 3), evac_own(t))
                            if g < len(nxt):
                                nxt[g][1]()
                            proj_fm(xnT, Bx, 512, range(4 * g + 3, 4 * g + 4), evac_own(t))
                    elif kind == "halo":
                        for fa, fb in nxt:
                            fa()
                        proj_fm(xnT, Bx, 256, (8, 9), evac_halo)
                        for fa, fb in nxt:
                            fb()
                    else:
                        proj_fm(xnT_s, B_xnTs, DEC, range(16), evac_s)

                d_out = kb.dsem()
                kb.dma(d_out, lambda q: q.dma_start(out=pool_o, in_=uext[:, :, NB_OWN - 1, 128:144]), reads=[B_uext])
                kb.dma(d_out, lambda q: q.dma_start(out=pools_o, in_=uext_s[:, :, DEC:DEC + 16]), reads=[B_uexts])

                kb.barrier()
                sa1.close()
                stop(0.4)

                if True:
                    sp_ = sa.enter_context(contextlib.ExitStack())
                    pm_ops = []
                    pooledT = sbuf(sp_, "pooledT", [128, 2, S_OWN], BF16)
                    pooledT_s = sbuf(sp_, "pooledT_s", [128, 2, DEC], BF16)
                    B_pl = Buf()
                    B_pls = Buf()
                    a1 = sbuf(sp_, "a1", [128, NB_OWN, 144], F32)
                    a2 = sbuf(sp_, "a2", [128, NB_OWN, 144], F32)
                    tmpc = sbuf(sp_, "tmpc", [128, 16], F32)
                    B_a = Buf()
                    D = "dve"

                    def tt(out, in0, in1, op, reads, writes, eng=D):
                        pm_ops.append(lambda: kb.op(eng, lambda e: e.tensor_tensor(out=out, in0=in0, in1=in1, op=op),
                                                    reads=reads, writes=writes))

                    def stt(out, in0, sc, in1, reads, writes):
                        pm_ops.append(lambda: kb.op(D, lambda e: e.scalar_tensor_tensor(
                            out=out, in0=in0, scalar=sc, in1=in1, op0=ALU.mult, op1=ALU.subtract), reads=reads, writes=writes))

                    def pool_mix(u, nb, L, pooled, Bu, Bp, corr):
                        for cc in range(2):
                            uu = u[:, cc]
                            A1, A2, A3, A4 = (a[:, 0:nb, 0:L] for a in (a1, a2, a1, a2))
                            tt(A1[:, :, 1:L], uu[:, :, 1:L], uu[:, :, 0:L - 1], ALU.add, [Bu, Bp], [B_a])
                            tt(A2[:, :, 3:L], A1[:, :, 3:L], A1[:, :, 1:L - 2], ALU.add, [B_a], [B_a])
                            if cc == 0:
                                srcs = ((0, 64, A1, 0.5), (64, 128, A2, 0.25))
                            else:
                                tt(A3[:, :, 7:L], A2[:, :, 7:L], A2[:, :, 3:L - 4], ALU.add, [B_a], [B_a])
                                tt(A4[:, :, 15:L], A3[:, :, 15:L], A3[:, :, 7:L - 8], ALU.add, [B_a], [B_a])
                                srcs = ((0, 64, A3, 0.125), (64, 128, A4, 0.0625))
                            for (p0, p1, A, sc) in srcs:
                                stt(pooled[p0:p1, cc], A[p0:p1, :, 16:L], sc, uu[p0:p1, :, 16:L], [B_a, Bu], [Bp])
                                if corr:
                                    tt(tmpc[p0:p1, :], A[p0:p1, 0, 16:32], rc[p0:p1, cc, :], ALU.mult, [B_a, B_const], [B_a])
                                    tt(pooled[p0:p1, cc, 0, 0:16], tmpc[p0:p1, :], uu[p0:p1, 0, 16:32], ALU.subtract,
                                       [B_a, Bu], [Bp])

                    pool_mix(uext, NB_OWN, 144, pooledT[:, :, :].rearrange("p c (b k) -> p c b k", k=128), B_uext, B_pl, True)
                    pool_mix(uext_s[:, :, :].rearrange("p c (b k) -> p c b k", b=1), 1, 16 + DEC,
                             pooledT_s[:, :, :].rearrange("p c (b k) -> p c b k", b=1), B_uexts, B_pls, False)

                    def pool_mm(pl, Bpl, n, cat_ap, Bcat):
                        for cc in range(2):
                            ps, Bp = mm_ring[mmi[0] % 4]
                            mmi[0] += 1
                            for e_ in range(2):
                                r = slice(64 * e_, 64 * e_ + 64)
                                kb.op("pe", lambda e, r=r, cc=cc, ps=ps: e.matmul(
                                    ps[r, 0:n], lhsT=wp_bf[r, cc, :], rhs=pl(cc, r), start=True, stop=True),
                                    reads=[Bpl, B_const], writes=[Bp], mark=(e_ == 1))
                            kb.op("dve", lambda e, cc=cc, ps=ps: e.scalar_tensor_tensor(
                                out=cat_ap(cc), in0=ps[:, 0:n], scalar=psc[:, cc:cc + 1], in1=cat_ap(cc), op0=ALU.mult,
                                op1=ALU.mult), reads=[Bp, B_const], writes=[Bcat(cc)])

                    def all_pool_mm():
                        for t in range(NT):
                            cols = slice(t * 512, (t + 1) * 512)
                            pool_mm(lambda cc, r, cols=cols: pooledT[r, cc, cols], B_pl, 512,
                                    lambda cc, cols=cols: catT[:, 4 + cc, cols], lambda cc, t=t: B_catT[4 + cc][t])
                        pool_mm(lambda cc, r: pooledT_s[r, cc, :], B_pls, DEC, lambda cc: catT_s[:, 4 + cc, :],
                                lambda cc: B_catTs[4 + cc])

                stop(0.5)
                with contextlib.ExitStack() as sm:
                    PTm = [(sbuf(sm, "PTm%d" % i, [128, 512], BF16), Buf()) for i in range(3)]
                    recs = [(sbuf(sm, "recm%d" % i, [128, 512], F32), sbuf(sm, "tmpm%d" % i, [128, 512], F32), Buf()) for i in range(2)]
                    pti = [0]

                    cci = [0]

                    def mem_attn(n, mkT_, mvb_, Bmk_, q_ap, Bq, cat_ap, Bcat):
                        for cc in range(2):
                            if cci[0] % 2 == 0:
                                (O, BO), (SU, BS) = pp_ring[0], pp_ring[1]
                            else:
                                (O, BO), (SU, BS) = mm_ring[2], mm_ring[3]
                            cci[0] += 1
                            units = [(e_, jb) for e_ in range(2) for jb in range(2)]

                            def qk(k):
                                e_, jb = units[k]
                                r = slice(64 * e_, 64 * e_ + 64)
                                S, BSc = mm_ring[k % 2]
                                kb.op("pe", lambda e: e.matmul(
                                    S[:, 0:n], lhsT=mkT_[r, cc, jb * 128:(jb + 1) * 128], rhs=q_ap(cc, r),
                                    start=True, stop=True), reads=[Bmk_, Bq(cc)], writes=[BSc])

                            qk(0)
                            for k, (e_, jb) in enumerate(units):
                                if k + 1 < len(units):
                                    qk(k + 1)
                                S, BSc = mm_ring[k % 2]
                                r = slice(64 * e_, 64 * e_ + 64)
                                hm = 2 * cc + e_
                                PT, BPT = PTm[pti[0] % 3]
                                pti[0] += 1
                                kb.op("act", lambda e, S=S, PT=PT: e.activation(out=PT[:, 0:n], in_=S[:, 0:n], func=AF.Exp,
                                                                                scale=0.125), reads=[BSc], writes=[BPT])
                                kb.op("pe", lambda e, O=O, r=r, jb=jb, hm=hm, PT=PT: e.matmul(
                                    O[r, 0:n], lhsT=mvb_[:, jb, hm * 64:(hm + 1) * 64], rhs=PT[:, 0:n],
                                    start=(jb == 0), stop=(jb == 1)), reads=[Bmk_, BPT], writes=[BO], mark=False)
                                kb.op("pe", lambda e, SU=SU, r=r, jb=jb, PT=PT: e.matmul(
                                    SU[r, 0:n], lhsT=ones_bf[:, 0:64], rhs=PT[:, 0:n],
                                    start=(jb == 0), stop=(jb == 1)), reads=[BPT, B_const], writes=[BS])
                            rm, tm, Brm = recs[cci[0] % 2]
                            kb.op("act", lambda e, SU=SU: e.activation(out=rm[:, 0:n], in_=SU[:, 0:n], func=AF.Ln),
                                  reads=[BS], writes=[Brm])
                            kb.op("act", lambda e: e.activation(out=rm[:, 0:n], in_=rm[:, 0:n], func=AF.Exp, scale=-1.0),
                                  reads=[Brm], writes=[Brm])
                            kb.op("dve", lambda e, O=O: e.tensor_tensor(out=tm[:, 0:n], in0=O[:, 0:n], in1=rm[:, 0:n],
                                                                         op=ALU.mult), reads=[BO, Brm], writes=[Brm])
                            kb.op("dve", lambda e, cc=cc: e.tensor_tensor(out=cat_ap(cc), in0=tm[:, 0:n], in1=cat_ap(cc),
                                                                          op=ALU.mult), reads=[Brm], writes=[Bcat(cc)])

                    per = (len(pm_ops) + 4) // 5

                    def drain(k):
                        for _ in range(min(k, len(pm_ops))):
                            pm_ops.pop(0)()

                    drain(per)
                    for t in range(NT):
                        cols = slice(t * 512, (t + 1) * 512)
                        mem_attn(512, mkT, mvb, B_mk, lambda cc, r, cols=cols: qmT[r, cc, cols], lambda cc, t=t: B_qmT[cc][t],
                                 lambda cc, cols=cols: catT[:, 6 + cc, cols], lambda cc, t=t: B_catT[6 + cc][t])
                        drain(per)
                    mem_attn(DEC, mkT_s, mvb_s, B_mks, lambda cc, r: qmT_s[r, cc, :], lambda cc: B_qmTs,
                             lambda cc: catT_s[:, 6 + cc, :], lambda cc: B_catTs[6 + cc])
                    drain(len(pm_ops))
                    all_pool_mm()
                    kb.barrier()
                sp_.close()
                kb.barrier()

            bi = [0]

            def ingest(R, m, ksrc, Bks, vsrc, Bvs, KT_ap, BK, V_ap, BV):
                i = bi[0]
                kbf, Bkb = R["kbf"][i % 2]
                kb.op("dve", lambda e: e.tensor_copy(out=kbf[0:m, :], in_=ksrc), reads=[Bks], writes=[Bkb])
                kb.op("dve", lambda e: e.tensor_copy(out=V_ap, in_=vsrc), reads=[Bvs], writes=[BV])
                ktp, Bkt = R["ktp"][0]

                def tail():
                    for c in range(4):
                        kb.op("pe", lambda e, c=c: e.transpose(ktp[:, c, 0:m], kbf[0:m, c * 128:(c + 1) * 128],
                                                               ident_bf[0:m, 0:m]),
                              reads=[Bkb, B_const], writes=[Bkt], mark=(c == 3))
                    kb.op("act", lambda e: e.activation(out=KT_ap, in_=ktp[:, :, 0:m], func=AF.Identity),
                          reads=[Bkt], writes=[BK])
                return tail

            def kv_block(R, xnT, Bx, col, m, zf_ap, B_zf, kout, vout, KT_ap, BK, V_ap, BV, mid=None):
                i = bi[0]
                kps, Bkp = R["kps"][i % 2]
                vps, Bvp = R["vps"][i % 2]
                zps, Bzp = R["zps"][0]
                for (ps, Bp, c0, n) in ((kps, Bkp, 0, 512), (vps, Bvp, 512, 512), (zps, Bzp, 1024, 8)):
                    for kc in range(8):
                        kb.op("pe", lambda e, kc=kc, ps=ps, c0=c0, n=n: e.matmul(
                            ps[0:m, 0:n], lhsT=xnT[:, kc, col:col + m], rhs=w_kv[:, kc, c0:c0 + n],
                            start=(kc == 0), stop=(kc == 7)), reads=[Bx, *B_wkv[kc]], writes=[Bp], mark=(kc == 7))
                    if mid is not None and c0 == 0:
                        mid()
                kb.op("dve", lambda e: e.tensor_tensor(out=zf_ap, in0=zps[0:m, :], in1=bfb[0:m, :], op=ALU.add),
                      reads=[Bzp, B_const], writes=[B_zf])
                tail = ingest(R, m, kps[0:m, :], Bkp, vps[0:m, :], Bvp, KT_ap, BK, V_ap, BV)
                if kout is not None:
                    ks_, Bk_, dk_ = R["kst"][i % len(R["kst"])]
                    vs_, Bv_, dv_ = R["vst"][i % len(R["vst"])]
                    kb.op("act", lambda e: e.activation(out=ks_[0:m, :], in_=kps[0:m, :], func=AF.Identity),
                          reads=[Bkp], writes=[Bk_])
                    kb.op("act", lambda e: e.activation(out=vs_[0:m, :], in_=vps[0:m, :], func=AF.Identity),
                          reads=[Bvp], writes=[Bv_])
                    kb.dma(dk_, lambda q: q.dma_start(out=kout, in_=ks_[0:m, :]), reads=[Bk_], q="pool")
                    kb.dma(dv_, lambda q: q.dma_start(out=vout, in_=vs_[0:m, :]), reads=[Bv_], q="pool")
                bi[0] += 1
                return tail

            def kv_res(st, depth=2):
                return dict(
                    kst=[(sbuf(st, "kst%d" % i, [128, 512], F32), Buf(), kb.dsem()) for i in range(depth)],
                    vst=[(sbuf(st, "vst%d" % i, [128, 512], F32), Buf(), kb.dsem()) for i in range(depth)],
                    kbf=[(sbuf(st, "kbf%d" % i, [128, 512], BF16), Buf()) for i in range(2)],
                    kps=[(psum(st, "kps%d" % i, [128, 512], F32), PB()) for i in range(2)],
                    vps=[(psum(st, "vps%d" % i, [128, 512], F32), PB()) for i in range(2)],
                    zps=[(psum(st, "zps%d" % i, [128, 8], F32), PB()) for i in range(1)],
                    ktp=[(psum(st, "ktp%d" % i, [128, 4, 128], BF16), PB()) for i in range(1)])

            def logf_chain(st, zf, B_zf, m, nb):
                ez = sbuf(st, "ez", [128, nb, 8], F32)
                B_ez = Buf()
                kb.op("act", lambda e: e.activation(out=ez[0:m], in_=zf[0:m], func=AF.Exp, scale=-1.0),
                      reads=[B_zf], writes=[B_ez])
                kb.op("act", lambda e: e.activation(out=ez[0:m], in_=ez[0:m], func=AF.Ln, bias=1.0, scale=1.0),
                      reads=[B_ez], writes=[B_ez])
                return ez, B_ez

            def scan_incl(bufs, B_f, src, n, dst):
                cur = src
                d = 1
                k = 0
                while d < n:
                    nxt = bufs[k % 2][:, 0:n, :]
                    kb.op("dve", lambda e, nxt=nxt, cur=cur, d=d: e.tensor_copy(out=nxt[:, 0:d, :], in_=cur[:, 0:d, :]),
                          reads=[B_f], writes=[B_f])
                    kb.op("dve", lambda e, nxt=nxt, cur=cur, d=d: e.tensor_tensor(
                        out=nxt[:, d:n, :], in0=cur[:, d:n, :], in1=cur[:, 0:n - d, :], op=ALU.add),
                        reads=[B_f], writes=[B_f])
                    cur = nxt
                    d *= 2
                    k += 1
                kb.op("dve", lambda e: e.tensor_copy(out=dst, in_=cur), reads=[B_f], writes=[B_f])

            pti = [0]
            pairi = [0]

            def attend(R, N, qcol0, QT_, BQ, Eq_, BEq, blocks, cat_ap, Bcat, B_bias):
                Sb, Ob, Ub, PTr, comb, rec, tmpo, B_rec = R
                for c in range(4):
                    nb = len(blocks)

                    def qk(j):
                        bk = blocks[j]
                        nk, c0 = bk["nk"], bk["c0"]
                        for e_ in range(2):
                            h = 2 * c + e_
                            S, BS = Sb[2 * (j % 2) + e_]
                            kb.op("pe", lambda e, S=S, e_=e_: e.matmul(
                                S[0:nk, c0:N], lhsT=bk["KT"](c), rhs=QT_[:, c, e_, qcol0 + c0:qcol0 + N],
                                start=True, stop=False), reads=[bk["BK"], BQ(c)], writes=[BS], mark=False)
                            kb.op("pe", lambda e, S=S, h=h: e.matmul(
                                S[0:nk, c0:N], lhsT=sel[:, h, 0:nk], rhs=Eq_[:, qcol0 + c0:qcol0 + N],
                                start=False, stop=True), reads=[BEq, B_const], writes=[BS])

                    qk(0)
                    for j in range(nb):
                        if j + 1 < nb:
                            qk(j + 1)
                        bk = blocks[j]
                        nk, c0 = bk["nk"], bk["c0"]
                        for e_ in range(2):
                            h = 2 * c + e_
                            S, BS = Sb[2 * (j % 2) + e_]
                            O, BO = Ob[e_]
                            U, BU = Ub[e_]
                            PT, BPT = PTr[pti[0] % len(PTr)]
                            pti[0] += 1
                            w = min(128, N - c0)
                            if bk["fix"] == "p":
                                kb.op("act", lambda e, S=S, PT=PT, h=h: e.activation(
                                    out=PT[0:nk, c0:c0 + w], in_=S[0:nk, c0:c0 + w], func=AF.Exp, bias=bk["biasp"](h), scale=0.125),
                                    reads=[BS, B_bias], writes=[BPT])
                                if c0 + w < N:
                                    kb.op("act", lambda e, S=S, PT=PT, h=h: e.activation(
                                        out=PT[0:nk, c0 + w:N], in_=S[0:nk, c0 + w:N], func=AF.Exp, bias=bk["bias"](h), scale=0.125),
                                        reads=[BS, B_bias], writes=[BPT])
                            else:
                                kb.op("act", lambda e, S=S, PT=PT, h=h: e.activation(
                                    out=PT[0:nk, c0:N], in_=S[0:nk, c0:N], func=AF.Exp, bias=bk["bias"](h), scale=0.125),
                                    reads=[BS, B_bias], writes=[BPT])
                            if bk["fix"] == "tri":
                                kb.op("pool", lambda e, PT=PT, w=w: e.affine_select(
                                    out=PT[0:nk, c0:c0 + w], in_=PT[0:nk, c0:c0 + w], pattern=[[1, w]],
                                    compare_op=ALU.is_ge, fill=0.0, base=0, channel_multiplier=-1),
                                    reads=[BPT], writes=[BPT])
                            kb.op("pe", lambda e, O=O, PT=PT, j=j: e.matmul(
                                O[:, c0:N], lhsT=bk["V"](c), rhs=PT[0:nk, c0:N], start=(j == 0), stop=(j == nb - 1)),
                                reads=[bk["BV"], BPT], writes=[BO], mark=False)
                            kb.op("pe", lambda e, U=U, PT=PT, j=j: e.matmul(
                                U[:, c0:N], lhsT=ones_bf[0:nk, :], rhs=PT[0:nk, c0:N], start=(j == 0),
                                stop=(j == nb - 1)), reads=[BPT, B_const], writes=[BU])
                    oc, uc, Bc4 = comb[pairi[0] % 2]
                    pairi[0] += 1
                    kb.op("dve", lambda e: e.tensor_copy(out=uc[0:64, 0:N], in_=Ub[0][0][0:64, 0:N]),
                          reads=[Ub[0][1]], writes=[Bc4[0]])
                    kb.op("act", lambda e: e.activation(out=uc[64:128, 0:N], in_=Ub[1][0][64:128, 0:N], func=AF.Identity),
                          reads=[Ub[1][1]], writes=[Bc4[1]])
                    kb.op("dve", lambda e: e.tensor_copy(out=oc[0:64, 0:N], in_=Ob[0][0][0:64, 0:N]),
                          reads=[Ob[0][1]], writes=[Bc4[2]])
                    kb.op("act", lambda e: e.activation(out=oc[64:128, 0:N], in_=Ob[1][0][64:128, 0:N], func=AF.Identity),
                          reads=[Ob[1][1]], writes=[Bc4[3]])
                    kb.op("dve", lambda e: e.reciprocal(out=rec[:, 0:N], in_=uc[:, 0:N]), reads=[Bc4[0], Bc4[1]], writes=[B_rec])
                    kb.op("dve", lambda e: e.tensor_tensor(out=tmpo[:, 0:N], in0=oc[:, 0:N], in1=rec[:, 0:N],
                                                           op=ALU.mult), reads=[Bc4[2], Bc4[3], B_rec], writes=[B_rec])
                    kb.op("dve", lambda e, c=c: e.tensor_tensor(out=cat_ap(c), in0=tmpo[:, 0:N], in1=cat_ap(c),
                                                                op=ALU.mult), reads=[B_rec], writes=[Bcat(c)])

            def attn_res(st):
                Sb = [(psum(st, "Sb%d" % i, [128, 512], F32), PB()) for i in range(4)]
                Ob = [(psum(st, "Ob%d" % i, [128, 512], F32), PB()) for i in range(2)]
                Ub = [(psum(st, "Ub%d" % i, [128, 512], F32), PB()) for i in range(2)]
                PTr = [(sbuf(st, "PT%d" % i, [128, 512], BF16), Buf()) for i in range(6)]
                comb = [(sbuf(st, "oc%d" % i, [128, 512], F32), sbuf(st, "uc%d" % i, [128, 512], F32), [Buf() for _ in range(4)]) for i in range(2)]
                rec = sbuf(st, "rec", [128, 512], F32)
                tmpo = sbuf(st, "tmpo", [128, 512], F32)
                return (Sb, Ob, Ub, PTr, comb, rec, tmpo, Buf())

            stop(0.6)
            with contextlib.ExitStack() as sb0:
                KTs = sbuf(sb0, "KTs", [128, 4, DEC], BF16)
                Vs = sbuf(sb0, "Vs", [128, 512], BF16)
                B_KTs = Buf()
                B_Vs = Buf()
                KTc = sbuf(sb0, "KTc", [128, 4, 2048], BF16)
                Vc = sbuf(sb0, "Vc", [128, NCB, 512], BF16)
                B_KTc = [Buf() for _ in range(NCB)]
                B_Vc = [Buf() for _ in range(NCB)]
                zfs = sbuf(sb0, "zfs", [128, 1, 8], F32)
                B_zfs = Buf()
                lfs = sbuf(sb0, "lfs", [128, NCB + 1, 8], F32)
                B_lfs = Buf()
                bias_s = sbuf(sb0, "bias_s", [128, NCB + 1, 8], F32)
                B_bias_s = Buf()
                d_lf = kb.dsem()
                kb.op("pool", lambda e: e.memset(lfs[:, NCB, :], 0.0), writes=[B_lfs])
                kb.dma(d_lf, lambda q: q.dma_start(out=lfs[:, 0:NCB, :], in_=clf.rearrange("(b k) h -> k b h", k=128)),
                       writes=[B_lfs])
                with contextlib.ExitStack() as s1:
                    R = kv_res(s1, depth=6)
                    kv_block(R, xnT_s, B_xnTs, 0, DEC, zfs[0:DEC, 0, :], B_zfs, ks_o, vs_o, KTs[:, :, :], B_KTs, Vs[0:DEC, :], B_Vs)()

                    ld_sems = [(kb.dsem(), kb.dsem()) for _ in range(6)]

                    def cload(cb):
                        ks_, Bk_, _ = R["kst"][(cb + 1) % 6]
                        vs_, Bv_, _ = R["vst"][(cb + 1) % 6]
                        dk_, dv_ = ld_sems[(cb + 1) % 6]
                        rows = slice(cb * 128, (cb + 1) * 128)
                        kb.dma(dk_, lambda q: q.dma_start(out=ks_[:], in_=ck[rows, :]), writes=[Bk_])
                        kb.dma(dv_, lambda q: q.dma_start(out=vs_[:], in_=cv[rows, :]), writes=[Bv_])

                    for cb in range(4):
                        cload(cb)
                    ptail = None
                    for cb in range(NCB):
                        if cb + 4 < NCB:
                            cload(cb + 4)
                        ks_, Bk_, dk_ = R["kst"][(cb + 1) % 6]
                        vs_, Bv_, dv_ = R["vst"][(cb + 1) % 6]
                        rows = slice(cb * 128, (cb + 1) * 128)
                        t_ = ingest(R, 128, ks_[:, :], Bk_, vs_[:, :], Bv_, KTc[:, :, rows], B_KTc[cb], Vc[:, cb, :], B_Vc[cb])
                        if ptail is not None:
                            ptail()
                        ptail = t_
                        bi[0] += 1
                    ptail()
                    kb.barrier()
                stop(1.1)
                with contextlib.ExitStack() as sf:
                    css = sbuf(sf, "css", [128, NCB + 1, 8], F32)
                    tots = sbuf(sf, "tots", [128, NCB + 1, 8], F32)
                    pres = sbuf(sf, "pres", [128, NCB + 1, 8], F32)
                    Fs = sbuf(sf, "Fs", [128, NCB + 1, 8], F32)
                    sa_ = sbuf(sf, "sa_", [128, 32, 8], F32)
                    sb2 = sbuf(sf, "sb2", [128, 32, 8], F32)
                    B_f = Buf()
                    ps1 = psum(sf, "ps1", [128, 512], F32)
                    ps2 = psum(sf, "ps2", [128, 512], F32)
                    ps3 = psum(sf, "ps3", [8, 512], F32)
                    B_p1, B_p2, B_p3 = PB(), PB(), PB()
                    D = "dve"
                    ez, B_ez = logf_chain(sf, zfs, B_zfs, DEC, 1)
                    kb.op(D, lambda e: e.tensor_scalar(out=lfs[0:DEC, NCB, :], in0=ez[0:DEC, 0, :], scalar1=-1.0,
                                                       scalar2=None, op0=ALU.mult), reads=[B_ez], writes=[B_lfs])
                    kb.dma(d_out, lambda q: q.dma_start(out=lfs_o, in_=lfs[0:DEC, NCB, :]), reads=[B_lfs])
                    n_s = (NCB + 1) * 8
                    lfs2 = lfs[:, :, :].rearrange("p b h -> p (b h)")
                    kb.op("pe", lambda e: e.matmul(ps1[:, 0:n_s], lhsT=tri_f[:], rhs=lfs2, start=True, stop=True),
                          reads=[B_lfs, B_const], writes=[B_p1])
                    kb.op("pe", lambda e: e.matmul(ps2[:, 0:n_s], lhsT=ones_f[:], rhs=lfs2, start=True, stop=True),
                          reads=[B_lfs, B_const], writes=[B_p2])
                    kb.op(D, lambda e: e.tensor_copy(out=css[:, :, :].rearrange("p b h -> p (b h)"), in_=ps1[:, 0:n_s]),
                          reads=[B_p1], writes=[B_f])
                    kb.op(D, lambda e: e.tensor_copy(out=tots[:, :, :].rearrange("p b h -> p (b h)"), in_=ps2[:, 0:n_s]),
                          reads=[B_p2], writes=[B_f])
                    scan_incl([sa_, sb2], B_f, tots[:, :, :], NCB + 1, pres[:, :, :])
                    kb.op(D, lambda e: e.tensor_tensor(out=Fs[:], in0=pres[:], in1=css[:], op=ALU.add), reads=[B_f], writes=[B_f])
                    kb.op(D, lambda e: e.tensor_tensor(out=Fs[:], in0=Fs[:], in1=tots[:], op=ALU.subtract),
                          reads=[B_f], writes=[B_f])
                    kb.op(D, lambda e: e.tensor_tensor(out=bias_s[:], in0=pres[:, NCB, :].unsqueeze(1).to_broadcast(
                        [128, NCB + 1, 8]), in1=Fs[:], op=ALU.subtract), reads=[B_f], writes=[B_bias_s])
                    kb.op("pe", lambda e: e.transpose(ps3[:, 0:128], Fs[:, NCB, :], ident_f[:]),
                          reads=[B_f, B_const], writes=[B_p3])
                    ft = sbuf(sf, "ft", [8, 512], F32)
                    B_ft = Buf()
                    kb.op(D, lambda e: e.tensor_copy(out=ft[:, 0:128], in_=ps3[:, 0:128]), reads=[B_p3], writes=[B_ft])
                    kb.op(D, lambda e: e.tensor_scalar(out=Eq_s[0:8, :], in0=ft[:, 0:DEC], scalar1=ft[:, DEC - 1:DEC],
                                                       scalar2=8.0, op0=ALU.subtract, op1=ALU.mult),
                          reads=[B_ft], writes=[B_Eqs])
                    kb.barrier()
                stop(1.2)
                with contextlib.ExitStack() as s3:
                    RA = attn_res(s3)
                    blocks = []
                    for cb in range(NCB):
                        rows = slice(cb * 128, (cb + 1) * 128)
                        blocks.append(dict(nk=128, c0=0, KT=lambda c, rows=rows: KTc[:, c, rows], BK=B_KTc[cb],
                                           V=lambda c, cb=cb: Vc[:, cb, c * 128:(c + 1) * 128], BV=B_Vc[cb],
                                           bias=lambda h, cb=cb: bias_s[:, cb, h:h + 1], fix=None))
                    blocks.append(dict(nk=DEC, c0=0, KT=lambda c: KTs[:, c, :], BK=B_KTs,
                                       V=lambda c: Vs[0:DEC, c * 128:(c + 1) * 128], BV=B_Vs,
                                       bias=lambda h: bias_s[0:DEC, NCB, h:h + 1], fix="tri"))
                    attend(RA, DEC, 0, QT_s, lambda c: B_QTs, Eq_s, B_Eqs, blocks, lambda c: catT_s[:, c, :],
                           lambda c: B_catTs[c], B_bias_s)
                    kb.barrier()
                kb.barrier()

            stop(1.5)
            with contextlib.ExitStack() as sb_:
                KT = sbuf(sb_, "KT", [128, 4, 4096], BF16)
                V = sbuf(sb_, "V", [128, NB_ALL, 512], BF16)
                B_KT = [Buf() for _ in range(NB_ALL)]
                B_V = [Buf() for _ in range(NB_ALL)]
                zfb = sbuf(sb_, "zfb", [128, NB_ALL, 8], F32)
                B_zf = Buf()
                B_lf = Buf()
                B_bias = Buf()
                with contextlib.ExitStack() as s1:
                    xt_ring = [(sbuf(s1, "xt%d" % i, [128, 1024], F32), Buf(), kb.dsem()) for i in range(3)]
                    xs_ring = [(sbuf(s1, "xsb%d" % i, [128, 1024], BF16), Buf()) for i in range(2)]
                    xnT_ring = [(sbuf(s1, "xnT%d" % i, [128, 8, 128], BF16), Buf()) for i in range(4)]
                    tp_ring = [(psum(s1, "tp%d" % i, [128, 8, 128], BF16), PB()) for i in range(2)]
                    R = kv_res(s1)
                    pst = {"i": 0}

                    def prep_tile1(src, nblk, xnT, Bx):
                        for b in range(nblk):
                            prep_block(pst, src[b * 128:(b + 1) * 128, :], 128, xnT, b * 128, Bx, xt_ring, xs_ring, tp_ring)

                    def pp(i):
                        xnT, Bx = xnT_ring[i % 4]
                        return prep_parts(pst, xall[i * 128:(i + 1) * 128, :], 128, xnT, 0, Bx, xt_ring, xs_ring, tp_ring)

                    parts = [pp(i) for i in range(NB_ALL)]
                    parts[0][0]()
                    parts[1][0]()
                    parts[0][1]()
                    prev_tail = [None]
                    for blk in range(NB_ALL):
                        xnT, Bx = xnT_ring[blk % 4]
                        own = blk < NB_OWN
                        rows = slice(blk * 128, (blk + 1) * 128)
                        if blk + 2 < NB_ALL:
                            parts[blk + 2][0]()
                        def mid(blk=blk, prev=prev_tail):
                            if prev[0] is not None:
                                prev[0]()
                            if blk + 1 < NB_ALL:
                                parts[blk + 1][1]()
                        prev_tail[0] = None if False else prev_tail[0]
                        t_ = kv_block(R, xnT, Bx, 0, 128, zfb[:, blk, :], B_zf, k_o[rows, :] if own else None,
                                      v_o[rows, :] if own else None, KT[:, :, rows], B_KT[blk], V[:, blk, :], B_V[blk],
                                      mid=mid)
                        prev_tail = [t_]
                    prev_tail[0]()
                    kb.barrier()

                lf = sbuf(sb_, "lf", [128, NB_ALL, 8], F32)
                bias = sbuf(sb_, "bias", [128, NT, NB_ALL, 8], F32)
                biasp = sbuf(sb_, "biasp", [128, NT, 4, 8], F32)
                if STOP_AFTER >= 2:
                    with contextlib.ExitStack() as sf:
                        cs = sbuf(sf, "cs", [128, NB_ALL, 8], F32)
                        tot = sbuf(sf, "tot", [128, NB_ALL, 8], F32)
                        pre = sbuf(sf, "pre", [128, NB_ALL, 8], F32)
                        Ff = sbuf(sf, "Ff", [128, NB_ALL, 8], F32)
                        sa_ = sbuf(sf, "sa_", [128, 32, 8], F32)
                        sb2 = sbuf(sf, "sb2", [128, 32, 8], F32)
                        Rr = sbuf(sf, "Rr", [128, 8], F32)
                        ft = sbuf(sf, "ft", [8, 512], F32)
                        B_ft = Buf()
                        B_f = Buf()
                        ps1 = psum(sf, "ps1", [128, 512], F32)
                        ps2 = psum(sf, "ps2", [128, 512], F32)
                        ps3 = psum(sf, "ps3", [8, 512], F32)
                        B_p1, B_p2, B_p3 = PB(), PB(), PB()
                        D = "dve"
                        ez, B_ez = logf_chain(sf, zfb, B_zf, 128, NB_ALL)
                        kb.op(D, lambda e: e.tensor_scalar(out=lf[:], in0=ez[:], scalar1=-1.0, scalar2=None,
                                                           op0=ALU.mult), reads=[B_ez], writes=[B_lf])
                        kb.dma(d_out, lambda q: q.dma_start(out=lf_o.rearrange("(b k) h -> k b h", k=128), in_=lf[:, 0:NB_OWN, :]),
                               reads=[B_lf])
                        lf2 = lf[:, :, :].rearrange("p b h -> p (b h)")
                        kb.op("pe", lambda e: e.matmul(ps1[:, 0:256], lhsT=tri_f[:], rhs=lf2, start=True, stop=True),
                              reads=[B_lf, B_const], writes=[B_p1])
                        kb.op("pe", lambda e: e.matmul(ps2[:, 0:256], lhsT=ones_f[:], rhs=lf2, start=True, stop=True),
                              reads=[B_lf, B_const], writes=[B_p2])
                        kb.op(D, lambda e: e.tensor_copy(out=cs[:, :, :].rearrange("p b h -> p (b h)"), in_=ps1[:, 0:256]),
                              reads=[B_p1], writes=[B_f])
                        kb.op(D, lambda e: e.tensor_copy(out=tot[:, :, :].rearrange("p b h -> p (b h)"), in_=ps2[:, 0:256]),
                              reads=[B_p2], writes=[B_f])
                        kb.op(D, lambda e: e.tensor_tensor(out=Ff[:, 0:16, :], in0=tot[:, 0:16, :], in1=tot[:, 16:32, :],
                                                           op=ALU.add), reads=[B_f], writes=[B_f])
                        scan_incl([sa_, sb2], B_f, Ff[:, 0:16, :], 16, pre[:, 16:32, :])
                        kb.op(D, lambda e: e.tensor_tensor(out=pre[:, 16:32, :], in0=pre[:, 16:32, :], in1=Ff[:, 0:16, :],
                                                           op=ALU.subtract), reads=[B_f], writes=[B_f])
                        kb.op(D, lambda e: e.scalar_tensor_tensor(out=pre[:, 0:16, :], in0=tot[:, 16:32, :], scalar=pq[:, 0:1],
                                                                  in1=pre[:, 16:32, :], op0=ALU.mult, op1=ALU.add),
                              reads=[B_f, B_const], writes=[B_f])
                        kb.op(D, lambda e: e.scalar_tensor_tensor(out=pre[:, 16:32, :], in0=tot[:, 0:16, :], scalar=pq[:, 1:2],
                                                                  in1=pre[:, 16:32, :], op0=ALU.mult, op1=ALU.add),
                              reads=[B_f, B_const], writes=[B_f])
                        kb.op(D, lambda e: e.tensor_tensor(out=Ff[:], in0=pre[:], in1=cs[:], op=ALU.add), reads=[B_f], writes=[B_f])
                        for s in range(NT):
                            lb = 4 * s + 3
                            kb.op(D, lambda e, lb=lb: e.tensor_tensor(out=Rr[:], in0=pre[:, lb, :], in1=tot[:, lb, :], op=ALU.add),
                                  reads=[B_f], writes=[B_f])
                            kb.op(D, lambda e, s=s: e.tensor_tensor(out=bias[:, s, :, :],
                                                                    in0=Rr[:, :].unsqueeze(1).to_broadcast([128, NB_ALL, 8]),
                                                                    in1=Ff[:], op=ALU.subtract), reads=[B_f], writes=[B_bias])
                            kb.op(D, lambda e, s=s: e.tensor_scalar(out=biasp[:, s, :, :], in0=bias[:, s, NB_OWN + 4 * s:NB_OWN + 4 * s + 4, :],
                                                                    scalar1=pq[:, 2:3], scalar2=None, op0=ALU.add),
                                  reads=[B_bias, B_const], writes=[B_bias])
                            for b in range(4):
                                kb.op("pe", lambda e, s=s, b=b: e.transpose(ps3[:, b * 128:(b + 1) * 128], Ff[:, 4 * s + b, :],
                                                                            ident_f[:]),
                                      reads=[B_f, B_const], writes=[B_p3], mark=(b == 3))
                            kb.op(D, lambda e: e.tensor_copy(out=ft[:, :], in_=ps3[:, :]), reads=[B_p3], writes=[B_ft])
                            kb.op(D, lambda e, s=s: e.tensor_scalar(out=Eq[0:8, s * 512:(s + 1) * 512], in0=ft[:, :],
                                                                    scalar1=ft[:, 511:512], scalar2=8.0, op0=ALU.subtract,
                                                                    op1=ALU.mult), reads=[B_ft], writes=[B_Eq[s]])
                        kb.barrier()

                if STOP_AFTER >= 3:
                    with contextlib.ExitStack() as s3:
                        RA = attn_res(s3)

                        def mkblk(s, blk, c0, fix):
                            rows = slice(blk * 128, (blk + 1) * 128)
                            return dict(nk=128, c0=c0, KT=lambda c: KT[:, c, rows], BK=B_KT[blk],
                                        V=lambda c: V[:, blk, c * 128:(c + 1) * 128], BV=B_V[blk],
                                        bias=lambda h: bias[:, s, blk, h:h + 1],
                                        biasp=lambda h: biasp[:, s, blk - NB_OWN - 4 * s, h:h + 1], fix=fix)

                        for s in range(NT):
                            blocks = []
                            for i in range(4 * s):
                                blocks.append(mkblk(s, i, 0, None))
                            for jj in range(4):
                                blocks.append(mkblk(s, 4 * s + jj, 128 * jj, "tri"))
                            for i in range(4 * s):
                                blocks.append(mkblk(s, NB_OWN + i, 0, None))
                            for jj in range(4):
                                blocks.append(mkblk(s, NB_OWN + 4 * s + jj, 128 * jj, "p"))
                            attend(RA, 512, s * 512, QT, lambda c, s=s: B_QT[c][s], Eq, B_Eq[s], blocks,
                                   lambda c, s=s: catT[:, c, s * 512:(s + 1) * 512], lambda c, s=s: B_catT[c][s], B_bias)
                        kb.barrier()
                kb.barrier()

            if STOP_AFTER >= 4:
                with contextlib.ExitStack() as sc:
                    w_o = sbuf(sc, "w_o", [128, 8, 1024], BF16)
                    gfin = sbuf(sc, "gfin", [128, 1024], F32)
                    B_gfin = Buf()
                    d_gf = kb.dsem()
                    kb.dma(d_gf, lambda q: q.dma_start(out=gfin[:], in_=gfin_d), writes=[B_gfin])
                    B_wo = Buf()
                    wst = [(sbuf(sc, "wst%d" % i, [128, 1024], F32), Buf(), kb.dsem()) for i in range(2)]
                    xr = [(sbuf(sc, "xr%d" % i, [128, 1024], F32), Buf(), kb.dsem()) for i in range(3)]
                    hb = [(sbuf(sc, "hb%d" % i, [128, 1024], F32), Buf()) for i in range(2)]
                    yb = [(sbuf(sc, "yb%d" % i, [128, 1024], F32), Buf(), kb.dsem()) for i in range(2)]
                    yps = [(psum(sc, "yps%d" % i, [128, 1024], F32), PB()) for i in range(2)]
                    B_wo = [(Buf(), Buf()) for _ in range(8)]
                    for kc in range(8):
                        w, Bw, dw = wst[kc % 2]
                        kb.dma(dw, lambda q, w=w, kc=kc: q.dma_start(out=w[:], in_=w_out[kc * 128:(kc + 1) * 128, :]), writes=[Bw])
                        kb.op("act", lambda e, w=w, kc=kc: e.activation(out=w_o[:, kc, 0:512], in_=w[:, 0:512], func=AF.Identity),
                              reads=[Bw], writes=[B_wo[kc][0]])
                        kb.op("dve", lambda e, w=w, kc=kc: e.tensor_copy(out=w_o[:, kc, 512:1024], in_=w[:, 512:1024]),
                              reads=[Bw], writes=[B_wo[kc][1]])
                    blks = [(i, 128) for i in range(NB_OWN)] + [(NB_OWN, DEC)]

                    def load_x(i):
                        b, m = blks[i]
                        x_, Bx_, dx_ = xr[i % 3]
                        src = xall[b * 128:(b + 1) * 128, :] if b < NB_OWN else xs
                        kb.dma(dx_, lambda q: q.dma_start(out=x_[0:m, :], in_=src), writes=[Bx_])

                    load_x(0)
                    load_x(1)
                    fin5 = [None]
                    for i, (b, m) in enumerate(blks):
                        if i + 2 < len(blks):
                            load_x(i + 2)
                        x_, Bx_, dx_ = xr[i % 3]
                        yp, Byp = yps[i % 2]
                        h_, Bh = hb[i % 2]
                        y_, By, dy = yb[i % 2]
                        t, bb = divmod(b, 4)
                        for half in range(2):
                            for kc in range(8):
                                if b < NB_OWN:
                                    lhsT = catT[:, kc, b * 128:(b + 1) * 128]
                                    Bc = B_catT[kc][t]
                                else:
                                    lhsT = catT_s[:, kc, :]
                                    Bc = B_catTs[kc]
                                kb.op("pe", lambda e, lhsT=lhsT, kc=kc, half=half, yp=yp: e.matmul(
                                    yp[0:m, half * 512:(half + 1) * 512], lhsT=lhsT, rhs=w_o[:, kc, half * 512:(half + 1) * 512],
                                    start=(kc == 0), stop=(kc == 7)), reads=[Bc, *B_wo[kc]], writes=[Byp],
                                    mark=(kc == 7 and half == 1))
                        si = ss_idx[0] % 64
                        ss_idx[0] += 1
                        kb.op("dve", lambda e, h_=h_, yp=yp, x_=x_: e.tensor_tensor(out=h_[0:m, :], in0=yp[0:m, :], in1=x_[0:m, :],
                                                                                    op=ALU.add), reads=[Byp, Bx_], writes=[Bh])
                        kb.op("act", lambda e, h_=h_, si=si: e.activation(out=junk[0:m, :], in_=h_[0:m, :], func=AF.Square,
                                                                          accum_out=ss[0:m, si:si + 1]),
                              reads=[Bh], writes=[B_junk, B_ss[si]])
                        kb.op("pool", lambda e, si=si: e.tensor_scalar(out=rstd[0:m, si:si + 1], in0=ss[0:m, si:si + 1],
                                                                       scalar1=1.0 / 1024.0, scalar2=EPS, op0=ALU.mult,
                                                                       op1=ALU.add), reads=[B_ss[si]], writes=[B_ss[si]])
                        kb.op("pool", lambda e, si=si: e.tensor_tensor(out=rstd[0:m, si:si + 1], in0=rstd[0:m, si:si + 1],
                                                                       in1=negh[0:m, 0:1], op=ALU.pow),
                              reads=[B_ss[si]], writes=[B_ss[si]])
                        if fin5[0] is not None:
                            fin5[0]()

                        def fin(y_=y_, h_=h_, si=si, By=By, Bh=Bh, dy=dy, b=b, m=m):
                            kb.op("dve", lambda e: e.scalar_tensor_tensor(
                                out=y_[0:m, :], in0=h_[0:m, :], scalar=rstd[0:m, si:si + 1], in1=gfin[0:m, :], op0=ALU.mult,
                                op1=ALU.mult), reads=[Bh, B_ss[si], B_gfin], writes=[By])
                            dst = y_o[b * 128:(b + 1) * 128, :] if b < NB_OWN else ys_o
                            kb.dma(dy, lambda q: q.dma_start(out=dst, in_=y_[0:m, :]), reads=[By], q="pool")
                        fin5[0] = fin
                    fin5[0]()
                    kb.barrier()

        except StopBuild:
            pass

        for d in kb.dsems:
            if d.n > 0:
                kb._wait("sp", (d.sem, d.n, d.key))
    return nc


_NC = None


def _get_nc():
    global _NC
    if _NC is None:
        _NC = build_nc()
    return _NC


def kernel(x_prompt, x_sample, mem_prompt, cache_fox_k, cache_fox_v, cache_fox_logf, cache_mem_k, cache_mem_v,
           state_pool, g_norm, w_in, b_f, w_pool, pool_scale, g_mem, w_mem_kv, w_out, g_final):
    f = np.float32
    x_prompt = np.asarray(x_prompt, f)
    x_sample = np.asarray(x_sample, f)
    nc = _get_nc()
    wpl = np.zeros((128, 2, 64), f)
    wp = np.asarray(w_pool, f)[0]
    for cc in range(2):
        for e in range(2):
            wpl[e * 64:(e + 1) * 64, cc, :] = wp[2 * cc + e]
    common = {
        "w_in": np.ascontiguousarray(np.asarray(w_in, f)[0]),
        "w_out": np.ascontiguousarray(np.asarray(w_out, f)[0]),
        "w_mkv": np.ascontiguousarray(np.asarray(w_mem_kv, f)[0]),
        "gl": np.ascontiguousarray(np.asarray(g_norm, f)[0].reshape(8, 128).T),
        "gml": np.ascontiguousarray(np.asarray(g_mem, f)[0].reshape(8, 128).T),
        "gfin": np.ascontiguousarray(np.broadcast_to(np.asarray(g_final, f)[None, :], (128, 1024))),
        "bfb": np.ascontiguousarray(np.broadcast_to(np.asarray(b_f, f)[0][None, :], (128, 8))),
        "psc": np.ascontiguousarray(np.asarray(pool_scale, f)[0].reshape(2, 128).T),
        "wpl": wpl,
    }
    wins = (2, 4, 8, 16)
    in_maps = []
    for c in range(8):
        b, p = divmod(c, 2)
        xb = x_prompt[b].reshape(32, 128, 1024)
        own = xb[p::2]
        oth = xb[1 - p::2]
        xall = np.ascontiguousarray(np.concatenate([own, oth], 0).reshape(4096, 1024))
        halo = np.zeros((16, 16, 1024), f)
        for i in range(16):
            g0 = (2 * i + p) * 128
            if g0 >= 16:
                halo[i] = x_prompt[b, g0 - 16:g0]
        rc = np.zeros((128, 2, 16), f)
        for cc in range(2):
            for e in range(2):
                w = wins[2 * cc + e]
                if p == 0:
                    cnt = np.minimum(w, np.arange(16) + 1).astype(f)
                else:
                    cnt = np.full(16, w, f)
                rc[e * 64:(e + 1) * 64, cc, :] = (1.0 / cnt)[None, :]
        pqv = np.zeros((128, 3), f)
        pqv[:, 0] = p
        pqv[:, 1] = 1 - p
        pqv[:, 2] = 0.0 if p == 1 else -30000.0
        sp = np.asarray(state_pool, f)[0, c]
        spT = np.zeros((128, 2, 16), f)
        spT[:, :, 1:16] = sp.T.reshape(2, 128, 15).transpose(1, 0, 2)
        m = dict(common)
        m.update({
            "xall": xall, "xhalo": np.ascontiguousarray(halo.reshape(256, 1024)),
            "xmem": np.ascontiguousarray(np.asarray(mem_prompt, f)[b]),
            "xs": np.ascontiguousarray(x_sample[c]),
            "ck": np.ascontiguousarray(np.asarray(cache_fox_k, f)[0, c].reshape(2048, 512)),
            "cv": np.ascontiguousarray(np.asarray(cache_fox_v, f)[0, c].reshape(2048, 512)),
            "clf": np.ascontiguousarray(np.asarray(cache_fox_logf, f)[0, c]),
            "cmk": np.ascontiguousarray(np.asarray(cache_mem_k, f)[0, c].reshape(256, 256)),
            "cmv": np.ascontiguousarray(np.asarray(cache_mem_v, f)[0, c].reshape(256, 256)),
            "spT": spT, "pq": pqv, "rc": rc,
        })
        in_maps.append(m)
    res = run_bass_kernel_spmd(nc, in_maps, core_ids=list(range(8)))
    R = res.results
    y_p = np.zeros((4, 32, 128, 1024), f)
    k_p = np.zeros((4, 32, 128, 512), f)
    v_p = np.zeros((4, 32, 128, 512), f)
    lf_p = np.zeros((4, 32, 128, 8), f)
    mk_p = np.zeros((1, 4, 256, 4, 64), f)
    mv_p = np.zeros((1, 4, 256, 4, 64), f)
    pool_p = np.zeros((1, 4, 15, 256), f)
    y_s = np.zeros((8, DEC, 1024), f)
    k_s = np.zeros((1, 8, DEC, 8, 64), f)
    v_s = np.zeros((1, 8, DEC, 8, 64), f)
    lf_s = np.zeros((1, 8, DEC, 8), f)
    pool_s = np.zeros((1, 8, 15, 256), f)
    for c in range(8):
        b, p = divmod(c, 2)
        r = R[c]
        y_p[b, p::2] = r["y_o"].reshape(16, 128, 1024)
        k_p[b, p::2] = r["k_o"].reshape(16, 128, 512)
        v_p[b, p::2] = r["v_o"].reshape(16, 128, 512)
        lf_p[b, p::2] = r["lf_o"].reshape(16, 128, 8)
        if p == 1:
            mk_p[0, b] = r["mk_o"].reshape(256, 4, 64)
            mv_p[0, b] = r["mv_o"].reshape(256, 4, 64)
            po = r["pool_o"]
            pool_p[0, b] = po.transpose(1, 0, 2).reshape(256, 16)[:, 1:].T
        y_s[c] = r["ys_o"]
        k_s[0, c] = r["ks_o"].reshape(DEC, 8, 64)
        v_s[0, c] = r["vs_o"].reshape(DEC, 8, 64)
        lf_s[0, c] = r["lfs_o"]
        pool_s[0, c] = r["pools_o"].transpose(1, 0, 2).reshape(256, 16)[:, 1:].T
    return (y_p.reshape(4, 4096, 1024), y_s, k_p.reshape(1, 4, 4096, 8, 64), v_p.reshape(1, 4, 4096, 8, 64),
            lf_p.reshape(1, 4, 4096, 8), mk_p, mv_p, pool_p, k_s, v_s, lf_s, pool_s)
```

```python
import contextlib
import numpy as np
import concourse.bass as bass
import concourse.mybir as mybir
from concourse.bass_utils import run_bass_kernel_spmd

F32 = mybir.dt.float32
BF16 = mybir.dt.bfloat16
AF = mybir.ActivationFunctionType
ALU = mybir.AluOpType

NB_OWN = 16
NB_ALL = 32
NT = 4
S_OWN = 2048
DEC = 64
NCB = 16
EPS = 1e-6
STOP_AFTER = 99
import os
STOP_AT = float(os.environ.get('KSTOP', '99'))


class StopBuild(Exception):
    pass


class Buf:
    __slots__ = ("w", "r", "excl")

    def __init__(self, excl=False):
        self.w = None
        self.r = {}
        self.excl = excl


def PB():
    return Buf(excl=True)


class DSem:
    def __init__(self, sem, key):
        self.sem = sem
        self.key = key
        self.n = 0


class KB:
    def __init__(self, nc, es):
        self.nc = nc
        self.es = es
        self.engs = {"pe": nc.tensor, "act": nc.scalar, "dve": nc.vector, "pool": nc.gpsimd, "sp": nc.sync}
        self.sem = {k: es.enter_context(nc.semaphore("s_" + k)) for k in self.engs}
        self.cnt = {k: 0 for k in self.engs}
        self.seen = {k: {} for k in self.engs}
        self.pend = {k: ([], []) for k in self.engs}
        self.dsems = []
        self.nd = 0
        self.dead = False
        self.nops = 0
        self.limit = int(os.environ.get('KLIMIT', '0'))
        self.trace = []

    def dsem(self):
        self.nd += 1
        d = DSem(self.es.enter_context(self.nc.semaphore("d%d" % self.nd)), "d%d" % self.nd)
        self.dsems.append(d)
        return d

    def _wait(self, eng, ev):
        if ev is None:
            return
        sem, val, key = ev
        if self.seen[eng].get(key, 0) >= val:
            return
        self.seen[eng][key] = val
        self.engs[eng].wait_ge(sem, val)

    def _deps(self, eng, reads, writes):
        for b in reads:
            self._wait(eng, b.w)
        for b in writes:
            self._wait(eng, b.w)
            for ev in list(b.r.values()):
                self._wait(eng, ev)

    @staticmethod
    def _commit(ev, reads, writes):
        for b in reads:
            b.r[ev[2]] = ev
        for b in writes:
            b.w = ev
            b.r = {}

    def _tick(self, what):
        self.nops += 1
        if self.limit:
            self.trace.append(what)
        if self.limit and self.nops >= self.limit and (what[0] != "pe" or what[1]):
            self.barrier()
            self.dead = True
            print("KLIMIT stop at", self.nops, self.trace[-6:])

    def op(self, eng, fn, reads=(), writes=(), mark=True):
        if self.dead:
            return None
        if any(b.excl for b in reads):
            writes = list(writes) + [b for b in reads if b.excl]
            reads = [b for b in reads if not b.excl]
        self._deps(eng, reads, writes)
        ins = fn(self.engs[eng])
        pr, pw = self.pend[eng]
        pr.extend(reads)
        pw.extend(writes)
        if not mark:
            return None
        self.cnt[eng] += 1
        ins.then_inc(self.sem[eng], 1)
        ev = (self.sem[eng], self.cnt[eng], eng)
        self._commit(ev, pr, pw)
        self.pend[eng] = ([], [])
        import traceback
        self._tick((eng, mark, traceback.extract_stack(limit=3)[0].lineno))
        return ev

    def dma(self, ds, fn, reads=(), writes=(), q="sp"):
        if self.dead:
            return None
        self._deps(q, reads, writes)
        ins = fn(self.engs[q])
        ds.n += 16
        ins.then_inc(ds.sem, 16)
        ev = (ds.sem, ds.n, ds.key)
        self._commit(ev, reads, writes)
        import traceback
        self._tick(("dma", True, traceback.extract_stack(limit=3)[0].lineno))
        return ev

    def barrier(self):
        if self.dead:
            return
        evs = [(self.sem[k], self.cnt[k], k) for k in self.engs if self.cnt[k] > 0]
        for d in self.dsems:
            if d.n > 0:
                evs.append((d.sem, d.n, d.key))
        for eng in self.engs:
            for ev in evs:
                if ev[2] != eng:
                    self._wait(eng, ev)


class _View2:
    def __init__(self, full, shape):
        self.full = full
        self.shape = shape

    def __getitem__(self, idx):
        if not isinstance(idx, tuple):
            idx = (idx,)
        idx = idx + (slice(None),) * (2 - len(idx))
        p, f = idx
        p = _norm(p, self.shape[0])
        f = _norm(f, self.shape[1])
        return self.full[p, f]


class _View3:
    def __init__(self, full, shape):
        self.full = full
        self.shape = shape

    def __getitem__(self, idx):
        if not isinstance(idx, tuple):
            idx = (idx,)
        idx = idx + (slice(None),) * (3 - len(idx))
        p, a, b = idx
        P_, A, B = self.shape
        p = _norm(p, P_)
        v = self.full[p, 0:A * B].rearrange("p (a b) -> p a b", b=B)
        return v[:, a, b]


def _norm(sl, n):
    if isinstance(sl, slice):
        a = 0 if sl.start is None else sl.start
        b = n if sl.stop is None else sl.stop
        return slice(a, b)
    return sl


def build_nc():
    nc = bass.Bass("TRN2", target_bir_lowering=False)

    def din(name, shape):
        return nc.dram_tensor(name, list(shape), F32, kind="ExternalInput").ap()

    def dout(name, shape):
        return nc.dram_tensor(name, list(shape), F32, kind="ExternalOutput").ap()

    xall = din("xall", [4096, 1024])
    xhalo = din("xhalo", [256, 1024])
    xmem = din("xmem", [256, 1024])
    xs = din("xs", [DEC, 1024])
    ck = din("ck", [2048, 512])
    cv = din("cv", [2048, 512])
    clf = din("clf", [2048, 8])
    cmk = din("cmk", [256, 256])
    cmv = din("cmv", [256, 256])
    spT = din("spT", [128, 2, 16])
    w_in = din("w_in", [1024, 3080])
    w_out = din("w_out", [1024, 1024])
    w_mkv = din("w_mkv", [1024, 512])
    gl_d = din("gl", [128, 8])
    gml_d = din("gml", [128, 8])
    gfin_d = din("gfin", [128, 1024])
    bf_d = din("bfb", [128, 8])
    pq_d = din("pq", [128, 3])
    rc_d = din("rc", [128, 2, 16])
    psc_d = din("psc", [128, 2])
    wp_d = din("wpl", [128, 2, 64])

    y_o = dout("y_o", [S_OWN, 1024])
    k_o = dout("k_o", [S_OWN, 512])
    v_o = dout("v_o", [S_OWN, 512])
    lf_o = dout("lf_o", [S_OWN, 8])
    mk_o = dout("mk_o", [256, 256])
    mv_o = dout("mv_o", [256, 256])
    pool_o = dout("pool_o", [128, 2, 16])
    ys_o = dout("ys_o", [DEC, 1024])
    ks_o = dout("ks_o", [DEC, 512])
    vs_o = dout("vs_o", [DEC, 512])
    lfs_o = dout("lfs_o", [DEC, 8])
    pools_o = dout("pools_o", [128, 2, 16])

    with contextlib.ExitStack() as es:
        kb = KB(nc, es)

        def stop(level):
            if os.environ.get('KVERB'):
                print('stop point', level, 'nops', kb.nops)
            if STOP_AT <= level:
                kb.barrier()
                kb.dead = True

        uid = [0]

        def sbuf(st, name, shape, dt):
            uid[0] += 1
            return st.enter_context(nc.sbuf_tensor("sb%d_%s" % (uid[0], name), list(shape), dt))

        def psum(st, name, shape, dt):
            uid[0] += 1
            n = int(np.prod(shape[1:])) * (4 if dt == F32 else 2)
            assert n <= 2048 or n == 4096, (name, n)
            full = st.enter_context(nc.psum_tensor("ps%d_%s" % (uid[0], name), [128, (max(n, 2048)) // (4 if dt == F32 else 2)], dt))
            if len(shape) == 2:
                return _View2(full, shape)
            return _View3(full, shape)

        ident_bf = sbuf(es, "ident_bf", [128, 128], BF16)
        ident_f = sbuf(es, "ident_f", [128, 128], F32)
        tri_f = sbuf(es, "tri_f", [128, 128], F32)
        ones_f = sbuf(es, "ones_f", [128, 128], F32)
        ones_bf = sbuf(es, "ones_bf", [128, 128], BF16)
        sel = sbuf(es, "selz", [128, 8, 128], BF16)
        negh = sbuf(es, "negh", [128, 1], F32)
        bfb = sbuf(es, "bfb", [128, 8], F32)
        pq = sbuf(es, "pq", [128, 3], F32)
        rc = sbuf(es, "rc", [128, 2, 16], F32)
        psc = sbuf(es, "psc", [128, 2], F32)
        gl = sbuf(es, "gl", [128, 8], F32)
        gml = sbuf(es, "gml", [128, 8], F32)
        wp_f = sbuf(es, "wp_f", [128, 2, 64], F32)
        wp_bf = sbuf(es, "wp_bf", [128, 2, 64], BF16)
        catT = sbuf(es, "catT", [128, 8, S_OWN], BF16)
        QT = sbuf(es, "QTz", [128, 4, 2, S_OWN], BF16)
        Eq = sbuf(es, "Eqz", [128, S_OWN], BF16)
        w_kv = sbuf(es, "w_kv", [128, 8, 1032], BF16)
        catT_s = sbuf(es, "catT_s", [128, 8, DEC], BF16)
        QT_s = sbuf(es, "QTz_s", [128, 4, 2, DEC], BF16)
        Eq_s = sbuf(es, "Eqz_s", [128, DEC], BF16)
        xnT_s = sbuf(es, "xnT_s", [128, 8, DEC], BF16)
        ss = sbuf(es, "ss", [128, 64], F32)
        rstd = sbuf(es, "rstd", [128, 64], F32)
        junk = sbuf(es, "junk", [128, 1024], BF16)

        B_const = Buf()
        B_catT = [[Buf() for _ in range(NT)] for _ in range(8)]
        B_QT = [[Buf() for _ in range(NT)] for _ in range(4)]
        B_Eq = [Buf() for _ in range(NT)]
        B_wkv = [(Buf(), Buf()) for _ in range(8)]
        B_catTs = [Buf() for _ in range(8)]
        B_QTs = Buf()
        B_Eqs = Buf()
        B_xnTs = Buf()
        B_ss = [Buf() for _ in range(64)]
        B_junk = Buf()

        d_const = kb.dsem()
        B_cdma = Buf()
        nconst = 0
        for dst, src in ((gl, gl_d), (gml, gml_d), (bfb, bf_d), (pq, pq_d), (rc, rc_d), (psc, psc_d), (wp_f, wp_d)):
            ev_c = kb.dma(d_const, lambda q, dst=dst, src=src: q.dma_start(out=dst[:], in_=src))
            nconst += 1
        B_cdma.w = ev_c
        P = "pool"
        kb.op(P, lambda e: e.memset(negh[:], -0.5), writes=[B_const])
        kb.op(P, lambda e: e.memset(ident_f[:], 0.0), writes=[B_const])
        kb.op(P, lambda e: e.affine_select(out=ident_f[:], in_=ident_f[:], pattern=[[-1, 128]],
                                           compare_op=ALU.not_equal, fill=1.0, base=0, channel_multiplier=1),
              writes=[B_const])
        kb.op(P, lambda e: e.tensor_copy(out=ident_bf[:], in_=ident_f[:]), writes=[B_const])
        kb.op(P, lambda e: e.memset(ones_f[:], 1.0), writes=[B_const])
        kb.op(P, lambda e: e.memset(ones_bf[:], 1.0), writes=[B_const])
        kb.op(P, lambda e: e.memset(tri_f[:], 1.0), writes=[B_const])
        kb.op(P, lambda e: e.affine_select(out=tri_f[:], in_=tri_f[:], pattern=[[1, 128]], compare_op=ALU.is_ge,
                                           fill=0.0, base=0, channel_multiplier=-1), writes=[B_const])
        kb.op(P, lambda e: e.memset(sel[:], 0.0), writes=[B_const])
        kb.op(P, lambda e: e.affine_select(out=sel[0:8], in_=sel[0:8], pattern=[[1, 8], [0, 128]],
                                           compare_op=ALU.not_equal, fill=1.0, base=0, channel_multiplier=-1),
              writes=[B_const])
        kb.op(P, lambda e: e.tensor_copy(out=wp_bf[:], in_=wp_f[:]), reads=[B_cdma], writes=[B_const])
        zq = [b_ for row in B_QT for b_ in row]
        kb.op(P, lambda e: e.memset(QT_s[:], 0.0), writes=[B_QTs])
        kb.op(P, lambda e: e.memset(Eq_s[:], 0.0), writes=[B_Eqs])
        kb.op(P, lambda e: e.memset(Eq[:], 0.0), writes=list(B_Eq))
        kb.op(P, lambda e: e.memset(QT[:, :, :, 0:1024], 0.0), writes=zq)
        kb.op(P, lambda e: e.memset(QT[:, :, :, 1024:2048], 0.0), writes=zq)
        stop0 = True

        ss_idx = [0]

        def prep_parts(st, src_ap, m, dstT, dcol, Bdst, xt_ring, xs_ring, tp_ring):
            i = st["i"]
            st["i"] += 1
            xt, Bxt, dxt = xt_ring[i % len(xt_ring)]
            xsb, Bxs = xs_ring[i % len(xs_ring)]
            tp, Btp = tp_ring[i % len(tp_ring)]
            si = ss_idx[0] % 64
            ss_idx[0] += 1

            def fa():
                kb.dma(dxt, lambda q: q.dma_start(out=xt[0:m, :], in_=src_ap), writes=[Bxt])
                kb.op("act", lambda e: e.activation(out=junk[0:m, :], in_=xt[0:m, :], func=AF.Square,
                                                    accum_out=ss[0:m, si:si + 1]),
                      reads=[Bxt], writes=[B_junk, B_ss[si]])
                kb.op("pool", lambda e: e.tensor_scalar(out=rstd[0:m, si:si + 1], in0=ss[0:m, si:si + 1],
                                                        scalar1=1.0 / 1024.0, scalar2=EPS, op0=ALU.mult, op1=ALU.add),
                      reads=[B_ss[si]], writes=[B_ss[si]])
                kb.op("pool", lambda e: e.tensor_tensor(out=rstd[0:m, si:si + 1], in0=rstd[0:m, si:si + 1],
                                                        in1=negh[0:m, 0:1], op=ALU.pow),
                      reads=[B_ss[si]], writes=[B_ss[si]])
                kb.op("dve", lambda e: e.tensor_scalar(out=xsb[0:m, :], in0=xt[0:m, :], scalar1=rstd[0:m, si:si + 1],
                                                       scalar2=None, op0=ALU.mult),
                      reads=[Bxt, B_ss[si]], writes=[Bxs])

            def fb():
                for kc in range(8):
                    kb.op("pe", lambda e, kc=kc: e.transpose(tp[:, kc, 0:m], xsb[0:m, kc * 128:(kc + 1) * 128],
                                                             ident_bf[0:m, 0:m]),
                          reads=[Bxs], writes=[Btp], mark=(kc == 7))
                kb.op("dve", lambda e: e.tensor_copy(out=dstT[:, :, dcol:dcol + m], in_=tp[:, :, 0:m]),
                      reads=[Btp], writes=[Bdst])
            return fa, fb

        def prep_block(st, src_ap, m, dstT, dcol, Bdst, xt_ring, xs_ring, tp_ring):
            fa, fb = prep_parts(st, src_ap, m, dstT, dcol, Bdst, xt_ring, xs_ring, tp_ring)
            fa()
            fb()

        try:
            with contextlib.ExitStack() as sa:
                w_own = sbuf(sa, "w_own", [128, 8, 2048], BF16)
                B_wown = [(Buf(), Buf()) for _ in range(8)]
                mkT = sbuf(sa, "mkT", [128, 2, 256], BF16)
                mvb = sbuf(sa, "mvb", [128, 2, 256], BF16)
                mkT_s = sbuf(sa, "mkT_s", [128, 2, 256], BF16)
                mvb_s = sbuf(sa, "mvb_s", [128, 2, 256], BF16)
                B_mk = Buf()
                B_mks = Buf()
                qmT = sbuf(sa, "qmT", [128, 2, S_OWN], BF16)
                B_qmT = [[Buf() for _ in range(NT)] for _ in range(2)]
                qmT_s = sbuf(sa, "qmT_s", [128, 2, DEC], BF16)
                B_qmTs = Buf()
                uext = sbuf(sa, "uext", [128, 2, NB_OWN, 144], F32)
                B_uext = Buf()
                uext_s = sbuf(sa, "uext_s", [128, 2, 16 + DEC], F32)
                B_uexts = Buf()
                w_mk = sbuf(sa, "w_mk", [128, 8, 512], BF16)
                B_wmk = [(Buf(), Buf()) for _ in range(8)]
                tp_ring = [(psum(sa, "tp%d" % i, [128, 8, 128], BF16), PB()) for i in range(2)]
                mm_ring = [(psum(sa, "mm%d" % i, [128, 512], F32), PB()) for i in range(4)]
                pp_ring = [(psum(sa, "pp%d" % i, [128, 512], F32), PB()) for i in range(2)]
                mmi = [0]

                with contextlib.ExitStack() as sw:
                    wst = [(sbuf(sw, "wst%d" % i, [128, 772], F32), Buf(), kb.dsem()) for i in range(10)]
                    segs = ((0, 512, w_own, 0, B_wown), (512, 1536, w_kv, 0, B_wkv), (1536, 2048, w_own, 512, B_wown),
                            (2048, 2056, w_kv, 1024, B_wkv), (2056, 3080, w_own, 1024, B_wown))
                    parts = ((0, 768), (768, 1536), (1536, 2308), (2308, 3080))
                    li = 0
                    ci = 0
                    for kc in range(8):
                        for (pa, pb) in parts:
                            w, Bw, dw = wst[li % 10]
                            li += 1
                            kb.dma(dw, lambda q, w=w, kc=kc, pa=pa, pb=pb: q.dma_start(
                                out=w[:, 0:pb - pa], in_=w_in[kc * 128:(kc + 1) * 128, pa:pb]), writes=[Bw])
                            for (sa_, sb_, dst, d0, Bd) in segs:
                                lo, hi = max(pa, sa_), min(pb, sb_)
                                if lo >= hi:
                                    continue
                                o_ap = dst[:, kc, d0 + lo - sa_:d0 + hi - sa_]
                                i_ap = w[:, lo - pa:hi - pa]
                                if ci % 2 == 0 and hi - lo > 16:
                                    kb.op("act", lambda e, o_ap=o_ap, i_ap=i_ap, kc=kc: e.activation(
                                        out=o_ap, in_=i_ap, func=AF.Identity, scale=gl[:, kc:kc + 1]),
                                        reads=[Bw, B_cdma], writes=[Bd[kc][0]])
                                else:
                                    kb.op("dve", lambda e, o_ap=o_ap, i_ap=i_ap, kc=kc: e.tensor_scalar(
                                        out=o_ap, in0=i_ap, scalar1=gl[:, kc:kc + 1], scalar2=None, op0=ALU.mult),
                                        reads=[Bw, B_cdma], writes=[Bd[kc][1]])
                                ci += 1
                    for kc in range(8):
                        w, Bw, dw = wst[li % 10]
                        li += 1
                        kb.dma(dw, lambda q, w=w, kc=kc: q.dma_start(out=w[:, 0:512], in_=w_mkv[kc * 128:(kc + 1) * 128, :]),
                               writes=[Bw])
                        if kc % 2 == 0:
                            kb.op("act", lambda e, w=w, kc=kc: e.activation(
                                out=w_mk[:, kc, :], in_=w[:, 0:512], func=AF.Identity, scale=gml[:, kc:kc + 1]),
                                reads=[Bw, B_cdma], writes=[B_wmk[kc][0]])
                        else:
                            kb.op("dve", lambda e, w=w, kc=kc: e.tensor_scalar(
                                out=w_mk[:, kc, :], in0=w[:, 0:512], scalar1=gml[:, kc:kc + 1], scalar2=None, op0=ALU.mult),
                                reads=[Bw, B_cdma], writes=[B_wmk[kc][1]])
                    kb.barrier()
                sa1 = sa.enter_context(contextlib.ExitStack())
                xt_ring = []
                for i in range(2):
                    xt_ring.append((sbuf(sa1, "xt%d" % i, [128, 1024], F32), Buf(), kb.dsem()))
                xs_ring = [(sbuf(sa1, "xsb%d" % i, [128, 1024], BF16), Buf()) for i in range(2)]
                xnT_ring = [(sbuf(sa1, "xnT%d" % i, [128, 8, 512], BF16), Buf()) for i in range(2)]
                stop(0.2)

                pst = {"i": 0}

                def proj_fm(xnT, Bx, n, ccs, evac):
                    for cc in ccs:
                        ps, Bp = mm_ring[mmi[0] % 4]
                        mmi[0] += 1
                        for kc in range(8):
                            kb.op("pe", lambda e, kc=kc, cc=cc, ps=ps: e.matmul(
                                ps[:, 0:n], lhsT=w_own[:, kc, cc * 128:(cc + 1) * 128], rhs=xnT[:, kc, 0:n],
                                start=(kc == 0), stop=(kc == 7)), reads=[Bx, *B_wown[kc]], writes=[Bp], mark=(kc == 7))
                        evac(cc, ps, Bp)

                def evac_own(t):
                    def f(cc, ps, Bp):
                        cols = slice(t * 512, (t + 1) * 512)
                        if cc < 4:
                            kb.op("dve", lambda e: e.tensor_copy(out=QT[0:64, cc, 0, cols], in_=ps[0:64, :]),
                                  reads=[Bp], writes=[B_QT[cc][t]])
                            kb.op("dve", lambda e: e.tensor_copy(out=QT[64:128, cc, 1, cols], in_=ps[64:128, :]),
                                  reads=[Bp], writes=[B_QT[cc][t]])
                        elif cc < 8:
                            kb.op("act", lambda e: e.activation(out=catT[:, cc - 4, cols], in_=ps[:, :], func=AF.Silu),
                                  reads=[Bp], writes=[B_catT[cc - 4][t]])
                        elif cc < 10:
                            kb.op("dve", lambda e: e.tensor_copy(
                                out=uext[:, cc - 8, t * 4:(t + 1) * 4, 16:144],
                                in_=ps[:, :].rearrange("p (b k) -> p b k", k=128)), reads=[Bp], writes=[B_uext])
                        elif cc < 12:
                            kb.op("act", lambda e: e.activation(out=catT[:, cc - 6, cols], in_=ps[:, :], func=AF.Silu),
                                  reads=[Bp], writes=[B_catT[cc - 6][t]])
                        elif cc < 14:
                            kb.op("dve", lambda e: e.tensor_copy(out=qmT[:, cc - 12, cols], in_=ps[:, :]),
                                  reads=[Bp], writes=[B_qmT[cc - 12][t]])
                        else:
                            kb.op("act", lambda e: e.activation(out=catT[:, cc - 8, cols], in_=ps[:, :], func=AF.Silu),
                                  reads=[Bp], writes=[B_catT[cc - 8][t]])
                    return f

                def evac_halo(cc, ps, Bp):
                    kb.op("dve", lambda e: e.tensor_copy(out=uext[:, cc - 8, :, 0:16],
                                                         in_=ps[:, 0:256].rearrange("p (b k) -> p b k", k=16)),
                          reads=[Bp], writes=[B_uext])

                def evac_s(cc, ps, Bp):
                    n = DEC
                    if cc < 4:
                        kb.op("dve", lambda e: e.tensor_copy(out=QT_s[0:64, cc, 0, :], in_=ps[0:64, 0:n]), reads=[Bp], writes=[B_QTs])
                        kb.op("dve", lambda e: e.tensor_copy(out=QT_s[64:128, cc, 1, :], in_=ps[64:128, 0:n]), reads=[Bp], writes=[B_QTs])
                    elif cc < 8:
                        kb.op("act", lambda e: e.activation(out=catT_s[:, cc - 4, :], in_=ps[:, 0:n], func=AF.Silu),
                              reads=[Bp], writes=[B_catTs[cc - 4]])
                    elif cc < 10:
                        kb.op("dve", lambda e: e.tensor_copy(out=uext_s[:, cc - 8, 16:16 + n], in_=ps[:, 0:n]),
                              reads=[Bp], writes=[B_uexts])
                    elif cc < 12:
                        kb.op("act", lambda e: e.activation(out=catT_s[:, cc - 6, :], in_=ps[:, 0:n], func=AF.Silu),
                              reads=[Bp], writes=[B_catTs[cc - 6]])
                    elif cc < 14:
                        kb.op("dve", lambda e: e.tensor_copy(out=qmT_s[:, cc - 12, :], in_=ps[:, 0:n]),
                              reads=[Bp], writes=[B_qmTs])
                    else:
                        kb.op("act", lambda e: e.activation(out=catT_s[:, cc - 8, :], in_=ps[:, 0:n], func=AF.Silu),
                              reads=[Bp], writes=[B_catTs[cc - 8]])

                def prep_tile(src, nblk, blkrows, xnT, Bx):
                    for b in range(nblk):
                        prep_block(pst, src[b * blkrows:(b + 1) * blkrows, :], blkrows, xnT, b * blkrows, Bx,
                                   xt_ring, xs_ring, tp_ring)

                smk = sa1.enter_context(contextlib.ExitStack())
                mst = [(sbuf(smk, "mst%d" % i, [128, 512], F32), Buf(), kb.dsem()) for i in range(2)]
                mkb = [(sbuf(smk, "mkb%d" % i, [128, 256], BF16), Buf()) for i in range(2)]
                xnTm, Bxm = xnT_ring[0]
                prep_tile(xmem, 2, 128, xnTm, Bxm)
                for blk in range(2):
                    ps, Bp = mm_ring[mmi[0] % 4]
                    mmi[0] += 1
                    for kc in range(8):
                        kb.op("pe", lambda e, kc=kc, ps=ps, blk=blk: e.matmul(
                            ps[:, :], lhsT=xnTm[:, kc, blk * 128:(blk + 1) * 128], rhs=w_mk[:, kc, :],
                            start=(kc == 0), stop=(kc == 7)), reads=[Bxm, *B_wmk[kc]], writes=[Bp], mark=(kc == 7))
                    m, Bm, dm = mst[blk]
                    kbf, Bk = mkb[blk]
                    kb.op("act", lambda e, m=m, ps=ps: e.activation(out=m[:], in_=ps[:, :], func=AF.Identity),
                          reads=[Bp], writes=[Bm])
                    kb.op("dve", lambda e, kbf=kbf, ps=ps: e.tensor_copy(out=kbf[:], in_=ps[:, 0:256]),
                          reads=[Bp], writes=[Bk])
                    kb.op("dve", lambda e, ps=ps, blk=blk: e.tensor_copy(out=mvb[:, blk, :], in_=ps[:, 256:512]),
                          reads=[Bp], writes=[B_mk])
                    kb.dma(dm, lambda q, m=m, blk=blk: q.dma_start(out=mk_o[blk * 128:(blk + 1) * 128, :], in_=m[:, 0:256]),
                           reads=[Bm])
                    kb.dma(dm, lambda q, m=m, blk=blk: q.dma_start(out=mv_o[blk * 128:(blk + 1) * 128, :], in_=m[:, 256:512]),
                           reads=[Bm])
                    tp, Btp = tp_ring[blk % 2]
                    for cc in range(2):
                        kb.op("pe", lambda e, cc=cc, tp=tp, kbf=kbf: e.transpose(
                            tp[:, cc, :], kbf[:, cc * 128:(cc + 1) * 128], ident_bf[:]),
                            reads=[Bk, B_const], writes=[Btp], mark=(cc == 1))
                    kb.op("dve", lambda e, tp=tp, blk=blk: e.tensor_copy(out=mkT[:, :, blk * 128:(blk + 1) * 128],
                                                                          in_=tp[:, 0:2, :]),
                          reads=[Btp], writes=[B_mk])
                for blk in range(2):
                    m, Bm, dm = mst[blk]
                    kbf, Bk = mkb[blk]
                    kb.dma(dm, lambda q, m=m, blk=blk: q.dma_start(out=m[:, 0:256], in_=cmk[blk * 128:(blk + 1) * 128, :]),
                           writes=[Bm])
                    kb.dma(dm, lambda q, m=m, blk=blk: q.dma_start(out=m[:, 256:512], in_=cmv[blk * 128:(blk + 1) * 128, :]),
                           writes=[Bm])
                    kb.op("dve", lambda e, kbf=kbf, m=m: e.tensor_copy(out=kbf[:], in_=m[:, 0:256]), reads=[Bm], writes=[Bk])
                    kb.op("dve", lambda e, m=m, blk=blk: e.tensor_copy(out=mvb_s[:, blk, :], in_=m[:, 256:512]),
                          reads=[Bm], writes=[B_mks])
                    tp, Btp = tp_ring[blk % 2]
                    for cc in range(2):
                        kb.op("pe", lambda e, cc=cc, tp=tp, kbf=kbf: e.transpose(
                            tp[:, cc, :], kbf[:, cc * 128:(cc + 1) * 128], ident_bf[:]),
                            reads=[Bk, B_const], writes=[Btp], mark=(cc == 1))
                    kb.op("dve", lambda e, tp=tp, blk=blk: e.tensor_copy(out=mkT_s[:, :, blk * 128:(blk + 1) * 128],
                                                                          in_=tp[:, 0:2, :]),
                          reads=[Btp], writes=[B_mks])


                kb.barrier()
                smk.close()
                stop(0.3)
                d_misc = kb.dsem()
                kb.dma(d_misc, lambda q: q.dma_start(out=uext_s[:, :, 0:16], in_=spT), writes=[B_uexts])
                tiles = [("own", t) for t in range(NT)] + [("halo", 0), ("s", 0)]

                def prep_list(kind, t, slot):
                    xnT, Bx = xnT_ring[slot]
                    if kind == "own":
                        return [prep_parts(pst, xall[t * 512 + b * 128:t * 512 + (b + 1) * 128, :], 128, xnT, b * 128, Bx,
                                           xt_ring, xs_ring, tp_ring) for b in range(4)]
                    elif kind == "halo":
                        return [prep_parts(pst, xhalo[b * 128:(b + 1) * 128, :], 128, xnT, b * 128, Bx,
                                           xt_ring, xs_ring, tp_ring) for b in range(2)]
                    return [prep_parts(pst, xs, DEC, xnT_s, 0, B_xnTs, xt_ring, xs_ring, tp_ring)]

                for fa, fb in prep_list(tiles[0][0], tiles[0][1], 0):
                    fa()
                    fb()
                for i, (kind, t) in enumerate(tiles):
                    nxt = prep_list(tiles[i + 1][0], tiles[i + 1][1], (i + 1) % 2) if i + 1 < len(tiles) else []
                    xnT, Bx = xnT_ring[i % 2]
                    if kind == "own":
                        for g in range(4):
                            if g < len(nxt):
                                nxt[g][0]()
                            proj_fm(xnT, Bx, 512, range(4 * g, 4 * g + 3), evac_own(t))
                            if g < len(nxt):
                                nxt[g][1]()
                            proj_fm(xnT, Bx, 512, range(4 * g + 3, 4 * g + 4), evac_own(t))
                    elif kind == "halo":
                        for fa, fb in nxt:
                            fa()
                        proj_fm(xnT, Bx, 256, (8, 9), evac_halo)
                        for fa, fb in nxt:
                            fb()
                    else:
                        proj_fm(xnT_s, B_xnTs, DEC, range(16), evac_s)

                d_out = kb.dsem()
                kb.dma(d_out, lambda q: q.dma_start(out=pool_o, in_=uext[:, :, NB_OWN - 1, 128:144]), reads=[B_uext])
                kb.dma(d_out, lambda q: q.dma_start(out=pools_o, in_=uext_s[:, :, DEC:DEC + 16]), reads=[B_uexts])

                kb.barrier()
                sa1.close()
                stop(0.4)

                if True:
                    sp_ = sa.enter_context(contextlib.ExitStack())
                    pm_ops = []
                    pooledT = sbuf(sp_, "pooledT", [128, 2, S_OWN], BF16)
                    pooledT_s = sbuf(sp_, "pooledT_s", [128, 2, DEC], BF16)
                    B_pl = Buf()
                    B_pls = Buf()
                    a1 = sbuf(sp_, "a1", [128, NB_OWN, 144], F32)
                    a2 = sbuf(sp_, "a2", [128, NB_OWN, 144], F32)
                    tmpc = sbuf(sp_, "tmpc", [128, 16], F32)
                    B_a = Buf()
                    D = "dve"

                    def tt(out, in0, in1, op, reads, writes, eng=D):
                        pm_ops.append(lambda: kb.op(eng, lambda e: e.tensor_tensor(out=out, in0=in0, in1=in1, op=op),
                                                    reads=reads, writes=writes))

                    def stt(out, in0, sc, in1, reads, writes):
                        pm_ops.append(lambda: kb.op(D, lambda e: e.scalar_tensor_tensor(
                            out=out, in0=in0, scalar=sc, in1=in1, op0=ALU.mult, op1=ALU.subtract), reads=reads, writes=writes))

                    def pool_mix(u, nb, L, pooled, Bu, Bp, corr):
                        for cc in range(2):
                            uu = u[:, cc]
                            A1, A2, A3, A4 = (a[:, 0:nb, 0:L] for a in (a1, a2, a1, a2))
                            tt(A1[:, :, 1:L], uu[:, :, 1:L], uu[:, :, 0:L - 1], ALU.add, [Bu, Bp], [B_a])
                            tt(A2[:, :, 3:L], A1[:, :, 3:L], A1[:, :, 1:L - 2], ALU.add, [B_a], [B_a])
                            if cc == 0:
                                srcs = ((0, 64, A1, 0.5), (64, 128, A2, 0.25))
                            else:
                                tt(A3[:, :, 7:L], A2[:, :, 7:L], A2[:, :, 3:L - 4], ALU.add, [B_a], [B_a])
                                tt(A4[:, :, 15:L], A3[:, :, 15:L], A3[:, :, 7:L - 8], ALU.add, [B_a], [B_a])
                                srcs = ((0, 64, A3, 0.125), (64, 128, A4, 0.0625))
                            for (p0, p1, A, sc) in srcs:
                                stt(pooled[p0:p1, cc], A[p0:p1, :, 16:L], sc, uu[p0:p1, :, 16:L], [B_a, Bu], [Bp])
                                if corr:
                                    tt(tmpc[p0:p1, :], A[p0:p1, 0, 16:32], rc[p0:p1, cc, :], ALU.mult, [B_a, B_const], [B_a])
                                    tt(pooled[p0:p1, cc, 0, 0:16], tmpc[p0:p1, :], uu[p0:p1, 0, 16:32], ALU.subtract,
                                       [B_a, Bu], [Bp])

                    pool_mix(uext, NB_OWN, 144, pooledT[:, :, :].rearrange("p c (b k) -> p c b k", k=128), B_uext, B_pl, True)
                    pool_mix(uext_s[:, :, :].rearrange("p c (b k) -> p c b k", b=1), 1, 16 + DEC,
                             pooledT_s[:, :, :].rearrange("p c (b k) -> p c b k", b=1), B_uexts, B_pls, False)

                    def pool_mm(pl, Bpl, n, cat_ap, Bcat):
                        for cc in range(2):
                            ps, Bp = mm_ring[mmi[0] % 4]
                            mmi[0] += 1
                            for e_ in range(2):
                                r = slice(64 * e_, 64 * e_ + 64)
                                kb.op("pe", lambda e, r=r, cc=cc, ps=ps: e.matmul(
                                    ps[r, 0:n], lhsT=wp_bf[r, cc, :], rhs=pl(cc, r), start=True, stop=True),
                                    reads=[Bpl, B_const], writes=[Bp], mark=(e_ == 1))
                            kb.op("dve", lambda e, cc=cc, ps=ps: e.scalar_tensor_tensor(
                                out=cat_ap(cc), in0=ps[:, 0:n], scalar=psc[:, cc:cc + 1], in1=cat_ap(cc), op0=ALU.mult,
                                op1=ALU.mult), reads=[Bp, B_const], writes=[Bcat(cc)])

                    def all_pool_mm():
                        for t in range(NT):
                            cols = slice(t * 512, (t + 1) * 512)
                            pool_mm(lambda cc, r, cols=cols: pooledT[r, cc, cols], B_pl, 512,
                                    lambda cc, cols=cols: catT[:, 4 + cc, cols], lambda cc, t=t: B_catT[4 + cc][t])
                        pool_mm(lambda cc, r: pooledT_s[r, cc, :], B_pls, DEC, lambda cc: catT_s[:, 4 + cc, :],
                                lambda cc: B_catTs[4 + cc])

                stop(0.5)
                with contextlib.ExitStack() as sm:
                    PTm = [(sbuf(sm, "PTm%d" % i, [128, 512], BF16), Buf()) for i in range(3)]
                    recs = [(sbuf(sm, "recm%d" % i, [128, 512], F32), sbuf(sm, "tmpm%d" % i, [128, 512], F32), Buf()) for i in range(2)]
                    pti = [0]

                    cci = [0]

                    def mem_attn(n, mkT_, mvb_, Bmk_, q_ap, Bq, cat_ap, Bcat):
                        for cc in range(2):
                            if cci[0] % 2 == 0:
                                (O, BO), (SU, BS) = pp_ring[0], pp_ring[1]
                            else:
                                (O, BO), (SU, BS) = mm_ring[2], mm_ring[3]
                            cci[0] += 1
                            units = [(e_, jb) for e_ in range(2) for jb in range(2)]

                            def qk(k):
                                e_, jb = units[k]
                                r = slice(64 * e_, 64 * e_ + 64)
                                S, BSc = mm_ring[k % 2]
                                kb.op("pe", lambda e: e.matmul(
                                    S[:, 0:n], lhsT=mkT_[r, cc, jb * 128:(jb + 1) * 128], rhs=q_ap(cc, r),
                                    start=True, stop=True), reads=[Bmk_, Bq(cc)], writes=[BSc])

                            qk(0)
                            for k, (e_, jb) in enumerate(units):
                                if k + 1 < len(units):
                                    qk(k + 1)
                                S, BSc = mm_ring[k % 2]
                                r = slice(64 * e_, 64 * e_ + 64)
                                hm = 2 * cc + e_
                                PT, BPT = PTm[pti[0] % 3]
                                pti[0] += 1
                                kb.op("act", lambda e, S=S, PT=PT: e.activation(out=PT[:, 0:n], in_=S[:, 0:n], func=AF.Exp,
                                                                                scale=0.125), reads=[BSc], writes=[BPT])
                                kb.op("pe", lambda e, O=O, r=r, jb=jb, hm=hm, PT=PT: e.matmul(
                                    O[r, 0:n], lhsT=mvb_[:, jb, hm * 64:(hm + 1) * 64], rhs=PT[:, 0:n],
                                    start=(jb == 0), stop=(jb == 1)), reads=[Bmk_, BPT], writes=[BO], mark=False)
                                kb.op("pe", lambda e, SU=SU, r=r, jb=jb, PT=PT: e.matmul(
                                    SU[r, 0:n], lhsT=ones_bf[:, 0:64], rhs=PT[:, 0:n],
                                    start=(jb == 0), stop=(jb == 1)), reads=[BPT, B_const], writes=[BS])
                            rm, tm, Brm = recs[cci[0] % 2]
                            kb.op("act", lambda e, SU=SU: e.activation(out=rm[:, 0:n], in_=SU[:, 0:n], func=AF.Ln),
                                  reads=[BS], writes=[Brm])
                            kb.op("act", lambda e: e.activation(out=rm[:, 0:n], in_=rm[:, 0:n], func=AF.Exp, scale=-1.0),
                                  reads=[Brm], writes=[Brm])
                            kb.op("dve", lambda e, O=O: e.tensor_tensor(out=tm[:, 0:n], in0=O[:, 0:n], in1=rm[:, 0:n],
                                                                         op=ALU.mult), reads=[BO, Brm], writes=[Brm])
                            kb.op("dve", lambda e, cc=cc: e.tensor_tensor(out=cat_ap(cc), in0=tm[:, 0:n], in1=cat_ap(cc),
                                                                          op=ALU.mult), reads=[Brm], writes=[Bcat(cc)])

                    per = (len(pm_ops) + 4) // 5

                    def drain(k):
                        for _ in range(min(k, len(pm_ops))):
                            pm_ops.pop(0)()

                    drain(per)
                    for t in range(NT):
                        cols = slice(t * 512, (t + 1) * 512)
                        mem_attn(512, mkT, mvb, B_mk, lambda cc, r, cols=cols: qmT[r, cc, cols], lambda cc, t=t: B_qmT[cc][t],
                                 lambda cc, cols=cols: catT[:, 6 + cc, cols], lambda cc, t=t: B_catT[6 + cc][t])
                        drain(per)
                    mem_attn(DEC, mkT_s, mvb_s, B_mks, lambda cc, r: qmT_s[r, cc, :], lambda cc: B_qmTs,
                             lambda cc: catT_s[:, 6 + cc, :], lambda cc: B_catTs[6 + cc])
                    drain(len(pm_ops))
                    all_pool_mm()
                    kb.barrier()
                sp_.close()
                kb.barrier()

            bi = [0]

            def ingest(R, m, ksrc, Bks, vsrc, Bvs, KT_ap, BK, V_ap, BV):
                i = bi[0]
                kbf, Bkb = R["kbf"][i % 2]
                kb.op("dve", lambda e: e.tensor_copy(out=kbf[0:m, :], in_=ksrc), reads=[Bks], writes=[Bkb])
                kb.op("dve", lambda e: e.tensor_copy(out=V_ap, in_=vsrc), reads=[Bvs], writes=[BV])
                ktp, Bkt = R["ktp"][0]

                def tail():
                    for c in range(4):
                        kb.op("pe", lambda e, c=c: e.transpose(ktp[:, c, 0:m], kbf[0:m, c * 128:(c + 1) * 128],
                                                               ident_bf[0:m, 0:m]),
                              reads=[Bkb, B_const], writes=[Bkt], mark=(c == 3))
                    kb.op("act", lambda e: e.activation(out=KT_ap, in_=ktp[:, :, 0:m], func=AF.Identity),
                          reads=[Bkt], writes=[BK])
                return tail

            def kv_block(R, xnT, Bx, col, m, zf_ap, B_zf, kout, vout, KT_ap, BK, V_ap, BV, mid=None):
                i = bi[0]
                kps, Bkp = R["kps"][i % 2]
                vps, Bvp = R["vps"][i % 2]
                zps, Bzp = R["zps"][0]
                for (ps, Bp, c0, n) in ((kps, Bkp, 0, 512), (vps, Bvp, 512, 512), (zps, Bzp, 1024, 8)):
                    for kc in range(8):
                        kb.op("pe", lambda e, kc=kc, ps=ps, c0=c0, n=n: e.matmul(
                            ps[0:m, 0:n], lhsT=xnT[:, kc, col:col + m], rhs=w_kv[:, kc, c0:c0 + n],
                            start=(kc == 0), stop=(kc == 7)), reads=[Bx, *B_wkv[kc]], writes=[Bp], mark=(kc == 7))
                    if mid is not None and c0 == 0:
                        mid()
                kb.op("dve", lambda e: e.tensor_tensor(out=zf_ap, in0=zps[0:m, :], in1=bfb[0:m, :], op=ALU.add),
                      reads=[Bzp, B_const], writes=[B_zf])
                tail = ingest(R, m, kps[0:m, :], Bkp, vps[0:m, :], Bvp, KT_ap, BK, V_ap, BV)
                if kout is not None:
                    ks_, Bk_, dk_ = R["kst"][i % len(R["kst"])]
                    vs_, Bv_, dv_ = R["vst"][i % len(R["vst"])]
                    kb.op("act", lambda e: e.activation(out=ks_[0:m, :], in_=kps[0:m, :], func=AF.Identity),
                          reads=[Bkp], writes=[Bk_])
                    kb.op("act", lambda e: e.activation(out=vs_[0:m, :], in_=vps[0:m, :], func=AF.Identity),
                          reads=[Bvp], writes=[Bv_])
                    kb.dma(dk_, lambda q: q.dma_start(out=kout, in_=ks_[0:m, :]), reads=[Bk_], q="pool")
                    kb.dma(dv_, lambda q: q.dma_start(out=vout, in_=vs_[0:m, :]), reads=[Bv_], q="pool")
                bi[0] += 1
                return tail

            def kv_res(st, depth=2):
                return dict(
                    kst=[(sbuf(st, "kst%d" % i, [128, 512], F32), Buf(), kb.dsem()) for i in range(depth)],
                    vst=[(sbuf(st, "vst%d" % i, [128, 512], F32), Buf(), kb.dsem()) for i in range(depth)],
                    kbf=[(sbuf(st, "kbf%d" % i, [128, 512], BF16), Buf()) for i in range(2)],
                    kps=[(psum(st, "kps%d" % i, [128, 512], F32), PB()) for i in range(2)],
                    vps=[(psum(st, "vps%d" % i, [128, 512], F32), PB()) for i in range(2)],
                    zps=[(psum(st, "zps%d" % i, [128, 8], F32), PB()) for i in range(1)],
                    ktp=[(psum(st, "ktp%d" % i, [128, 4, 128], BF16), PB()) for i in range(1)])

            def logf_chain(st, zf, B_zf, m, nb):
                ez = sbuf(st, "ez", [128, nb, 8], F32)
                B_ez = Buf()
                kb.op("act", lambda e: e.activation(out=ez[0:m], in_=zf[0:m], func=AF.Exp, scale=-1.0),
                      reads=[B_zf], writes=[B_ez])
                kb.op("act", lambda e: e.activation(out=ez[0:m], in_=ez[0:m], func=AF.Ln, bias=1.0, scale=1.0),
                      reads=[B_ez], writes=[B_ez])
                return ez, B_ez

            def scan_incl(bufs, B_f, src, n, dst):
                cur = src
                d = 1
                k = 0
                while d < n:
                    nxt = bufs[k % 2][:, 0:n, :]
                    kb.op("dve", lambda e, nxt=nxt, cur=cur, d=d: e.tensor_copy(out=nxt[:, 0:d, :], in_=cur[:, 0:d, :]),
                          reads=[B_f], writes=[B_f])
                    kb.op("dve", lambda e, nxt=nxt, cur=cur, d=d: e.tensor_tensor(
                        out=nxt[:, d:n, :], in0=cur[:, d:n, :], in1=cur[:, 0:n - d, :], op=ALU.add),
                        reads=[B_f], writes=[B_f])
                    cur = nxt
                    d *= 2
                    k += 1
                kb.op("dve", lambda e: e.tensor_copy(out=dst, in_=cur), reads=[B_f], writes=[B_f])

            pti = [0]
            pairi = [0]

            def attend(R, N, qcol0, QT_, BQ, Eq_, BEq, blocks, cat_ap, Bcat, B_bias):
                Sb, Ob, Ub, PTr, comb, rec, tmpo, B_rec = R
                for c in range(4):
                    nb = len(blocks)

                    def qk(j):
                        bk = blocks[j]
                        nk, c0 = bk["nk"], bk["c0"]
                        for e_ in range(2):
                            h = 2 * c + e_
                            S, BS = Sb[2 * (j % 2) + e_]
                            kb.op("pe", lambda e, S=S, e_=e_: e.matmul(
                                S[0:nk, c0:N], lhsT=bk["KT"](c), rhs=QT_[:, c, e_, qcol0 + c0:qcol0 + N],
                                start=True, stop=False), reads=[bk["BK"], BQ(c)], writes=[BS], mark=False)
                            kb.op("pe", lambda e, S=S, h=h: e.matmul(
                                S[0:nk, c0:N], lhsT=sel[:, h, 0:nk], rhs=Eq_[:, qcol0 + c0:qcol0 + N],
                                start=False, stop=True), reads=[BEq, B_const], writes=[BS])

                    qk(0)
                    for j in range(nb):
                        if j + 1 < nb:
                            qk(j + 1)
                        bk = blocks[j]
                        nk, c0 = bk["nk"], bk["c0"]
                        for e_ in range(2):
                            h = 2 * c + e_
                            S, BS = Sb[2 * (j % 2) + e_]
                            O, BO = Ob[e_]
                            U, BU = Ub[e_]
                            PT, BPT = PTr[pti[0] % len(PTr)]
                            pti[0] += 1
                            w = min(128, N - c0)
                            if bk["fix"] == "p":
                                kb.op("act", lambda e, S=S, PT=PT, h=h: e.activation(
                                    out=PT[0:nk, c0:c0 + w], in_=S[0:nk, c0:c0 + w], func=AF.Exp, bias=bk["biasp"](h), scale=0.125),
                                    reads=[BS, B_bias], writes=[BPT])
                                if c0 + w < N:
                                    kb.op("act", lambda e, S=S, PT=PT, h=h: e.activation(
                                        out=PT[0:nk, c0 + w:N], in_=S[0:nk, c0 + w:N], func=AF.Exp, bias=bk["bias"](h), scale=0.125),
                                        reads=[BS, B_bias], writes=[BPT])
                            else:
                                kb.op("act", lambda e, S=S, PT=PT, h=h: e.activation(
                                    out=PT[0:nk, c0:N], in_=S[0:nk, c0:N], func=AF.Exp, bias=bk["bias"](h), scale=0.125),
                                    reads=[BS, B_bias], writes=[BPT])
                            if bk["fix"] == "tri":
                                kb.op("pool", lambda e, PT=PT, w=w: e.affine_select(
                                    out=PT[0:nk, c0:c0 + w], in_=PT[0:nk, c0:c0 + w], pattern=[[1, w]],
                                    compare_op=ALU.is_ge, fill=0.0, base=0, channel_multiplier=-1),
                                    reads=[BPT], writes=[BPT])
                            kb.op("pe", lambda e, O=O, PT=PT, j=j: e.matmul(
                                O[:, c0:N], lhsT=bk["V"](c), rhs=PT[0:nk, c0:N], start=(j == 0), stop=(j == nb - 1)),
                                reads=[bk["BV"], BPT], writes=[BO], mark=False)
                            kb.op("pe", lambda e, U=U, PT=PT, j=j: e.matmul(
                                U[:, c0:N], lhsT=ones_bf[0:nk, :], rhs=PT[0:nk, c0:N], start=(j == 0),
                                stop=(j == nb - 1)), reads=[BPT, B_const], writes=[BU])
                    oc, uc, Bc4 = comb[pairi[0] % 2]
                    pairi[0] += 1
                    kb.op("dve", lambda e: e.tensor_copy(out=uc[0:64, 0:N], in_=Ub[0][0][0:64, 0:N]),
                          reads=[Ub[0][1]], writes=[Bc4[0]])
                    kb.op("act", lambda e: e.activation(out=uc[64:128, 0:N], in_=Ub[1][0][64:128, 0:N], func=AF.Identity),
                          reads=[Ub[1][1]], writes=[Bc4[1]])
                    kb.op("dve", lambda e: e.tensor_copy(out=oc[0:64, 0:N], in_=Ob[0][0][0:64, 0:N]),
                          reads=[Ob[0][1]], writes=[Bc4[2]])
                    kb.op("act", lambda e: e.activation(out=oc[64:128, 0:N], in_=Ob[1][0][64:128, 0:N], func=AF.Identity),
                          reads=[Ob[1][1]], writes=[Bc4[3]])
                    kb.op("dve", lambda e: e.reciprocal(out=rec[:, 0:N], in_=uc[:, 0:N]), reads=[Bc4[0], Bc4[1]], writes=[B_rec])
                    kb.op("dve", lambda e: e.tensor_tensor(out=tmpo[:, 0:N], in0=oc[:, 0:N], in1=rec[:, 0:N],
                                                           op=ALU.mult), reads=[Bc4[2], Bc4[3], B_rec], writes=[B_rec])
                    kb.op("dve", lambda e, c=c: e.tensor_tensor(out=cat_ap(c), in0=tmpo[:, 0:N], in1=cat_ap(c),
                                                                op=ALU.mult), reads=[B_rec], writes=[Bcat(c)])

            def attn_res(st):
                Sb = [(psum(st, "Sb%d" % i, [128, 512], F32), PB()) for i in range(4)]
                Ob = [(psum(st, "Ob%d" % i, [128, 512], F32), PB()) for i in range(2)]
                Ub = [(psum(st, "Ub%d" % i, [128, 512], F32), PB()) for i in range(2)]
                PTr = [(sbuf(st, "PT%d" % i, [128, 512], BF16), Buf()) for i in range(6)]
                comb = [(sbuf(st, "oc%d" % i, [128, 512], F32), sbuf(st, "uc%d" % i, [128, 512], F32), [Buf() for _ in range(4)]) for i in range(2)]
                rec = sbuf(st, "rec", [128, 512], F32)
                tmpo = sbuf(st, "tmpo", [128, 512], F32)
                return (Sb, Ob, Ub, PTr, comb, rec, tmpo, Buf())

            stop(0.6)
            with contextlib.ExitStack() as sb0:
                KTs = sbuf(sb0, "KTs", [128, 4, DEC], BF16)
                Vs = sbuf(sb0, "Vs", [128, 512], BF16)
                B_KTs = Buf()
                B_Vs = Buf()
                KTc = sbuf(sb0, "KTc", [128, 4, 2048], BF16)
                Vc = sbuf(sb0, "Vc", [128, NCB, 512], BF16)
                B_KTc = [Buf() for _ in range(NCB)]
                B_Vc = [Buf() for _ in range(NCB)]
                zfs = sbuf(sb0, "zfs", [128, 1, 8], F32)
                B_zfs = Buf()
                lfs = sbuf(sb0, "lfs", [128, NCB + 1, 8], F32)
                B_lfs = Buf()
                bias_s = sbuf(sb0, "bias_s", [128, NCB + 1, 8], F32)
                B_bias_s = Buf()
                d_lf = kb.dsem()
                kb.op("pool", lambda e: e.memset(lfs[:, NCB, :], 0.0), writes=[B_lfs])
                kb.dma(d_lf, lambda q: q.dma_start(out=lfs[:, 0:NCB, :], in_=clf.rearrange("(b k) h -> k b h", k=128)),
                       writes=[B_lfs])
                with contextlib.ExitStack() as s1:
                    R = kv_res(s1, depth=6)
                    kv_block(R, xnT_s, B_xnTs, 0, DEC, zfs[0:DEC, 0, :], B_zfs, ks_o, vs_o, KTs[:, :, :], B_KTs, Vs[0:DEC, :], B_Vs)()

                    ld_sems = [(kb.dsem(), kb.dsem()) for _ in range(6)]

                    def cload(cb):
                        ks_, Bk_, _ = R["kst"][(cb + 1) % 6]
                        vs_, Bv_, _ = R["vst"][(cb + 1) % 6]
                        dk_, dv_ = ld_sems[(cb + 1) % 6]
                        rows = slice(cb * 128, (cb + 1) * 128)
                        kb.dma(dk_, lambda q: q.dma_start(out=ks_[:], in_=ck[rows, :]), writes=[Bk_])
                        kb.dma(dv_, lambda q: q.dma_start(out=vs_[:], in_=cv[rows, :]), writes=[Bv_])

                    for cb in range(4):
                        cload(cb)
                    ptail = None
                    for cb in range(NCB):
                        if cb + 4 < NCB:
                            cload(cb + 4)
                        ks_, Bk_, dk_ = R["kst"][(cb + 1) % 6]
                        vs_, Bv_, dv_ = R["vst"][(cb + 1) % 6]
                        rows = slice(cb * 128, (cb + 1) * 128)
                        t_ = ingest(R, 128, ks_[:, :], Bk_, vs_[:, :], Bv_, KTc[:, :, rows], B_KTc[cb], Vc[:, cb, :], B_Vc[cb])
                        if ptail is not None:
                            ptail()
                        ptail = t_
                        bi[0] += 1
                    ptail()
                    kb.barrier()
                stop(1.1)
                with contextlib.ExitStack() as sf:
                    css = sbuf(sf, "css", [128, NCB + 1, 8], F32)
                    tots = sbuf(sf, "tots", [128, NCB + 1, 8], F32)
                    pres = sbuf(sf, "pres", [128, NCB + 1, 8], F32)
                    Fs = sbuf(sf, "Fs", [128, NCB + 1, 8], F32)
                    sa_ = sbuf(sf, "sa_", [128, 32, 8], F32)
                    sb2 = sbuf(sf, "sb2", [128, 32, 8], F32)
                    B_f = Buf()
                    ps1 = psum(sf, "ps1", [128, 512], F32)
                    ps2 = psum(sf, "ps2", [128, 512], F32)
                    ps3 = psum(sf, "ps3", [8, 512], F32)
                    B_p1, B_p2, B_p3 = PB(), PB(), PB()
                    D = "dve"
                    ez, B_ez = logf_chain(sf, zfs, B_zfs, DEC, 1)
                    kb.op(D, lambda e: e.tensor_scalar(out=lfs[0:DEC, NCB, :], in0=ez[0:DEC, 0, :], scalar1=-1.0,
                                                       scalar2=None, op0=ALU.mult), reads=[B_ez], writes=[B_lfs])
                    kb.dma(d_out, lambda q: q.dma_start(out=lfs_o, in_=lfs[0:DEC, NCB, :]), reads=[B_lfs])
                    n_s = (NCB + 1) * 8
                    lfs2 = lfs[:, :, :].rearrange("p b h -> p (b h)")
                    kb.op("pe", lambda e: e.matmul(ps1[:, 0:n_s], lhsT=tri_f[:], rhs=lfs2, start=True, stop=True),
                          reads=[B_lfs, B_const], writes=[B_p1])
                    kb.op("pe", lambda e: e.matmul(ps2[:, 0:n_s], lhsT=ones_f[:], rhs=lfs2, start=True, stop=True),
                          reads=[B_lfs, B_const], writes=[B_p2])
                    kb.op(D, lambda e: e.tensor_copy(out=css[:, :, :].rearrange("p b h -> p (b h)"), in_=ps1[:, 0:n_s]),
                          reads=[B_p1], writes=[B_f])
                    kb.op(D, lambda e: e.tensor_copy(out=tots[:, :, :].rearrange("p b h -> p (b h)"), in_=ps2[:, 0:n_s]),
                          reads=[B_p2], writes=[B_f])
                    scan_incl([sa_, sb2], B_f, tots[:, :, :], NCB + 1, pres[:, :, :])
                    kb.op(D, lambda e: e.tensor_tensor(out=Fs[:], in0=pres[:], in1=css[:], op=ALU.add), reads=[B_f], writes=[B_f])
                    kb.op(D, lambda e: e.tensor_tensor(out=Fs[:], in0=Fs[:], in1=tots[:], op=ALU.subtract),
                          reads=[B_f], writes=[B_f])
                    kb.op(D, lambda e: e.tensor_tensor(out=bias_s[:], in0=pres[:, NCB, :].unsqueeze(1).to_broadcast(
                        [128, NCB + 1, 8]), in1=Fs[:], op=ALU.subtract), reads=[B_f], writes=[B_bias_s])
                    kb.op("pe", lambda e: e.transpose(ps3[:, 0:128], Fs[:, NCB, :], ident_f[:]),
                          reads=[B_f, B_const], writes=[B_p3])
                    ft = sbuf(sf, "ft", [8, 512], F32)
                    B_ft = Buf()
                    kb.op(D, lambda e: e.tensor_copy(out=ft[:, 0:128], in_=ps3[:, 0:128]), reads=[B_p3], writes=[B_ft])
                    kb.op(D, lambda e: e.tensor_scalar(out=Eq_s[0:8, :], in0=ft[:, 0:DEC], scalar1=ft[:, DEC - 1:DEC],
                                                       scalar2=8.0, op0=ALU.subtract, op1=ALU.mult),
                          reads=[B_ft], writes=[B_Eqs])
                    kb.barrier()
                stop(1.2)
                with contextlib.ExitStack() as s3:
                    RA = attn_res(s3)
                    blocks = []
                    for cb in range(NCB):
                        rows = slice(cb * 128, (cb + 1) * 128)
                        blocks.append(dict(nk=128, c0=0, KT=lambda c, rows=rows: KTc[:, c, rows], BK=B_KTc[cb],
                                           V=lambda c, cb=cb: Vc[:, cb, c * 128:(c + 1) * 128], BV=B_Vc[cb],
                                           bias=lambda h, cb=cb: bias_s[:, cb, h:h + 1], fix=None))
                    blocks.append(dict(nk=DEC, c0=0, KT=lambda c: KTs[:, c, :], BK=B_KTs,
                                       V=lambda c: Vs[0:DEC, c * 128:(c + 1) * 128], BV=B_Vs,
                                       bias=lambda h: bias_s[0:DEC, NCB, h:h + 1], fix="tri"))
                    attend(RA, DEC, 0, QT_s, lambda c: B_QTs, Eq_s, B_Eqs, blocks, lambda c: catT_s[:, c, :],
                           lambda c: B_catTs[c], B_bias_s)
                    kb.barrier()
                kb.barrier()

            stop(1.5)
            with contextlib.ExitStack() as sb_:
                KT = sbuf(sb_, "KT", [128, 4, 4096], BF16)
                V = sbuf(sb_, "V", [128, NB_ALL, 512], BF16)
                B_KT = [Buf() for _ in range(NB_ALL)]
                B_V = [Buf() for _ in range(NB_ALL)]
                zfb = sbuf(sb_, "zfb", [128, NB_ALL, 8], F32)
                B_zf = Buf()
                B_lf = Buf()
                B_bias = Buf()
                with contextlib.ExitStack() as s1:
                    xt_ring = [(sbuf(s1, "xt%d" % i, [128, 1024], F32), Buf(), kb.dsem()) for i in range(3)]
                    xs_ring = [(sbuf(s1, "xsb%d" % i, [128, 1024], BF16), Buf()) for i in range(2)]
                    xnT_ring = [(sbuf(s1, "xnT%d" % i, [128, 8, 128], BF16), Buf()) for i in range(4)]
                    tp_ring = [(psum(s1, "tp%d" % i, [128, 8, 128], BF16), PB()) for i in range(2)]
                    R = kv_res(s1)
                    pst = {"i": 0}

                    def prep_tile1(src, nblk, xnT, Bx):
                        for b in range(nblk):
                            prep_block(pst, src[b * 128:(b + 1) * 128, :], 128, xnT, b * 128, Bx, xt_ring, xs_ring, tp_ring)

                    def pp(i):
                        xnT, Bx = xnT_ring[i % 4]
                        return prep_parts(pst, xall[i * 128:(i + 1) * 128, :], 128, xnT, 0, Bx, xt_ring, xs_ring, tp_ring)

                    parts = [pp(i) for i in range(NB_ALL)]
                    parts[0][0]()
                    parts[1][0]()
                    parts[0][1]()
                    prev_tail = [None]
                    for blk in range(NB_ALL):
                        xnT, Bx = xnT_ring[blk % 4]
                        own = blk < NB_OWN
                        rows = slice(blk * 128, (blk + 1) * 128)
                        if blk + 2 < NB_ALL:
                            parts[blk + 2][0]()
                        def mid(blk=blk, prev=prev_tail):
                            if prev[0] is not None:
                                prev[0]()
                            if blk + 1 < NB_ALL:
                                parts[blk + 1][1]()
                        prev_tail[0] = None if False else prev_tail[0]
                        t_ = kv_block(R, xnT, Bx, 0, 128, zfb[:, blk, :], B_zf, k_o[rows, :] if own else None,
                                      v_o[rows, :] if own else None, KT[:, :, rows], B_KT[blk], V[:, blk, :], B_V[blk],
                                      mid=mid)
                        prev_tail = [t_]
                    prev_tail[0]()
                    kb.barrier()

                lf = sbuf(sb_, "lf", [128, NB_ALL, 8], F32)
                bias = sbuf(sb_, "bias", [128, NT, NB_ALL, 8], F32)
                biasp = sbuf(sb_, "biasp", [128, NT, 4, 8], F32)
                if STOP_AFTER >= 2:
                    with contextlib.ExitStack() as sf:
                        cs = sbuf(sf, "cs", [128, NB_ALL, 8], F32)
                        tot = sbuf(sf, "tot", [128, NB_ALL, 8], F32)
                        pre = sbuf(sf, "pre", [128, NB_ALL, 8], F32)
                        Ff = sbuf(sf, "Ff", [128, NB_ALL, 8], F32)
                        sa_ = sbuf(sf, "sa_", [128, 32, 8], F32)
                        sb2 = sbuf(sf, "sb2", [128, 32, 8], F32)
                        Rr = sbuf(sf, "Rr", [128, 8], F32)
                        ft = sbuf(sf, "ft", [8, 512], F32)
                        B_ft = Buf()
                        B_f = Buf()
                        ps1 = psum(sf, "ps1", [128, 512], F32)
                        ps2 = psum(sf, "ps2", [128, 512], F32)
                        ps3 = psum(sf, "ps3", [8, 512], F32)
                        B_p1, B_p2, B_p3 = PB(), PB(), PB()
                        D = "dve"
                        ez, B_ez = logf_chain(sf, zfb, B_zf, 128, NB_ALL)
                        kb.op(D, lambda e: e.tensor_scalar(out=lf[:], in0=ez[:], scalar1=-1.0, scalar2=None,
                                                           op0=ALU.mult), reads=[B_ez], writes=[B_lf])
                        kb.dma(d_out, lambda q: q.dma_start(out=lf_o.rearrange("(b k) h -> k b h", k=128), in_=lf[:, 0:NB_OWN, :]),
                               reads=[B_lf])
                        lf2 = lf[:, :, :].rearrange("p b h -> p (b h)")
                        kb.op("pe", lambda e: e.matmul(ps1[:, 0:256], lhsT=tri_f[:], rhs=lf2, start=True, stop=True),
                              reads=[B_lf, B_const], writes=[B_p1])
                        kb.op("pe", lambda e: e.matmul(ps2[:, 0:256], lhsT=ones_f[:], rhs=lf2, start=True, stop=True),
                              reads=[B_lf, B_const], writes=[B_p2])
                        kb.op(D, lambda e: e.tensor_copy(out=cs[:, :, :].rearrange("p b h -> p (b h)"), in_=ps1[:, 0:256]),
                              reads=[B_p1], writes=[B_f])
                        kb.op(D, lambda e: e.tensor_copy(out=tot[:, :, :].rearrange("p b h -> p (b h)"), in_=ps2[:, 0:256]),
                              reads=[B_p2], writes=[B_f])
                        kb.op(D, lambda e: e.tensor_tensor(out=Ff[:, 0:16, :], in0=tot[:, 0:16, :], in1=tot[:, 16:32, :],
                                                           op=ALU.add), reads=[B_f], writes=[B_f])
                        scan_incl([sa_, sb2], B_f, Ff[:, 0:16, :], 16, pre[:, 16:32, :])
                        kb.op(D, lambda e: e.tensor_tensor(out=pre[:, 16:32, :], in0=pre[:, 16:32, :], in1=Ff[:, 0:16, :],
                                                           op=ALU.subtract), reads=[B_f], writes=[B_f])
                        kb.op(D, lambda e: e.scalar_tensor_tensor(out=pre[:, 0:16, :], in0=tot[:, 16:32, :], scalar=pq[:, 0:1],
                                                                  in1=pre[:, 16:32, :], op0=ALU.mult, op1=ALU.add),
                              reads=[B_f, B_const], writes=[B_f])
                        kb.op(D, lambda e: e.scalar_tensor_tensor(out=pre[:, 16:32, :], in0=tot[:, 0:16, :], scalar=pq[:, 1:2],
                                                                  in1=pre[:, 16:32, :], op0=ALU.mult, op1=ALU.add),
                              reads=[B_f, B_const], writes=[B_f])
                        kb.op(D, lambda e: e.tensor_tensor(out=Ff[:], in0=pre[:], in1=cs[:], op=ALU.add), reads=[B_f], writes=[B_f])
                        for s in range(NT):
                            lb = 4 * s + 3
                            kb.op(D, lambda e, lb=lb: e.tensor_tensor(out=Rr[:], in0=pre[:, lb, :], in1=tot[:, lb, :], op=ALU.add),
                                  reads=[B_f], writes=[B_f])
                            kb.op(D, lambda e, s=s: e.tensor_tensor(out=bias[:, s, :, :],
                                                                    in0=Rr[:, :].unsqueeze(1).to_broadcast([128, NB_ALL, 8]),
                                                                    in1=Ff[:], op=ALU.subtract), reads=[B_f], writes=[B_bias])
                            kb.op(D, lambda e, s=s: e.tensor_scalar(out=biasp[:, s, :, :], in0=bias[:, s, NB_OWN + 4 * s:NB_OWN + 4 * s + 4, :],
                                                                    scalar1=pq[:, 2:3], scalar2=None, op0=ALU.add),
                                  reads=[B_bias, B_const], writes=[B_bias])
                            for b in range(4):
                                kb.op("pe", lambda e, s=s, b=b: e.transpose(ps3[:, b * 128:(b + 1) * 128], Ff[:, 4 * s + b, :],
                                                                            ident_f[:]),
                                      reads=[B_f, B_const], writes=[B_p3], mark=(b == 3))
                            kb.op(D, lambda e: e.tensor_copy(out=ft[:, :], in_=ps3[:, :]), reads=[B_p3], writes=[B_ft])
                            kb.op(D, lambda e, s=s: e.tensor_scalar(out=Eq[0:8, s * 512:(s + 1) * 512], in0=ft[:, :],
                                                                    scalar1=ft[:, 511:512], scalar2=8.0, op0=ALU.subtract,
                                                                    op1=ALU.mult), reads=[B_ft], writes=[B_Eq[s]])
                        kb.barrier()

                if STOP_AFTER >= 3:
                    with contextlib.ExitStack() as s3:
                        RA = attn_res(s3)

                        def mkblk(s, blk, c0, fix):
                            rows = slice(blk * 128, (blk + 1) * 128)
                            return dict(nk=128, c0=c0, KT=lambda c: KT[:, c, rows], BK=B_KT[blk],
                                        V=lambda c: V[:, blk, c * 128:(c + 1) * 128], BV=B_V[blk],
                                        bias=lambda h: bias[:, s, blk, h:h + 1],
                                        biasp=lambda h: biasp[:, s, blk - NB_OWN - 4 * s, h:h + 1], fix=fix)

                        for s in range(NT):
                            blocks = []
                            for i in range(4 * s):
                                blocks.append(mkblk(s, i, 0, None))
                            for jj in range(4):
                                blocks.append(mkblk(s, 4 * s + jj, 128 * jj, "tri"))
                            for i in range(4 * s):
                                blocks.append(mkblk(s, NB_OWN + i, 0, None))
                            for jj in range(4):
                                blocks.append(mkblk(s, NB_OWN + 4 * s + jj, 128 * jj, "p"))
                            attend(RA, 512, s * 512, QT, lambda c, s=s: B_QT[c][s], Eq, B_Eq[s], blocks,
                                   lambda c, s=s: catT[:, c, s * 512:(s + 1) * 512], lambda c, s=s: B_catT[c][s], B_bias)
                        kb.barrier()
                kb.barrier()

            if STOP_AFTER >= 4:
                with contextlib.ExitStack() as sc:
                    w_o = sbuf(sc, "w_o", [128, 8, 1024], BF16)
                    gfin = sbuf(sc, "gfin", [128, 1024], F32)
                    B_gfin = Buf()
                    d_gf = kb.dsem()
                    kb.dma(d_gf, lambda q: q.dma_start(out=gfin[:], in_=gfin_d), writes=[B_gfin])
                    B_wo = Buf()
                    wst = [(sbuf(sc, "wst%d" % i, [128, 1024], F32), Buf(), kb.dsem()) for i in range(4)]
                    xr = [(sbuf(sc, "xr%d" % i, [128, 1024], F32), Buf(), kb.dsem()) for i in range(3)]
                    hb = [(sbuf(sc, "hb%d" % i, [128, 1024], F32), Buf()) for i in range(2)]
                    yb = [(sbuf(sc, "yb%d" % i, [128, 1024], F32), Buf(), kb.dsem()) for i in range(2)]
                    yps = [(psum(sc, "yps%d" % i, [128, 1024], F32), PB()) for i in range(2)]
                    B_wo = [(Buf(), Buf()) for _ in range(8)]
                    for kc in range(8):
                        w, Bw, dw = wst[kc % 4]
                        kb.dma(dw, lambda q, w=w, kc=kc: q.dma_start(out=w[:], in_=w_out[kc * 128:(kc + 1) * 128, :]), writes=[Bw])
                        kb.op("act", lambda e, w=w, kc=kc: e.activation(out=w_o[:, kc, 0:512], in_=w[:, 0:512], func=AF.Identity),
                              reads=[Bw], writes=[B_wo[kc][0]])
                        kb.op("dve", lambda e, w=w, kc=kc: e.tensor_copy(out=w_o[:, kc, 512:1024], in_=w[:, 512:1024]),
                              reads=[Bw], writes=[B_wo[kc][1]])
                    blks = [(i, 128) for i in range(NB_OWN)] + [(NB_OWN, DEC)]

                    def load_x(i):
                        b, m = blks[i]
                        x_, Bx_, dx_ = xr[i % 3]
                        src = xall[b * 128:(b + 1) * 128, :] if b < NB_OWN else xs
                        kb.dma(dx_, lambda q: q.dma_start(out=x_[0:m, :], in_=src), writes=[Bx_])

                    load_x(0)
                    load_x(1)
                    fin5 = [None]
                    for i, (b, m) in enumerate(blks):
                        if i + 2 < len(blks):
                            load_x(i + 2)
                        x_, Bx_, dx_ = xr[i % 3]
                        yp, Byp = yps[i % 2]
                        h_, Bh = hb[i % 2]
                        y_, By, dy = yb[i % 2]
                        t, bb = divmod(b, 4)
                        for half in range(2):
                            for kc in range(8):
                                if b < NB_OWN:
                                    lhsT = catT[:, kc, b * 128:(b + 1) * 128]
                                    Bc = B_catT[kc][t]
                                else:
                                    lhsT = catT_s[:, kc, :]
                                    Bc = B_catTs[kc]
                                kb.op("pe", lambda e, lhsT=lhsT, kc=kc, half=half, yp=yp: e.matmul(
                                    yp[0:m, half * 512:(half + 1) * 512], lhsT=lhsT, rhs=w_o[:, kc, half * 512:(half + 1) * 512],
                                    start=(kc == 0), stop=(kc == 7)), reads=[Bc, *B_wo[kc]], writes=[Byp],
                                    mark=(kc == 7 and half == 1))
                        si = ss_idx[0] % 64
                        ss_idx[0] += 1
                        kb.op("dve", lambda e, h_=h_, yp=yp, x_=x_: e.tensor_tensor(out=h_[0:m, :], in0=yp[0:m, :], in1=x_[0:m, :],
                                                                                    op=ALU.add), reads=[Byp, Bx_], writes=[Bh])
                        kb.op("act", lambda e, h_=h_, si=si: e.activation(out=junk[0:m, :], in_=h_[0:m, :], func=AF.Square,
                                                                          accum_out=ss[0:m, si:si + 1]),
                              reads=[Bh], writes=[B_junk, B_ss[si]])
                        kb.op("pool", lambda e, si=si: e.tensor_scalar(out=rstd[0:m, si:si + 1], in0=ss[0:m, si:si + 1],
                                                                       scalar1=1.0 / 1024.0, scalar2=EPS, op0=ALU.mult,
                                                                       op1=ALU.add), reads=[B_ss[si]], writes=[B_ss[si]])
                        kb.op("pool", lambda e, si=si: e.tensor_tensor(out=rstd[0:m, si:si + 1], in0=rstd[0:m, si:si + 1],
                                                                       in1=negh[0:m, 0:1], op=ALU.pow),
                              reads=[B_ss[si]], writes=[B_ss[si]])
                        if fin5[0] is not None:
                            fin5[0]()

                        def fin(y_=y_, h_=h_, si=si, By=By, Bh=Bh, dy=dy, b=b, m=m):
                            kb.op("dve", lambda e: e.scalar_tensor_tensor(
                                out=y_[0:m, :], in0=h_[0:m, :], scalar=rstd[0:m, si:si + 1], in1=gfin[0:m, :], op0=ALU.mult,
                                op1=ALU.mult), reads=[Bh, B_ss[si], B_gfin], writes=[By])
                            dst = y_o[b * 128:(b + 1) * 128, :] if b < NB_OWN else ys_o
                            kb.dma(dy, lambda q: q.dma_start(out=dst, in_=y_[0:m, :]), reads=[By], q="pool")
                        fin5[0] = fin
                    fin5[0]()
                    kb.barrier()

        except StopBuild:
            pass

        for d in kb.dsems:
            if d.n > 0:
                kb._wait("sp", (d.sem, d.n, d.key))
    return nc


_NC = None


def _get_nc():
    global _NC
    if _NC is None:
        _NC = build_nc()
    return _NC


def kernel(x_prompt, x_sample, mem_prompt, cache_fox_k, cache_fox_v, cache_fox_logf, cache_mem_k, cache_mem_v,
           state_pool, g_norm, w_in, b_f, w_pool, pool_scale, g_mem, w_mem_kv, w_out, g_final):
    f = np.float32
    x_prompt = np.asarray(x_prompt, f)
    x_sample = np.asarray(x_sample, f)
    nc = _get_nc()
    wpl = np.zeros((128, 2, 64), f)
    wp = np.asarray(w_pool, f)[0]
    for cc in range(2):
        for e in range(2):
            wpl[e * 64:(e + 1) * 64, cc, :] = wp[2 * cc + e]
    common = {
        "w_in": np.ascontiguousarray(np.asarray(w_in, f)[0]),
        "w_out": np.ascontiguousarray(np.asarray(w_out, f)[0]),
        "w_mkv": np.ascontiguousarray(np.asarray(w_mem_kv, f)[0]),
        "gl": np.ascontiguousarray(np.asarray(g_norm, f)[0].reshape(8, 128).T),
        "gml": np.ascontiguousarray(np.asarray(g_mem, f)[0].reshape(8, 128).T),
        "gfin": np.ascontiguousarray(np.broadcast_to(np.asarray(g_final, f)[None, :], (128, 1024))),
        "bfb": np.ascontiguousarray(np.broadcast_to(np.asarray(b_f, f)[0][None, :], (128, 8))),
        "psc": np.ascontiguousarray(np.asarray(pool_scale, f)[0].reshape(2, 128).T),
        "wpl": wpl,
    }
    wins = (2, 4, 8, 16)
    in_maps = []
    for c in range(8):
        b, p = divmod(c, 2)
        xb = x_prompt[b].reshape(32, 128, 1024)
        own = xb[p::2]
        oth = xb[1 - p::2]
        xall = np.ascontiguousarray(np.concatenate([own, oth], 0).reshape(4096, 1024))
        halo = np.zeros((16, 16, 1024), f)
        for i in range(16):
            g0 = (2 * i + p) * 128
            if g0 >= 16:
                halo[i] = x_prompt[b, g0 - 16:g0]
        rc = np.zeros((128, 2, 16), f)
        for cc in range(2):
            for e in range(2):
                w = wins[2 * cc + e]
                if p == 0:
                    cnt = np.minimum(w, np.arange(16) + 1).astype(f)
                else:
                    cnt = np.full(16, w, f)
                rc[e * 64:(e + 1) * 64, cc, :] = (1.0 / cnt)[None, :]
        pqv = np.zeros((128, 3), f)
        pqv[:, 0] = p
        pqv[:, 1] = 1 - p
        pqv[:, 2] = 0.0 if p == 1 else -30000.0
        sp = np.asarray(state_pool, f)[0, c]
        spT = np.zeros((128, 2, 16), f)
        spT[:, :, 1:16] = sp.T.reshape(2, 128, 15).transpose(1, 0, 2)
        m = dict(common)
        m.update({
            "xall": xall, "xhalo": np.ascontiguousarray(halo.reshape(256, 1024)),
            "xmem": np.ascontiguousarray(np.asarray(mem_prompt, f)[b]),
            "xs": np.ascontiguousarray(x_sample[c]),
            "ck": np.ascontiguousarray(np.asarray(cache_fox_k, f)[0, c].reshape(2048, 512)),
            "cv": np.ascontiguousarray(np.asarray(cache_fox_v, f)[0, c].reshape(2048, 512)),
            "clf": np.ascontiguousarray(np.asarray(cache_fox_logf, f)[0, c]),
            "cmk": np.ascontiguousarray(np.asarray(cache_mem_k, f)[0, c].reshape(256, 256)),
            "cmv": np.ascontiguousarray(np.asarray(cache_mem_v, f)[0, c].reshape(256, 256)),
            "spT": spT, "pq": pqv, "rc": rc,
        })
        in_maps.append(m)
    res = run_bass_kernel_spmd(nc, in_maps, core_ids=list(range(8)))
    R = res.results
    y_p = np.zeros((4, 32, 128, 1024), f)
    k_p = np.zeros((4, 32, 128, 512), f)
    v_p = np.zeros((4, 32, 128, 512), f)
    lf_p = np.zeros((4, 32, 128, 8), f)
    mk_p = np.zeros((1, 4, 256, 4, 64), f)
    mv_p = np.zeros((1, 4, 256, 4, 64), f)
    pool_p = np.zeros((1, 4, 15, 256), f)
    y_s = np.zeros((8, DEC, 1024), f)
    k_s = np.zeros((1, 8, DEC, 8, 64), f)
    v_s = np.zeros((1, 8, DEC, 8, 64), f)
    lf_s = np.zeros((1, 8, DEC, 8), f)
    pool_s = np.zeros((1, 8, 15, 256), f)
    for c in range(8):
        b, p = divmod(c, 2)
        r = R[c]
        y_p[b, p::2] = r["y_o"].reshape(16, 128, 1024)
        k_p[b, p::2] = r["k_o"].reshape(16, 128, 512)
        v_p[b, p::2] = r["v_o"].reshape(16, 128, 512)
        lf_p[b, p::2] = r["lf_o"].reshape(16, 128, 8)
        if p == 1:
            mk_p[0, b] = r["mk_o"].reshape(256, 4, 64)
            mv_p[0, b] = r["mv_o"].reshape(256, 4, 64)
            po = r["pool_o"]
            pool_p[0, b] = po.transpose(1, 0, 2).reshape(256, 16)[:, 1:].T
        y_s[c] = r["ys_o"]
        k_s[0, c] = r["ks_o"].reshape(DEC, 8, 64)
        v_s[0, c] = r["vs_o"].reshape(DEC, 8, 64)
        lf_s[0, c] = r["lfs_o"]
        pool_s[0, c] = r["pools_o"].transpose(1, 0, 2).reshape(256, 16)[:, 1:].T
    return (y_p.reshape(4, 4096, 1024), y_s, k_p.reshape(1, 4, 4096, 8, 64), v_p.reshape(1, 4, 4096, 8, 64),
            lf_p.reshape(1, 4, 4096, 8), mk_p, mv_p, pool_p, k_s, v_s, lf_s, pool_s)
```

```python
import contextlib
import numpy as np
import concourse.bass as bass
import concourse.mybir as mybir
from concourse.bass_utils import run_bass_kernel_spmd

F32 = mybir.dt.float32
BF16 = mybir.dt.bfloat16
AF = mybir.ActivationFunctionType
ALU = mybir.AluOpType

NB_OWN = 16
NB_ALL = 32
NT = 4
S_OWN = 2048
DEC = 64
NCB = 16
EPS = 1e-6
STOP_AFTER = 99
import os
STOP_AT = float(os.environ.get('KSTOP', '99'))


class StopBuild(Exception):
    pass


class Buf:
    __slots__ = ("w", "r", "excl")

    def __init__(self, excl=False):
        self.w = None
        self.r = {}
        self.excl = excl


def PB():
    return Buf(excl=True)


class DSem:
    def __init__(self, sem, key):
        self.sem = sem
        self.key = key
        self.n = 0


class KB:
    def __init__(self, nc, es):
        self.nc = nc
        self.es = es
        self.engs = {"pe": nc.tensor, "act": nc.scalar, "dve": nc.vector, "pool": nc.gpsimd, "sp": nc.sync}
        self.sem = {k: es.enter_context(nc.semaphore("s_" + k)) for k in self.engs}
        self.cnt = {k: 0 for k in self.engs}
        self.seen = {k: {} for k in self.engs}
        self.pend = {k: ([], []) for k in self.engs}
        self.dsems = []
        self.nd = 0
        self.dead = False
        self.nops = 0
        self.limit = int(os.environ.get('KLIMIT', '0'))
        self.trace = []

    def dsem(self):
        self.nd += 1
        d = DSem(self.es.enter_context(self.nc.semaphore("d%d" % self.nd)), "d%d" % self.nd)
        self.dsems.append(d)
        return d

    def _wait(self, eng, ev):
        if ev is None:
            return
        sem, val, key = ev
        if self.seen[eng].get(key, 0) >= val:
            return
        self.seen[eng][key] = val
        self.engs[eng].wait_ge(sem, val)

    def _deps(self, eng, reads, writes):
        for b in reads:
            self._wait(eng, b.w)
        for b in writes:
            self._wait(eng, b.w)
            for ev in list(b.r.values()):
                self._wait(eng, ev)

    @staticmethod
    def _commit(ev, reads, writes):
        for b in reads:
            b.r[ev[2]] = ev
        for b in writes:
            b.w = ev
            b.r = {}

    def _tick(self, what):
        self.nops += 1
        if self.limit:
            self.trace.append(what)
        if self.limit and self.nops >= self.limit and (what[0] != "pe" or what[1]):
            self.barrier()
            self.dead = True
            print("KLIMIT stop at", self.nops, self.trace[-6:])

    def op(self, eng, fn, reads=(), writes=(), mark=True):
        if self.dead:
            return None
        if any(b.excl for b in reads):
            writes = list(writes) + [b for b in reads if b.excl]
            reads = [b for b in reads if not b.excl]
        self._deps(eng, reads, writes)
        ins = fn(self.engs[eng])
        pr, pw = self.pend[eng]
        pr.extend(reads)
        pw.extend(writes)
        if not mark:
            return None
        self.cnt[eng] += 1
        ins.then_inc(self.sem[eng], 1)
        ev = (self.sem[eng], self.cnt[eng], eng)
        self._commit(ev, pr, pw)
        self.pend[eng] = ([], [])
        import traceback
        self._tick((eng, mark, traceback.extract_stack(limit=3)[0].lineno))
        return ev

    def dma(self, ds, fn, reads=(), writes=(), q="sp"):
        if self.dead:
            return None
        self._deps(q, reads, writes)
        ins = fn(self.engs[q])
        ds.n += 16
        ins.then_inc(ds.sem, 16)
        ev = (ds.sem, ds.n, ds.key)
        self._commit(ev, reads, writes)
        import traceback
        self._tick(("dma", True, traceback.extract_stack(limit=3)[0].lineno))
        return ev

    def barrier(self):
        if self.dead:
            return
        evs = [(self.sem[k], self.cnt[k], k) for k in self.engs if self.cnt[k] > 0]
        for d in self.dsems:
            if d.n > 0:
                evs.append((d.sem, d.n, d.key))
        for eng in self.engs:
            for ev in evs:
                if ev[2] != eng:
                    self._wait(eng, ev)


class _View2:
    def __init__(self, full, shape):
        self.full = full
        self.shape = shape

    def __getitem__(self, idx):
        if not isinstance(idx, tuple):
            idx = (idx,)
        idx = idx + (slice(None),) * (2 - len(idx))
        p, f = idx
        p = _norm(p, self.shape[0])
        f = _norm(f, self.shape[1])
        return self.full[p, f]


class _View3:
    def __init__(self, full, shape):
        self.full = full
        self.shape = shape

    def __getitem__(self, idx):
        if not isinstance(idx, tuple):
            idx = (idx,)
        idx = idx + (slice(None),) * (3 - len(idx))
        p, a, b = idx
        P_, A, B = self.shape
        p = _norm(p, P_)
        v = self.full[p, 0:A * B].rearrange("p (a b) -> p a b", b=B)
        return v[:, a, b]


def _norm(sl, n):
    if isinstance(sl, slice):
        a = 0 if sl.start is None else sl.start
        b = n if sl.stop is None else sl.stop
        return slice(a, b)
    return sl


def build_nc():
    nc = bass.Bass("TRN2", target_bir_lowering=False)

    def din(name, shape):
        return nc.dram_tensor(name, list(shape), F32, kind="ExternalInput").ap()

    def dout(name, shape):
        return nc.dram_tensor(name, list(shape), F32, kind="ExternalOutput").ap()

    xall = din("xall", [4096, 1024])
    xhalo = din("xhalo", [256, 1024])
    xmem = din("xmem", [256, 1024])
    xs = din("xs", [DEC, 1024])
    ck = din("ck", [2048, 512])
    cv = din("cv", [2048, 512])
    clf = din("clf", [2048, 8])
    cmk = din("cmk", [256, 256])
    cmv = din("cmv", [256, 256])
    spT = din("spT", [128, 2, 16])
    w_in = din("w_in", [1024, 3080])
    w_out = din("w_out", [1024, 1024])
    w_mkv = din("w_mkv", [1024, 512])
    gl_d = din("gl", [128, 8])
    gml_d = din("gml", [128, 8])
    gfin_d = din("gfin", [128, 1024])
    bf_d = din("bfb", [128, 8])
    pq_d = din("pq", [128, 3])
    rc_d = din("rc", [128, 2, 16])
    psc_d = din("psc", [128, 2])
    wp_d = din("wpl", [128, 2, 64])

    y_o = dout("y_o", [S_OWN, 1024])
    k_o = dout("k_o", [S_OWN, 512])
    v_o = dout("v_o", [S_OWN, 512])
    lf_o = dout("lf_o", [S_OWN, 8])
    mk_o = dout("mk_o", [256, 256])
    mv_o = dout("mv_o", [256, 256])
    pool_o = dout("pool_o", [128, 2, 16])
    ys_o = dout("ys_o", [DEC, 1024])
    ks_o = dout("ks_o", [DEC, 512])
    vs_o = dout("vs_o", [DEC, 512])
    lfs_o = dout("lfs_o", [DEC, 8])
    pools_o = dout("pools_o", [128, 2, 16])

    with contextlib.ExitStack() as es:
        kb = KB(nc, es)

        def stop(level):
            if os.environ.get('KVERB'):
                print('stop point', level, 'nops', kb.nops)
            if STOP_AT <= level:
                kb.barrier()
                kb.dead = True

        uid = [0]

        def sbuf(st, name, shape, dt):
            uid[0] += 1
            return st.enter_context(nc.sbuf_tensor("sb%d_%s" % (uid[0], name), list(shape), dt))

        def psum(st, name, shape, dt):
            uid[0] += 1
            n = int(np.prod(shape[1:])) * (4 if dt == F32 else 2)
            assert n <= 2048 or n == 4096, (name, n)
            full = st.enter_context(nc.psum_tensor("ps%d_%s" % (uid[0], name), [128, (max(n, 2048)) // (4 if dt == F32 else 2)], dt))
            if len(shape) == 2:
                return _View2(full, shape)
            return _View3(full, shape)

        ident_bf = sbuf(es, "ident_bf", [128, 128], BF16)
        ident_f = sbuf(es, "ident_f", [128, 128], F32)
        tri_f = sbuf(es, "tri_f", [128, 128], F32)
        ones_f = sbuf(es, "ones_f", [128, 128], F32)
        ones_bf = sbuf(es, "ones_bf", [128, 128], BF16)
        sel = sbuf(es, "selz", [128, 8, 128], BF16)
        negh = sbuf(es, "negh", [128, 1], F32)
        bfb = sbuf(es, "bfb", [128, 8], F32)
        pq = sbuf(es, "pq", [128, 3], F32)
        rc = sbuf(es, "rc", [128, 2, 16], F32)
        psc = sbuf(es, "psc", [128, 2], F32)
        gl = sbuf(es, "gl", [128, 8], F32)
        gml = sbuf(es, "gml", [128, 8], F32)
        wp_f = sbuf(es, "wp_f", [128, 2, 64], F32)
        wp_bf = sbuf(es, "wp_bf", [128, 2, 64], BF16)
        catT = sbuf(es, "catT", [128, 8, S_OWN], BF16)
        QT = sbuf(es, "QTz", [128, 4, 2, S_OWN], BF16)
        Eq = sbuf(es, "Eqz", [128, S_OWN], BF16)
        w_kv = sbuf(es, "w_kv", [128, 8, 1032], BF16)
        catT_s = sbuf(es, "catT_s", [128, 8, DEC], BF16)
        QT_s = sbuf(es, "QTz_s", [128, 4, 2, DEC], BF16)
        Eq_s = sbuf(es, "Eqz_s", [128, DEC], BF16)
        xnT_s = sbuf(es, "xnT_s", [128, 8, DEC], BF16)
        ss = sbuf(es, "ss", [128, 64], F32)
        rstd = sbuf(es, "rstd", [128, 64], F32)
        junk = sbuf(es, "junk", [128, 1024], BF16)

        B_const = Buf()
        B_catT = [[Buf() for _ in range(NT)] for _ in range(8)]
        B_QT = [[Buf() for _ in range(NT)] for _ in range(4)]
        B_Eq = [Buf() for _ in range(NT)]
        B_wkv = [(Buf(), Buf()) for _ in range(8)]
        B_catTs = [Buf() for _ in range(8)]
        B_QTs = Buf()
        B_Eqs = Buf()
        B_xnTs = Buf()
        B_ss = [Buf() for _ in range(64)]
        B_junk = Buf()

        d_const = kb.dsem()
        B_cdma = Buf()
        nconst = 0
        for dst, src in ((gl, gl_d), (gml, gml_d), (bfb, bf_d), (pq, pq_d), (rc, rc_d), (psc, psc_d), (wp_f, wp_d)):
            ev_c = kb.dma(d_const, lambda q, dst=dst, src=src: q.dma_start(out=dst[:], in_=src))
            nconst += 1
        B_cdma.w = ev_c
        P = "pool"
        kb.op(P, lambda e: e.memset(negh[:], -0.5), writes=[B_const])
        kb.op(P, lambda e: e.memset(ident_f[:], 0.0), writes=[B_const])
        kb.op(P, lambda e: e.affine_select(out=ident_f[:], in_=ident_f[:], pattern=[[-1, 128]],
                                           compare_op=ALU.not_equal, fill=1.0, base=0, channel_multiplier=1),
              writes=[B_const])
        kb.op(P, lambda e: e.tensor_copy(out=ident_bf[:], in_=ident_f[:]), writes=[B_const])
        kb.op(P, lambda e: e.memset(ones_f[:], 1.0), writes=[B_const])
        kb.op(P, lambda e: e.memset(ones_bf[:], 1.0), writes=[B_const])
        kb.op(P, lambda e: e.memset(tri_f[:], 1.0), writes=[B_const])
        kb.op(P, lambda e: e.affine_select(out=tri_f[:], in_=tri_f[:], pattern=[[1, 128]], compare_op=ALU.is_ge,
                                           fill=0.0, base=0, channel_multiplier=-1), writes=[B_const])
        kb.op(P, lambda e: e.memset(sel[:], 0.0), writes=[B_const])
        kb.op(P, lambda e: e.affine_select(out=sel[0:8], in_=sel[0:8], pattern=[[1, 8], [0, 128]],
                                           compare_op=ALU.not_equal, fill=1.0, base=0, channel_multiplier=-1),
              writes=[B_const])
        kb.op(P, lambda e: e.tensor_copy(out=wp_bf[:], in_=wp_f[:]), reads=[B_cdma], writes=[B_const])
        zq = [b_ for row in B_QT for b_ in row]
        kb.op(P, lambda e: e.memset(QT_s[:], 0.0), writes=[B_QTs])
        kb.op(P, lambda e: e.memset(Eq_s[:], 0.0), writes=[B_Eqs])
        kb.op(P, lambda e: e.memset(Eq[:], 0.0), writes=list(B_Eq))
        kb.op(P, lambda e: e.memset(QT[:, :, :, 0:1024], 0.0), writes=zq)
        kb.op(P, lambda e: e.memset(QT[:, :, :, 1024:2048], 0.0), writes=zq)
        stop0 = True

        ss_idx = [0]

        def prep_parts(st, src_ap, m, dstT, dcol, Bdst, xt_ring, xs_ring, tp_ring):
            i = st["i"]
            st["i"] += 1
            xt, Bxt, dxt = xt_ring[i % len(xt_ring)]
            xsb, Bxs = xs_ring[i % len(xs_ring)]
            tp, Btp = tp_ring[i % len(tp_ring)]
            si = ss_idx[0] % 64
            ss_idx[0] += 1

            def fa():
                kb.dma(dxt, lambda q: q.dma_start(out=xt[0:m, :], in_=src_ap), writes=[Bxt])
                kb.op("act", lambda e: e.activation(out=junk[0:m, :], in_=xt[0:m, :], func=AF.Square,
                                                    accum_out=ss[0:m, si:si + 1]),
                      reads=[Bxt], writes=[B_junk, B_ss[si]])
                kb.op("pool", lambda e: e.tensor_scalar(out=rstd[0:m, si:si + 1], in0=ss[0:m, si:si + 1],
                                                        scalar1=1.0 / 1024.0, scalar2=EPS, op0=ALU.mult, op1=ALU.add),
                      reads=[B_ss[si]], writes=[B_ss[si]])
                kb.op("pool", lambda e: e.tensor_tensor(out=rstd[0:m, si:si + 1], in0=rstd[0:m, si:si + 1],
                                                        in1=negh[0:m, 0:1], op=ALU.pow),
                      reads=[B_ss[si]], writes=[B_ss[si]])
                kb.op("dve", lambda e: e.tensor_scalar(out=xsb[0:m, :], in0=xt[0:m, :], scalar1=rstd[0:m, si:si + 1],
                                                       scalar2=None, op0=ALU.mult),
                      reads=[Bxt, B_ss[si]], writes=[Bxs])

            def fb():
                for kc in range(8):
                    kb.op("pe", lambda e, kc=kc: e.transpose(tp[:, kc, 0:m], xsb[0:m, kc * 128:(kc + 1) * 128],
                                                             ident_bf[0:m, 0:m]),
                          reads=[Bxs], writes=[Btp], mark=(kc == 7))
                kb.op("dve", lambda e: e.tensor_copy(out=dstT[:, :, dcol:dcol + m], in_=tp[:, :, 0:m]),
                      reads=[Btp], writes=[Bdst])
            return fa, fb

        def prep_block(st, src_ap, m, dstT, dcol, Bdst, xt_ring, xs_ring, tp_ring):
            fa, fb = prep_parts(st, src_ap, m, dstT, dcol, Bdst, xt_ring, xs_ring, tp_ring)
            fa()
            fb()

        try:
            with contextlib.ExitStack() as sa:
                w_own = sbuf(sa, "w_own", [128, 8, 2048], BF16)
                B_wown = [(Buf(), Buf()) for _ in range(8)]
                mkT = sbuf(sa, "mkT", [128, 2, 256], BF16)
                mvb = sbuf(sa, "mvb", [128, 2, 256], BF16)
                mkT_s = sbuf(sa, "mkT_s", [128, 2, 256], BF16)
                mvb_s = sbuf(sa, "mvb_s", [128, 2, 256], BF16)
                B_mk = Buf()
                B_mks = Buf()
                qmT = sbuf(sa, "qmT", [128, 2, S_OWN], BF16)
                B_qmT = [[Buf() for _ in range(NT)] for _ in range(2)]
                qmT_s = sbuf(sa, "qmT_s", [128, 2, DEC], BF16)
                B_qmTs = Buf()
                uext = sbuf(sa, "uext", [128, 2, NB_OWN, 144], F32)
                B_uext = Buf()
                uext_s = sbuf(sa, "uext_s", [128, 2, 16 + DEC], F32)
                B_uexts = Buf()
                w_mk = sbuf(sa, "w_mk", [128, 8, 512], BF16)
                B_wmk = [(Buf(), Buf()) for _ in range(8)]
                tp_ring = [(psum(sa, "tp%d" % i, [128, 8, 128], BF16), PB()) for i in range(2)]
                mm_ring = [(psum(sa, "mm%d" % i, [128, 512], F32), PB()) for i in range(4)]
                pp_ring = [(psum(sa, "pp%d" % i, [128, 512], F32), PB()) for i in range(2)]
                mmi = [0]

                with contextlib.ExitStack() as sw:
                    wst = [(sbuf(sw, "wst%d" % i, [128, 772], F32), Buf(), kb.dsem()) for i in range(10)]
                    segs = ((0, 512, w_own, 0, B_wown), (512, 1536, w_kv, 0, B_wkv), (1536, 2048, w_own, 512, B_wown),
                            (2048, 2056, w_kv, 1024, B_wkv), (2056, 3080, w_own, 1024, B_wown))
                    parts = ((0, 768), (768, 1536), (1536, 2308), (2308, 3080))
                    li = 0
                    ci = 0
                    for kc in range(8):
                        for (pa, pb) in parts:
                            w, Bw, dw = wst[li % 10]
                            li += 1
                            kb.dma(dw, lambda q, w=w, kc=kc, pa=pa, pb=pb: q.dma_start(
                                out=w[:, 0:pb - pa], in_=w_in[kc * 128:(kc + 1) * 128, pa:pb]), writes=[Bw])
                            for (sa_, sb_, dst, d0, Bd) in segs:
                                lo, hi = max(pa, sa_), min(pb, sb_)
                                if lo >= hi:
                                    continue
                                o_ap = dst[:, kc, d0 + lo - sa_:d0 + hi - sa_]
                                i_ap = w[:, lo - pa:hi - pa]
                                if ci % 2 == 0 and hi - lo > 16:
                                    kb.op("act", lambda e, o_ap=o_ap, i_ap=i_ap, kc=kc: e.activation(
                                        out=o_ap, in_=i_ap, func=AF.Identity, scale=gl[:, kc:kc + 1]),
                                        reads=[Bw, B_cdma], writes=[Bd[kc][0]])
                                else:
                                    kb.op("dve", lambda e, o_ap=o_ap, i_ap=i_ap, kc=kc: e.tensor_scalar(
                                        out=o_ap, in0=i_ap, scalar1=gl[:, kc:kc + 1], scalar2=None, op0=ALU.mult),
                                        reads=[Bw, B_cdma], writes=[Bd[kc][1]])
                                ci += 1
                    for kc in range(8):
                        w, Bw, dw = wst[li % 10]
                        li += 1
                        kb.dma(dw, lambda q, w=w, kc=kc: q.dma_start(out=w[:, 0:512], in_=w_mkv[kc * 128:(kc + 1) * 128, :]),
                               writes=[Bw])
                        if kc % 2 == 0:
                            kb.op("act", lambda e, w=w, kc=kc: e.activation(
                                out=w_mk[:, kc, :], in_=w[:, 0:512], func=AF.Identity, scale=gml[:, kc:kc + 1]),
                                reads=[Bw, B_cdma], writes=[B_wmk[kc][0]])
                        else:
                            kb.op("dve", lambda e, w=w, kc=kc: e.tensor_scalar(
                                out=w_mk[:, kc, :], in0=w[:, 0:512], scalar1=gml[:, kc:kc + 1], scalar2=None, op0=ALU.mult),
                                reads=[Bw, B_cdma], writes=[B_wmk[kc][1]])
                    kb.barrier()
                sa1 = sa.enter_context(contextlib.ExitStack())
                xt_ring = []
                for i in range(2):
                    xt_ring.append((sbuf(sa1, "xt%d" % i, [128, 1024], F32), Buf(), kb.dsem()))
                xs_ring = [(sbuf(sa1, "xsb%d" % i, [128, 1024], BF16), Buf()) for i in range(2)]
                xnT_ring = [(sbuf(sa1, "xnT%d" % i, [128, 8, 512], BF16), Buf()) for i in range(2)]
                stop(0.2)

                pst = {"i": 0}

                def proj_fm(xnT, Bx, n, ccs, evac):
                    for cc in ccs:
                        ps, Bp = mm_ring[mmi[0] % 4]
                        mmi[0] += 1
                        for kc in range(8):
                            kb.op("pe", lambda e, kc=kc, cc=cc, ps=ps: e.matmul(
                                ps[:, 0:n], lhsT=w_own[:, kc, cc * 128:(cc + 1) * 128], rhs=xnT[:, kc, 0:n],
                                start=(kc == 0), stop=(kc == 7)), reads=[Bx, *B_wown[kc]], writes=[Bp], mark=(kc == 7))
                        evac(cc, ps, Bp)

                def evac_own(t):
                    def f(cc, ps, Bp):
                        cols = slice(t * 512, (t + 1) * 512)
                        if cc < 4:
                            kb.op("dve", lambda e: e.tensor_copy(out=QT[0:64, cc, 0, cols], in_=ps[0:64, :]),
                                  reads=[Bp], writes=[B_QT[cc][t]])
                            kb.op("dve", lambda e: e.tensor_copy(out=QT[64:128, cc, 1, cols], in_=ps[64:128, :]),
                                  reads=[Bp], writes=[B_QT[cc][t]])
                        elif cc < 8:
                            kb.op("act", lambda e: e.activation(out=catT[:, cc - 4, cols], in_=ps[:, :], func=AF.Silu),
                                  reads=[Bp], writes=[B_catT[cc - 4][t]])
                        elif cc < 10:
                            kb.op("dve", lambda e: e.tensor_copy(
                                out=uext[:, cc - 8, t * 4:(t + 1) * 4, 16:144],
                                in_=ps[:, :].rearrange("p (b k) -> p b k", k=128)), reads=[Bp], writes=[B_uext])
                        elif cc < 12:
                            kb.op("act", lambda e: e.activation(out=catT[:, cc - 6, cols], in_=ps[:, :], func=AF.Silu),
                                  reads=[Bp], writes=[B_catT[cc - 6][t]])
                        elif cc < 14:
                            kb.op("dve", lambda e: e.tensor_copy(out=qmT[:, cc - 12, cols], in_=ps[:, :]),
                                  reads=[Bp], writes=[B_qmT[cc - 12][t]])
                        else:
                            kb.op("act", lambda e: e.activation(out=catT[:, cc - 8, cols], in_=ps[:, :], func=AF.Silu),
                                  reads=[Bp], writes=[B_catT[cc - 8][t]])
                    return f

                def evac_halo(cc, ps, Bp):
                    kb.op("dve", lambda e: e.tensor_copy(out=uext[:, cc - 8, :, 0:16],
                                                         in_=ps[:, 0:256].rearrange("p (b k) -> p b k", k=16)),
                          reads=[Bp], writes=[B_uext])

                def evac_s(cc, ps, Bp):
                    n = DEC
                    if cc < 4:
                        kb.op("dve", lambda e: e.tensor_copy(out=QT_s[0:64, cc, 0, :], in_=ps[0:64, 0:n]), reads=[Bp], writes=[B_QTs])
                        kb.op("dve", lambda e: e.tensor_copy(out=QT_s[64:128, cc, 1, :], in_=ps[64:128, 0:n]), reads=[Bp], writes=[B_QTs])
                    elif cc < 8:
                        kb.op("act", lambda e: e.activation(out=catT_s[:, cc - 4, :], in_=ps[:, 0:n], func=AF.Silu),
                              reads=[Bp], writes=[B_catTs[cc - 4]])
                    elif cc < 10:
                        kb.op("dve", lambda e: e.tensor_copy(out=uext_s[:, cc - 8, 16:16 + n], in_=ps[:, 0:n]),
                              reads=[Bp], writes=[B_uexts])
                    elif cc < 12:
                        kb.op("act", lambda e: e.activation(out=catT_s[:, cc - 6, :], in_=ps[:, 0:n], func=AF.Silu),
                              reads=[Bp], writes=[B_catTs[cc - 6]])
                    elif cc < 14:
                        kb.op("dve", lambda e: e.tensor_copy(out=qmT_s[:, cc - 12, :], in_=ps[:, 0:n]),
                              reads=[Bp], writes=[B_qmTs])
                    else:
                        kb.op("act", lambda e: e.activation(out=catT_s[:, cc - 8, :], in_=ps[:, 0:n], func=AF.Silu),
                              reads=[Bp], writes=[B_catTs[cc - 8]])

                def prep_tile(src, nblk, blkrows, xnT, Bx):
                    for b in range(nblk):
                        prep_block(pst, src[b * blkrows:(b + 1) * blkrows, :], blkrows, xnT, b * blkrows, Bx,
                                   xt_ring, xs_ring, tp_ring)

                smk = sa1.enter_context(contextlib.ExitStack())
                mst = [(sbuf(smk, "mst%d" % i, [128, 512], F32), Buf(), kb.dsem()) for i in range(2)]
                mkb = [(sbuf(smk, "mkb%d" % i, [128, 256], BF16), Buf()) for i in range(2)]
                xnTm, Bxm = xnT_ring[0]
                prep_tile(xmem, 2, 128, xnTm, Bxm)
                for blk in range(2):
                    ps, Bp = mm_ring[mmi[0] % 4]
                    mmi[0] += 1
                    for kc in range(8):
                        kb.op("pe", lambda e, kc=kc, ps=ps, blk=blk: e.matmul(
                            ps[:, :], lhsT=xnTm[:, kc, blk * 128:(blk + 1) * 128], rhs=w_mk[:, kc, :],
                            start=(kc == 0), stop=(kc == 7)), reads=[Bxm, *B_wmk[kc]], writes=[Bp], mark=(kc == 7))
                    m, Bm, dm = mst[blk]
                    kbf, Bk = mkb[blk]
                    kb.op("act", lambda e, m=m, ps=ps: e.activation(out=m[:], in_=ps[:, :], func=AF.Identity),
                          reads=[Bp], writes=[Bm])
                    kb.op("dve", lambda e, kbf=kbf, ps=ps: e.tensor_copy(out=kbf[:], in_=ps[:, 0:256]),
                          reads=[Bp], writes=[Bk])
                    kb.op("dve", lambda e, ps=ps, blk=blk: e.tensor_copy(out=mvb[:, blk, :], in_=ps[:, 256:512]),
                          reads=[Bp], writes=[B_mk])
                    kb.dma(dm, lambda q, m=m, blk=blk: q.dma_start(out=mk_o[blk * 128:(blk + 1) * 128, :], in_=m[:, 0:256]),
                           reads=[Bm])
                    kb.dma(dm, lambda q, m=m, blk=blk: q.dma_start(out=mv_o[blk * 128:(blk + 1) * 128, :], in_=m[:, 256:512]),
                           reads=[Bm])
                    tp, Btp = tp_ring[blk % 2]
                    for cc in range(2):
                        kb.op("pe", lambda e, cc=cc, tp=tp, kbf=kbf: e.transpose(
                            tp[:, cc, :], kbf[:, cc * 128:(cc + 1) * 128], ident_bf[:]),
                            reads=[Bk, B_const], writes=[Btp], mark=(cc == 1))
                    kb.op("dve", lambda e, tp=tp, blk=blk: e.tensor_copy(out=mkT[:, :, blk * 128:(blk + 1) * 128],
                                                                          in_=tp[:, 0:2, :]),
                          reads=[Btp], writes=[B_mk])
                for blk in range(2):
                    m, Bm, dm = mst[blk]
                    kbf, Bk = mkb[blk]
                    kb.dma(dm, lambda q, m=m, blk=blk: q.dma_start(out=m[:, 0:256], in_=cmk[blk * 128:(blk + 1) * 128, :]),
                           writes=[Bm])
                    kb.dma(dm, lambda q, m=m, blk=blk: q.dma_start(out=m[:, 256:512], in_=cmv[blk * 128:(blk + 1) * 128, :]),
                           writes=[Bm])
                    kb.op("dve", lambda e, kbf=kbf, m=m: e.tensor_copy(out=kbf[:], in_=m[:, 0:256]), reads=[Bm], writes=[Bk])
                    kb.op("dve", lambda e, m=m, blk=blk: e.tensor_copy(out=mvb_s[:, blk, :], in_=m[:, 256:512]),
                          reads=[Bm], writes=[B_mks])
                    tp, Btp = tp_ring[blk % 2]
                    for cc in range(2):
                        kb.op("pe", lambda e, cc=cc, tp=tp, kbf=kbf: e.transpose(
                            tp[:, cc, :], kbf[:, cc * 128:(cc + 1) * 128], ident_bf[:]),
                            reads=[Bk, B_const], writes=[Btp], mark=(cc == 1))
                    kb.op("dve", lambda e, tp=tp, blk=blk: e.tensor_copy(out=mkT_s[:, :, blk * 128:(blk + 1) * 128],
                                                                          in_=tp[:, 0:2, :]),
                          reads=[Btp], writes=[B_mks])


                kb.barrier()
                smk.close()
                stop(0.3)
                d_misc = kb.dsem()
                kb.dma(d_misc, lambda q: q.dma_start(out=uext_s[:, :, 0:16], in_=spT), writes=[B_uexts])
                tiles = [("own", t) for t in range(NT)] + [("halo", 0), ("s", 0)]

                def prep_list(kind, t, slot):
                    xnT, Bx = xnT_ring[slot]
                    if kind == "own":
                        return [prep_parts(pst, xall[t * 512 + b * 128:t * 512 + (b + 1) * 128, :], 128, xnT, b * 128, Bx,
                                           xt_ring, xs_ring, tp_ring) for b in range(4)]
                    elif kind == "halo":
                        return [prep_parts(pst, xhalo[b * 128:(b + 1) * 128, :], 128, xnT, b * 128, Bx,
                                           xt_ring, xs_ring, tp_ring) for b in range(2)]
                    return [prep_parts(pst, xs, DEC, xnT_s, 0, B_xnTs, xt_ring, xs_ring, tp_ring)]

                for fa, fb in prep_list(tiles[0][0], tiles[0][1], 0):
                    fa()
                    fb()
                for i, (kind, t) in enumerate(tiles):
                    nxt = prep_list(tiles[i + 1][0], tiles[i + 1][1], (i + 1) % 2) if i + 1 < len(tiles) else []
                    xnT, Bx = xnT_ring[i % 2]
                    if kind == "own":
                        for g in range(4):
                            if g < len(nxt):
                                nxt[g][0]()
                            proj_fm(xnT, Bx, 512, range(4 * g, 4 * g + 3), evac_own(t))
                            if g < len(nxt):
                                nxt[g][1]()
                            proj_fm(xnT, Bx, 512, range(4 * g + 3, 4 * g + 4), evac_own(t))
                    elif kind == "halo":
                        for fa, fb in nxt:
                            fa()
                        proj_fm(xnT, Bx, 256, (8, 9), evac_halo)
                        for fa, fb in nxt:
                            fb()
                    else:
                        proj_fm(xnT_s, B_xnTs, DEC, range(16), evac_s)

                d_out = kb.dsem()
                kb.dma(d_out, lambda q: q.dma_start(out=pool_o, in_=uext[:, :, NB_OWN - 1, 128:144]), reads=[B_uext])
                kb.dma(d_out, lambda q: q.dma_start(out=pools_o, in_=uext_s[:, :, DEC:DEC + 16]), reads=[B_uexts])

                kb.barrier()
                sa1.close()
                stop(0.4)

                if True:
                    sp_ = sa.enter_context(contextlib.ExitStack())
                    pm_ops = []
                    pooledT = sbuf(sp_, "pooledT", [128, 2, S_OWN], BF16)
                    pooledT_s = sbuf(sp_, "pooledT_s", [128, 2, DEC], BF16)
                    B_pl = Buf()
                    B_pls = Buf()
                    a1 = sbuf(sp_, "a1", [128, NB_OWN, 144], F32)
                    a2 = sbuf(sp_, "a2", [128, NB_OWN, 144], F32)
                    tmpc = sbuf(sp_, "tmpc", [128, 16], F32)
                    B_a = Buf()
                    D = "dve"

                    def tt(out, in0, in1, op, reads, writes, eng=D):
                        pm_ops.append(lambda: kb.op(eng, lambda e: e.tensor_tensor(out=out, in0=in0, in1=in1, op=op),
                                                    reads=reads, writes=writes))

                    def stt(out, in0, sc, in1, reads, writes):
                        pm_ops.append(lambda: kb.op(D, lambda e: e.scalar_tensor_tensor(
                            out=out, in0=in0, scalar=sc, in1=in1, op0=ALU.mult, op1=ALU.subtract), reads=reads, writes=writes))

                    def pool_mix(u, nb, L, pooled, Bu, Bp, corr):
                        for cc in range(2):
                            uu = u[:, cc]
                            A1, A2, A3, A4 = (a[:, 0:nb, 0:L] for a in (a1, a2, a1, a2))
                            tt(A1[:, :, 1:L], uu[:, :, 1:L], uu[:, :, 0:L - 1], ALU.add, [Bu, Bp], [B_a])
                            tt(A2[:, :, 3:L], A1[:, :, 3:L], A1[:, :, 1:L - 2], ALU.add, [B_a], [B_a])
                            if cc == 0:
                                srcs = ((0, 64, A1, 0.5), (64, 128, A2, 0.25))
                            else:
                                tt(A3[:, :, 7:L], A2[:, :, 7:L], A2[:, :, 3:L - 4], ALU.add, [B_a], [B_a])
                                tt(A4[:, :, 15:L], A3[:, :, 15:L], A3[:, :, 7:L - 8], ALU.add, [B_a], [B_a])
                                srcs = ((0, 64, A3, 0.125), (64, 128, A4, 0.0625))
                            for (p0, p1, A, sc) in srcs:
                                stt(pooled[p0:p1, cc], A[p0:p1, :, 16:L], sc, uu[p0:p1, :, 16:L], [B_a, Bu], [Bp])
                                if corr:
                                    tt(tmpc[p0:p1, :], A[p0:p1, 0, 16:32], rc[p0:p1, cc, :], ALU.mult, [B_a, B_const], [B_a])
                                    tt(pooled[p0:p1, cc, 0, 0:16], tmpc[p0:p1, :], uu[p0:p1, 0, 16:32], ALU.subtract,
                                       [B_a, Bu], [Bp])

                    pool_mix(uext, NB_OWN, 144, pooledT[:, :, :].rearrange("p c (b k) -> p c b k", k=128), B_uext, B_pl, True)
                    pool_mix(uext_s[:, :, :].rearrange("p c (b k) -> p c b k", b=1), 1, 16 + DEC,
                             pooledT_s[:, :, :].rearrange("p c (b k) -> p c b k", b=1), B_uexts, B_pls, False)

                    def pool_mm(pl, Bpl, n, cat_ap, Bcat):
                        for cc in range(2):
                            ps, Bp = mm_ring[mmi[0] % 4]
                            mmi[0] += 1
                            for e_ in range(2):
                                r = slice(64 * e_, 64 * e_ + 64)
                                kb.op("pe", lambda e, r=r, cc=cc, ps=ps: e.matmul(
                                    ps[r, 0:n], lhsT=wp_bf[r, cc, :], rhs=pl(cc, r), start=True, stop=True),
                                    reads=[Bpl, B_const], writes=[Bp], mark=(e_ == 1))
                            kb.op("dve", lambda e, cc=cc, ps=ps: e.scalar_tensor_tensor(
                                out=cat_ap(cc), in0=ps[:, 0:n], scalar=psc[:, cc:cc + 1], in1=cat_ap(cc), op0=ALU.mult,
                                op1=ALU.mult), reads=[Bp, B_const], writes=[Bcat(cc)])

                    def all_pool_mm():
                        for t in range(NT):
                            cols = slice(t * 512, (t + 1) * 512)
                            pool_mm(lambda cc, r, cols=cols: pooledT[r, cc, cols], B_pl, 512,
                                    lambda cc, cols=cols: catT[:, 4 + cc, cols], lambda cc, t=t: B_catT[4 + cc][t])
                        pool_mm(lambda cc, r: pooledT_s[r, cc, :], B_pls, DEC, lambda cc: catT_s[:, 4 + cc, :],
                                lambda cc: B_catTs[4 + cc])

                stop(0.5)
                with contextlib.ExitStack() as sm:
                    PTm = [(sbuf(sm, "PTm%d" % i, [128, 512], BF16), Buf()) for i in range(3)]
                    recs = [(sbuf(sm, "recm%d" % i, [128, 512], F32), sbuf(sm, "tmpm%d" % i, [128, 512], F32), Buf()) for i in range(2)]
                    pti = [0]

                    cci = [0]
                    epi_pending = []

                    def mem_attn(n, mkT_, mvb_, Bmk_, q_ap, Bq, cat_ap, Bcat):
                        for cc in range(2):
                            if cci[0] % 2 == 0:
                                (O, BO), (SU, BS) = pp_ring[0], pp_ring[1]
                            else:
                                (O, BO), (SU, BS) = mm_ring[2], mm_ring[3]
                            cci[0] += 1
                            units = [(e_, jb) for e_ in range(2) for jb in range(2)]

                            def qk(k):
                                e_, jb = units[k]
                                r = slice(64 * e_, 64 * e_ + 64)
                                S, BSc = mm_ring[k % 2]
                                kb.op("pe", lambda e: e.matmul(
                                    S[:, 0:n], lhsT=mkT_[r, cc, jb * 128:(jb + 1) * 128], rhs=q_ap(cc, r),
                                    start=True, stop=True), reads=[Bmk_, Bq(cc)], writes=[BSc])

                            qk(0)
                            for k, (e_, jb) in enumerate(units):
                                if k + 1 < len(units):
                                    qk(k + 1)
                                S, BSc = mm_ring[k % 2]
                                r = slice(64 * e_, 64 * e_ + 64)
                                hm = 2 * cc + e_
                                PT, BPT = PTm[pti[0] % 3]
                                pti[0] += 1
                                kb.op("act", lambda e, S=S, PT=PT: e.activation(out=PT[:, 0:n], in_=S[:, 0:n], func=AF.Exp,
                                                                                scale=0.125), reads=[BSc], writes=[BPT])
                                if k == 1 and epi_pending:
                                    epi_pending.pop(0)()
                                kb.op("pe", lambda e, O=O, r=r, jb=jb, hm=hm, PT=PT: e.matmul(
                                    O[r, 0:n], lhsT=mvb_[:, jb, hm * 64:(hm + 1) * 64], rhs=PT[:, 0:n],
                                    start=(jb == 0), stop=(jb == 1)), reads=[Bmk_, BPT], writes=[BO], mark=False)
                                kb.op("pe", lambda e, SU=SU, r=r, jb=jb, PT=PT: e.matmul(
                                    SU[r, 0:n], lhsT=ones_bf[:, 0:64], rhs=PT[:, 0:n],
                                    start=(jb == 0), stop=(jb == 1)), reads=[BPT, B_const], writes=[BS])
                            rm, tm, Brm = recs[cci[0] % 2]

                            def epi(SU=SU, BS=BS, O=O, BO=BO, rm=rm, tm=tm, Brm=Brm, cc=cc, n=n, cat_ap=cat_ap, Bcat=Bcat):
                                kb.op("act", lambda e: e.activation(out=rm[:, 0:n], in_=SU[:, 0:n], func=AF.Ln),
                                      reads=[BS], writes=[Brm])
                                kb.op("act", lambda e: e.activation(out=rm[:, 0:n], in_=rm[:, 0:n], func=AF.Exp, scale=-1.0),
                                      reads=[Brm], writes=[Brm])
                                kb.op("dve", lambda e: e.tensor_tensor(out=tm[:, 0:n], in0=O[:, 0:n], in1=rm[:, 0:n],
                                                                       op=ALU.mult), reads=[BO, Brm], writes=[Brm])
                                kb.op("dve", lambda e: e.tensor_tensor(out=cat_ap(cc), in0=tm[:, 0:n], in1=cat_ap(cc),
                                                                       op=ALU.mult), reads=[Brm], writes=[Bcat(cc)])
                            epi_pending.append(epi)

                    per = (len(pm_ops) + 4) // 5

                    def drain(k):
                        for _ in range(min(k, len(pm_ops))):
                            pm_ops.pop(0)()

                    drain(per)
                    for t in range(NT):
                        cols = slice(t * 512, (t + 1) * 512)
                        mem_attn(512, mkT, mvb, B_mk, lambda cc, r, cols=cols: qmT[r, cc, cols], lambda cc, t=t: B_qmT[cc][t],
                                 lambda cc, cols=cols: catT[:, 6 + cc, cols], lambda cc, t=t: B_catT[6 + cc][t])
                        drain(per)
                    mem_attn(DEC, mkT_s, mvb_s, B_mks, lambda cc, r: qmT_s[r, cc, :], lambda cc: B_qmTs,
                             lambda cc: catT_s[:, 6 + cc, :], lambda cc: B_catTs[6 + cc])
                    while epi_pending:
                        epi_pending.pop(0)()
                    drain(len(pm_ops))
                    all_pool_mm()
                    kb.barrier()
                sp_.close()
                kb.barrier()

            bi = [0]

            def ingest(R, m, ksrc, Bks, vsrc, Bvs, KT_ap, BK, V_ap, BV):
                i = bi[0]
                kbf, Bkb = R["kbf"][i % 2]
                kb.op("dve", lambda e: e.tensor_copy(out=kbf[0:m, :], in_=ksrc), reads=[Bks], writes=[Bkb])
                kb.op("dve", lambda e: e.tensor_copy(out=V_ap, in_=vsrc), reads=[Bvs], writes=[BV])
                ktp, Bkt = R["ktp"][0]

                def tail():
                    for c in range(4):
                        kb.op("pe", lambda e, c=c: e.transpose(ktp[:, c, 0:m], kbf[0:m, c * 128:(c + 1) * 128],
                                                               ident_bf[0:m, 0:m]),
                              reads=[Bkb, B_const], writes=[Bkt], mark=(c == 3))
                    kb.op("act", lambda e: e.activation(out=KT_ap, in_=ktp[:, :, 0:m], func=AF.Identity),
                          reads=[Bkt], writes=[BK])
                return tail

            def kv_block(R, xnT, Bx, col, m, zf_ap, B_zf, kout, vout, KT_ap, BK, V_ap, BV, mid=None):
                i = bi[0]
                kps, Bkp = R["kps"][i % 2]
                vps, Bvp = R["vps"][i % 2]
                zps, Bzp = R["zps"][0]
                for (ps, Bp, c0, n) in ((kps, Bkp, 0, 512), (vps, Bvp, 512, 512), (zps, Bzp, 1024, 8)):
                    for kc in range(8):
                        kb.op("pe", lambda e, kc=kc, ps=ps, c0=c0, n=n: e.matmul(
                            ps[0:m, 0:n], lhsT=xnT[:, kc, col:col + m], rhs=w_kv[:, kc, c0:c0 + n],
                            start=(kc == 0), stop=(kc == 7)), reads=[Bx, *B_wkv[kc]], writes=[Bp], mark=(kc == 7))
                    if mid is not None and c0 == 0:
                        mid()
                kb.op("dve", lambda e: e.tensor_tensor(out=zf_ap, in0=zps[0:m, :], in1=bfb[0:m, :], op=ALU.add),
                      reads=[Bzp, B_const], writes=[B_zf])
                tail = ingest(R, m, kps[0:m, :], Bkp, vps[0:m, :], Bvp, KT_ap, BK, V_ap, BV)
                if kout is not None:
                    ks_, Bk_, dk_ = R["kst"][i % len(R["kst"])]
                    vs_, Bv_, dv_ = R["vst"][i % len(R["vst"])]
                    kb.op("act", lambda e: e.activation(out=ks_[0:m, :], in_=kps[0:m, :], func=AF.Identity),
                          reads=[Bkp], writes=[Bk_])
                    kb.op("act", lambda e: e.activation(out=vs_[0:m, :], in_=vps[0:m, :], func=AF.Identity),
                          reads=[Bvp], writes=[Bv_])
                    kb.dma(dk_, lambda q: q.dma_start(out=kout, in_=ks_[0:m, :]), reads=[Bk_], q="pool")
                    kb.dma(dv_, lambda q: q.dma_start(out=vout, in_=vs_[0:m, :]), reads=[Bv_], q="pool")
                bi[0] += 1
                return tail

            def kv_res(st, depth=2):
                return dict(
                    kst=[(sbuf(st, "kst%d" % i, [128, 512], F32), Buf(), kb.dsem()) for i in range(depth)],
                    vst=[(sbuf(st, "vst%d" % i, [128, 512], F32), Buf(), kb.dsem()) for i in range(depth)],
                    kbf=[(sbuf(st, "kbf%d" % i, [128, 512], BF16), Buf()) for i in range(2)],
                    kps=[(psum(st, "kps%d" % i, [128, 512], F32), PB()) for i in range(2)],
                    vps=[(psum(st, "vps%d" % i, [128, 512], F32), PB()) for i in range(2)],
                    zps=[(psum(st, "zps%d" % i, [128, 8], F32), PB()) for i in range(1)],
                    ktp=[(psum(st, "ktp%d" % i, [128, 4, 128], BF16), PB()) for i in range(1)])

            def logf_chain(st, zf, B_zf, m, nb):
                ez = sbuf(st, "ez", [128, nb, 8], F32)
                B_ez = Buf()
                kb.op("act", lambda e: e.activation(out=ez[0:m], in_=zf[0:m], func=AF.Exp, scale=-1.0),
                      reads=[B_zf], writes=[B_ez])
                kb.op("act", lambda e: e.activation(out=ez[0:m], in_=ez[0:m], func=AF.Ln, bias=1.0, scale=1.0),
                      reads=[B_ez], writes=[B_ez])
                return ez, B_ez

            def scan_incl(bufs, B_f, src, n, dst):
                cur = src
                d = 1
                k = 0
                while d < n:
                    nxt = bufs[k % 2][:, 0:n, :]
                    kb.op("dve", lambda e, nxt=nxt, cur=cur, d=d: e.tensor_copy(out=nxt[:, 0:d, :], in_=cur[:, 0:d, :]),
                          reads=[B_f], writes=[B_f])
                    kb.op("dve", lambda e, nxt=nxt, cur=cur, d=d: e.tensor_tensor(
                        out=nxt[:, d:n, :], in0=cur[:, d:n, :], in1=cur[:, 0:n - d, :], op=ALU.add),
                        reads=[B_f], writes=[B_f])
                    cur = nxt
                    d *= 2
                    k += 1
                kb.op("dve", lambda e: e.tensor_copy(out=dst, in_=cur), reads=[B_f], writes=[B_f])

            pti = [0]
            pairi = [0]

            def attend(R, N, qcol0, QT_, BQ, Eq_, BEq, blocks, cat_ap, Bcat, B_bias):
                Sb, Ob, Ub, PTr, comb, rec, tmpo, B_rec = R
                for c in range(4):
                    nb = len(blocks)

                    def qk(j):
                        bk = blocks[j]
                        nk, c0 = bk["nk"], bk["c0"]
                        for e_ in range(2):
                            h = 2 * c + e_
                            S, BS = Sb[2 * (j % 2) + e_]
                            kb.op("pe", lambda e, S=S, e_=e_: e.matmul(
                                S[0:nk, c0:N], lhsT=bk["KT"](c), rhs=QT_[:, c, e_, qcol0 + c0:qcol0 + N],
                                start=True, stop=False), reads=[bk["BK"], BQ(c)], writes=[BS], mark=False)
                            kb.op("pe", lambda e, S=S, h=h: e.matmul(
                                S[0:nk, c0:N], lhsT=sel[:, h, 0:nk], rhs=Eq_[:, qcol0 + c0:qcol0 + N],
                                start=False, stop=True), reads=[BEq, B_const], writes=[BS])

                    qk(0)
                    for j in range(nb):
                        if j + 1 < nb:
                            qk(j + 1)
                        bk = blocks[j]
                        nk, c0 = bk["nk"], bk["c0"]
                        for e_ in range(2):
                            h = 2 * c + e_
                            S, BS = Sb[2 * (j % 2) + e_]
                            O, BO = Ob[e_]
                            U, BU = Ub[e_]
                            PT, BPT = PTr[pti[0] % len(PTr)]
                            pti[0] += 1
                            w = min(128, N - c0)
                            if bk["fix"] == "p":
                                kb.op("act", lambda e, S=S, PT=PT, h=h: e.activation(
                                    out=PT[0:nk, c0:c0 + w], in_=S[0:nk, c0:c0 + w], func=AF.Exp, bias=bk["biasp"](h), scale=0.125),
                                    reads=[BS, B_bias], writes=[BPT])
                                if c0 + w < N:
                                    kb.op("act", lambda e, S=S, PT=PT, h=h: e.activation(
                                        out=PT[0:nk, c0 + w:N], in_=S[0:nk, c0 + w:N], func=AF.Exp, bias=bk["bias"](h), scale=0.125),
                                        reads=[BS, B_bias], writes=[BPT])
                            else:
                                kb.op("act", lambda e, S=S, PT=PT, h=h: e.activation(
                                    out=PT[0:nk, c0:N], in_=S[0:nk, c0:N], func=AF.Exp, bias=bk["bias"](h), scale=0.125),
                                    reads=[BS, B_bias], writes=[BPT])
                            if bk["fix"] == "tri":
                                kb.op("pool", lambda e, PT=PT, w=w: e.affine_select(
                                    out=PT[0:nk, c0:c0 + w], in_=PT[0:nk, c0:c0 + w], pattern=[[1, w]],
                                    compare_op=ALU.is_ge, fill=0.0, base=0, channel_multiplier=-1),
                                    reads=[BPT], writes=[BPT])
                            kb.op("pe", lambda e, O=O, PT=PT, j=j: e.matmul(
                                O[:, c0:N], lhsT=bk["V"](c), rhs=PT[0:nk, c0:N], start=(j == 0), stop=(j == nb - 1)),
                                reads=[bk["BV"], BPT], writes=[BO], mark=False)
                            kb.op("pe", lambda e, U=U, PT=PT, j=j: e.matmul(
                                U[:, c0:N], lhsT=ones_bf[0:nk, :], rhs=PT[0:nk, c0:N], start=(j == 0),
                                stop=(j == nb - 1)), reads=[BPT, B_const], writes=[BU])
                    oc, uc, Bc4 = comb[pairi[0] % 2]
                    pairi[0] += 1
                    kb.op("dve", lambda e: e.tensor_copy(out=uc[0:64, 0:N], in_=Ub[0][0][0:64, 0:N]),
                          reads=[Ub[0][1]], writes=[Bc4[0]])
                    kb.op("act", lambda e: e.activation(out=uc[64:128, 0:N], in_=Ub[1][0][64:128, 0:N], func=AF.Identity),
                          reads=[Ub[1][1]], writes=[Bc4[1]])
                    kb.op("dve", lambda e: e.tensor_copy(out=oc[0:64, 0:N], in_=Ob[0][0][0:64, 0:N]),
                          reads=[Ob[0][1]], writes=[Bc4[2]])
                    kb.op("act", lambda e: e.activation(out=oc[64:128, 0:N], in_=Ob[1][0][64:128, 0:N], func=AF.Identity),
                          reads=[Ob[1][1]], writes=[Bc4[3]])
                    kb.op("dve", lambda e: e.reciprocal(out=rec[:, 0:N], in_=uc[:, 0:N]), reads=[Bc4[0], Bc4[1]], writes=[B_rec])
                    kb.op("dve", lambda e: e.tensor_tensor(out=tmpo[:, 0:N], in0=oc[:, 0:N], in1=rec[:, 0:N],
                                                           op=ALU.mult), reads=[Bc4[2], Bc4[3], B_rec], writes=[B_rec])
                    kb.op("dve", lambda e, c=c: e.tensor_tensor(out=cat_ap(c), in0=tmpo[:, 0:N], in1=cat_ap(c),
                                                                op=ALU.mult), reads=[B_rec], writes=[Bcat(c)])

            def attn_res(st):
                Sb = [(psum(st, "Sb%d" % i, [128, 512], F32), PB()) for i in range(4)]
                Ob = [(psum(st, "Ob%d" % i, [128, 512], F32), PB()) for i in range(2)]
                Ub = [(psum(st, "Ub%d" % i, [128, 512], F32), PB()) for i in range(2)]
                PTr = [(sbuf(st, "PT%d" % i, [128, 512], BF16), Buf()) for i in range(6)]
                comb = [(sbuf(st, "oc%d" % i, [128, 512], F32), sbuf(st, "uc%d" % i, [128, 512], F32), [Buf() for _ in range(4)]) for i in range(2)]
                rec = sbuf(st, "rec", [128, 512], F32)
                tmpo = sbuf(st, "tmpo", [128, 512], F32)
                return (Sb, Ob, Ub, PTr, comb, rec, tmpo, Buf())

            stop(0.6)
            with contextlib.ExitStack() as sb0:
                KTs = sbuf(sb0, "KTs", [128, 4, DEC], BF16)
                Vs = sbuf(sb0, "Vs", [128, 512], BF16)
                B_KTs = Buf()
                B_Vs = Buf()
                KTc = sbuf(sb0, "KTc", [128, 4, 2048], BF16)
                Vc = sbuf(sb0, "Vc", [128, NCB, 512], BF16)
                B_KTc = [Buf() for _ in range(NCB)]
                B_Vc = [Buf() for _ in range(NCB)]
                zfs = sbuf(sb0, "zfs", [128, 1, 8], F32)
                B_zfs = Buf()
                lfs = sbuf(sb0, "lfs", [128, NCB + 1, 8], F32)
                B_lfs = Buf()
                bias_s = sbuf(sb0, "bias_s", [128, NCB + 1, 8], F32)
                B_bias_s = Buf()
                d_lf = kb.dsem()
                kb.op("pool", lambda e: e.memset(lfs[:, NCB, :], 0.0), writes=[B_lfs])
                kb.dma(d_lf, lambda q: q.dma_start(out=lfs[:, 0:NCB, :], in_=clf.rearrange("(b k) h -> k b h", k=128)),
                       writes=[B_lfs])
                with contextlib.ExitStack() as s1:
                    R = kv_res(s1, depth=6)
                    kv_block(R, xnT_s, B_xnTs, 0, DEC, zfs[0:DEC, 0, :], B_zfs, ks_o, vs_o, KTs[:, :, :], B_KTs, Vs[0:DEC, :], B_Vs)()

                    ld_sems = [(kb.dsem(), kb.dsem()) for _ in range(6)]

                    def cload(cb):
                        ks_, Bk_, _ = R["kst"][(cb + 1) % 6]
                        vs_, Bv_, _ = R["vst"][(cb + 1) % 6]
                        dk_, dv_ = ld_sems[(cb + 1) % 6]
                        rows = slice(cb * 128, (cb + 1) * 128)
                        kb.dma(dk_, lambda q: q.dma_start(out=ks_[:], in_=ck[rows, :]), writes=[Bk_])
                        kb.dma(dv_, lambda q: q.dma_start(out=vs_[:], in_=cv[rows, :]), writes=[Bv_])

                    for cb in range(4):
                        cload(cb)
                    ptail = None
                    for cb in range(NCB):
                        if cb + 4 < NCB:
                            cload(cb + 4)
                        ks_, Bk_, dk_ = R["kst"][(cb + 1) % 6]
                        vs_, Bv_, dv_ = R["vst"][(cb + 1) % 6]
                        rows = slice(cb * 128, (cb + 1) * 128)
                        t_ = ingest(R, 128, ks_[:, :], Bk_, vs_[:, :], Bv_, KTc[:, :, rows], B_KTc[cb], Vc[:, cb, :], B_Vc[cb])
                        if ptail is not None:
                            ptail()
                        ptail = t_
                        bi[0] += 1
                    ptail()
                    kb.barrier()
                stop(1.1)
                with contextlib.ExitStack() as sf:
                    css = sbuf(sf, "css", [128, NCB + 1, 8], F32)
                    tots = sbuf(sf, "tots", [128, NCB + 1, 8], F32)
                    pres = sbuf(sf, "pres", [128, NCB + 1, 8], F32)
                    Fs = sbuf(sf, "Fs", [128, NCB + 1, 8], F32)
                    sa_ = sbuf(sf, "sa_", [128, 32, 8], F32)
                    sb2 = sbuf(sf, "sb2", [128, 32, 8], F32)
                    B_f = Buf()
                    ps1 = psum(sf, "ps1", [128, 512], F32)
                    ps2 = psum(sf, "ps2", [128, 512], F32)
                    ps3 = psum(sf, "ps3", [8, 512], F32)
                    B_p1, B_p2, B_p3 = PB(), PB(), PB()
                    D = "dve"
                    ez, B_ez = logf_chain(sf, zfs, B_zfs, DEC, 1)
                    kb.op(D, lambda e: e.tensor_scalar(out=lfs[0:DEC, NCB, :], in0=ez[0:DEC, 0, :], scalar1=-1.0,
                                                       scalar2=None, op0=ALU.mult), reads=[B_ez], writes=[B_lfs])
                    kb.dma(d_out, lambda q: q.dma_start(out=lfs_o, in_=lfs[0:DEC, NCB, :]), reads=[B_lfs])
                    n_s = (NCB + 1) * 8
                    lfs2 = lfs[:, :, :].rearrange("p b h -> p (b h)")
                    kb.op("pe", lambda e: e.matmul(ps1[:, 0:n_s], lhsT=tri_f[:], rhs=lfs2, start=True, stop=True),
                          reads=[B_lfs, B_const], writes=[B_p1])
                    kb.op("pe", lambda e: e.matmul(ps2[:, 0:n_s], lhsT=ones_f[:], rhs=lfs2, start=True, stop=True),
                          reads=[B_lfs, B_const], writes=[B_p2])
                    kb.op(D, lambda e: e.tensor_copy(out=css[:, :, :].rearrange("p b h -> p (b h)"), in_=ps1[:, 0:n_s]),
                          reads=[B_p1], writes=[B_f])
                    kb.op(D, lambda e: e.tensor_copy(out=tots[:, :, :].rearrange("p b h -> p (b h)"), in_=ps2[:, 0:n_s]),
                          reads=[B_p2], writes=[B_f])
                    scan_incl([sa_, sb2], B_f, tots[:, :, :], NCB + 1, pres[:, :, :])
                    kb.op(D, lambda e: e.tensor_tensor(out=Fs[:], in0=pres[:], in1=css[:], op=ALU.add), reads=[B_f], writes=[B_f])
                    kb.op(D, lambda e: e.tensor_tensor(out=Fs[:], in0=Fs[:], in1=tots[:], op=ALU.subtract),
                          reads=[B_f], writes=[B_f])
                    kb.op(D, lambda e: e.tensor_tensor(out=bias_s[:], in0=pres[:, NCB, :].unsqueeze(1).to_broadcast(
                        [128, NCB + 1, 8]), in1=Fs[:], op=ALU.subtract), reads=[B_f], writes=[B_bias_s])
                    kb.op("pe", lambda e: e.transpose(ps3[:, 0:128], Fs[:, NCB, :], ident_f[:]),
                          reads=[B_f, B_const], writes=[B_p3])
                    ft = sbuf(sf, "ft", [8, 512], F32)
                    B_ft = Buf()
                    kb.op(D, lambda e: e.tensor_copy(out=ft[:, 0:128], in_=ps3[:, 0:128]), reads=[B_p3], writes=[B_ft])
                    kb.op(D, lambda e: e.tensor_scalar(out=Eq_s[0:8, :], in0=ft[:, 0:DEC], scalar1=ft[:, DEC - 1:DEC],
                                                       scalar2=8.0, op0=ALU.subtract, op1=ALU.mult),
                          reads=[B_ft], writes=[B_Eqs])
                    kb.barrier()
                stop(1.2)
                with contextlib.ExitStack() as s3:
                    RA = attn_res(s3)
                    blocks = []
                    for cb in range(NCB):
                        rows = slice(cb * 128, (cb + 1) * 128)
                        blocks.append(dict(nk=128, c0=0, KT=lambda c, rows=rows: KTc[:, c, rows], BK=B_KTc[cb],
                                           V=lambda c, cb=cb: Vc[:, cb, c * 128:(c + 1) * 128], BV=B_Vc[cb],
                                           bias=lambda h, cb=cb: bias_s[:, cb, h:h + 1], fix=None))
                    blocks.append(dict(nk=DEC, c0=0, KT=lambda c: KTs[:, c, :], BK=B_KTs,
                                       V=lambda c: Vs[0:DEC, c * 128:(c + 1) * 128], BV=B_Vs,
                                       bias=lambda h: bias_s[0:DEC, NCB, h:h + 1], fix="tri"))
                    attend(RA, DEC, 0, QT_s, lambda c: B_QTs, Eq_s, B_Eqs, blocks, lambda c: catT_s[:, c, :],
                           lambda c: B_catTs[c], B_bias_s)
                    kb.barrier()
                kb.barrier()

            stop(1.5)
            with contextlib.ExitStack() as sb_:
                KT = sbuf(sb_, "KT", [128, 4, 4096], BF16)
                V = sbuf(sb_, "V", [128, NB_ALL, 512], BF16)
                B_KT = [Buf() for _ in range(NB_ALL)]
                B_V = [Buf() for _ in range(NB_ALL)]
                zfb = sbuf(sb_, "zfb", [128, NB_ALL, 8], F32)
                B_zf = Buf()
                B_lf = Buf()
                B_bias = Buf()
                with contextlib.ExitStack() as s1:
                    xt_ring = [(sbuf(s1, "xt%d" % i, [128, 1024], F32), Buf(), kb.dsem()) for i in range(3)]
                    xs_ring = [(sbuf(s1, "xsb%d" % i, [128, 1024], BF16), Buf()) for i in range(2)]
                    xnT_ring = [(sbuf(s1, "xnT%d" % i, [128, 8, 128], BF16), Buf()) for i in range(4)]
                    tp_ring = [(psum(s1, "tp%d" % i, [128, 8, 128], BF16), PB()) for i in range(2)]
                    R = kv_res(s1)
                    pst = {"i": 0}

                    def prep_tile1(src, nblk, xnT, Bx):
                        for b in range(nblk):
                            prep_block(pst, src[b * 128:(b + 1) * 128, :], 128, xnT, b * 128, Bx, xt_ring, xs_ring, tp_ring)

                    def pp(i):
                        xnT, Bx = xnT_ring[i % 4]
                        return prep_parts(pst, xall[i * 128:(i + 1) * 128, :], 128, xnT, 0, Bx, xt_ring, xs_ring, tp_ring)

                    parts = [pp(i) for i in range(NB_ALL)]
                    parts[0][0]()
                    parts[1][0]()
                    parts[0][1]()
                    prev_tail = [None]
                    for blk in range(NB_ALL):
                        xnT, Bx = xnT_ring[blk % 4]
                        own = blk < NB_OWN
                        rows = slice(blk * 128, (blk + 1) * 128)
                        if blk + 2 < NB_ALL:
                            parts[blk + 2][0]()
                        def mid(blk=blk, prev=prev_tail):
                            if prev[0] is not None:
                                prev[0]()
                            if blk + 1 < NB_ALL:
                                parts[blk + 1][1]()
                        prev_tail[0] = None if False else prev_tail[0]
                        t_ = kv_block(R, xnT, Bx, 0, 128, zfb[:, blk, :], B_zf, k_o[rows, :] if own else None,
                                      v_o[rows, :] if own else None, KT[:, :, rows], B_KT[blk], V[:, blk, :], B_V[blk],
                                      mid=mid)
                        prev_tail = [t_]
                    prev_tail[0]()
                    kb.barrier()

                lf = sbuf(sb_, "lf", [128, NB_ALL, 8], F32)
                bias = sbuf(sb_, "bias", [128, NT, NB_ALL, 8], F32)
                biasp = sbuf(sb_, "biasp", [128, NT, 4, 8], F32)
                if STOP_AFTER >= 2:
                    with contextlib.ExitStack() as sf:
                        cs = sbuf(sf, "cs", [128, NB_ALL, 8], F32)
                        tot = sbuf(sf, "tot", [128, NB_ALL, 8], F32)
                        pre = sbuf(sf, "pre", [128, NB_ALL, 8], F32)
                        Ff = sbuf(sf, "Ff", [128, NB_ALL, 8], F32)
                        sa_ = sbuf(sf, "sa_", [128, 32, 8], F32)
                        sb2 = sbuf(sf, "sb2", [128, 32, 8], F32)
                        Rr = sbuf(sf, "Rr", [128, 8], F32)
                        ft = sbuf(sf, "ft", [8, 512], F32)
                        B_ft = Buf()
                        B_f = Buf()
                        ps1 = psum(sf, "ps1", [128, 512], F32)
                        ps2 = psum(sf, "ps2", [128, 512], F32)
                        ps3 = psum(sf, "ps3", [8, 512], F32)
                        B_p1, B_p2, B_p3 = PB(), PB(), PB()
                        D = "dve"
                        ez, B_ez = logf_chain(sf, zfb, B_zf, 128, NB_ALL)
                        kb.op(D, lambda e: e.tensor_scalar(out=lf[:], in0=ez[:], scalar1=-1.0, scalar2=None,
                                                           op0=ALU.mult), reads=[B_ez], writes=[B_lf])
                        kb.dma(d_out, lambda q: q.dma_start(out=lf_o.rearrange("(b k) h -> k b h", k=128), in_=lf[:, 0:NB_OWN, :]),
                               reads=[B_lf])
                        lf2 = lf[:, :, :].rearrange("p b h -> p (b h)")
                        kb.op("pe", lambda e: e.matmul(ps1[:, 0:256], lhsT=tri_f[:], rhs=lf2, start=True, stop=True),
                              reads=[B_lf, B_const], writes=[B_p1])
                        kb.op("pe", lambda e: e.matmul(ps2[:, 0:256], lhsT=ones_f[:], rhs=lf2, start=True, stop=True),
                              reads=[B_lf, B_const], writes=[B_p2])
                        kb.op(D, lambda e: e.tensor_copy(out=cs[:, :, :].rearrange("p b h -> p (b h)"), in_=ps1[:, 0:256]),
                              reads=[B_p1], writes=[B_f])
                        kb.op(D, lambda e: e.tensor_copy(out=tot[:, :, :].rearrange("p b h -> p (b h)"), in_=ps2[:, 0:256]),
                              reads=[B_p2], writes=[B_f])
                        kb.op(D, lambda e: e.tensor_tensor(out=Ff[:, 0:16, :], in0=tot[:, 0:16, :], in1=tot[:, 16:32, :],
                                                           op=ALU.add), reads=[B_f], writes=[B_f])
                        scan_incl([sa_, sb2], B_f, Ff[:, 0:16, :], 16, pre[:, 16:32, :])
                        kb.op(D, lambda e: e.tensor_tensor(out=pre[:, 16:32, :], in0=pre[:, 16:32, :], in1=Ff[:, 0:16, :],
                                                           op=ALU.subtract), reads=[B_f], writes=[B_f])
                        kb.op(D, lambda e: e.scalar_tensor_tensor(out=pre[:, 0:16, :], in0=tot[:, 16:32, :], scalar=pq[:, 0:1],
                                                                  in1=pre[:, 16:32, :], op0=ALU.mult, op1=ALU.add),
                              reads=[B_f, B_const], writes=[B_f])
                        kb.op(D, lambda e: e.scalar_tensor_tensor(out=pre[:, 16:32, :], in0=tot[:, 0:16, :], scalar=pq[:, 1:2],
                                                                  in1=pre[:, 16:32, :], op0=ALU.mult, op1=ALU.add),
                              reads=[B_f, B_const], writes=[B_f])
                        kb.op(D, lambda e: e.tensor_tensor(out=Ff[:], in0=pre[:], in1=cs[:], op=ALU.add), reads=[B_f], writes=[B_f])
                        for s in range(NT):
                            lb = 4 * s + 3
                            kb.op(D, lambda e, lb=lb: e.tensor_tensor(out=Rr[:], in0=pre[:, lb, :], in1=tot[:, lb, :], op=ALU.add),
                                  reads=[B_f], writes=[B_f])
                            kb.op(D, lambda e, s=s: e.tensor_tensor(out=bias[:, s, :, :],
                                                                    in0=Rr[:, :].unsqueeze(1).to_broadcast([128, NB_ALL, 8]),
                                                                    in1=Ff[:], op=ALU.subtract), reads=[B_f], writes=[B_bias])
                            kb.op(D, lambda e, s=s: e.tensor_scalar(out=biasp[:, s, :, :], in0=bias[:, s, NB_OWN + 4 * s:NB_OWN + 4 * s + 4, :],
                                                                    scalar1=pq[:, 2:3], scalar2=None, op0=ALU.add),
                                  reads=[B_bias, B_const], writes=[B_bias])
                            for b in range(4):
                                kb.op("pe", lambda e, s=s, b=b: e.transpose(ps3[:, b * 128:(b + 1) * 128], Ff[:, 4 * s + b, :],
                                                                            ident_f[:]),
                                      reads=[B_f, B_const], writes=[B_p3], mark=(b == 3))
                            kb.op(D, lambda e: e.tensor_copy(out=ft[:, :], in_=ps3[:, :]), reads=[B_p3], writes=[B_ft])
                            kb.op(D, lambda e, s=s: e.tensor_scalar(out=Eq[0:8, s * 512:(s + 1) * 512], in0=ft[:, :],
                                                                    scalar1=ft[:, 511:512], scalar2=8.0, op0=ALU.subtract,
                                                                    op1=ALU.mult), reads=[B_ft], writes=[B_Eq[s]])
                        kb.barrier()

                if STOP_AFTER >= 3:
                    with contextlib.ExitStack() as s3:
                        RA = attn_res(s3)

                        def mkblk(s, blk, c0, fix):
                            rows = slice(blk * 128, (blk + 1) * 128)
                            return dict(nk=128, c0=c0, KT=lambda c: KT[:, c, rows], BK=B_KT[blk],
                                        V=lambda c: V[:, blk, c * 128:(c + 1) * 128], BV=B_V[blk],
                                        bias=lambda h: bias[:, s, blk, h:h + 1],
                                        biasp=lambda h: biasp[:, s, blk - NB_OWN - 4 * s, h:h + 1], fix=fix)

                        for s in range(NT):
                            blocks = []
                            for i in range(4 * s):
                                blocks.append(mkblk(s, i, 0, None))
                            for jj in range(4):
                                blocks.append(mkblk(s, 4 * s + jj, 128 * jj, "tri"))
                            for i in range(4 * s):
                                blocks.append(mkblk(s, NB_OWN + i, 0, None))
                            for jj in range(4):
                                blocks.append(mkblk(s, NB_OWN + 4 * s + jj, 128 * jj, "p"))
                            attend(RA, 512, s * 512, QT, lambda c, s=s: B_QT[c][s], Eq, B_Eq[s], blocks,
                                   lambda c, s=s: catT[:, c, s * 512:(s + 1) * 512], lambda c, s=s: B_catT[c][s], B_bias)
                        kb.barrier()
                kb.barrier()

            if STOP_AFTER >= 4:
                with contextlib.ExitStack() as sc:
                    w_o = sbuf(sc, "w_o", [128, 8, 1024], BF16)
                    gfin = sbuf(sc, "gfin", [128, 1024], F32)
                    B_gfin = Buf()
                    d_gf = kb.dsem()
                    kb.dma(d_gf, lambda q: q.dma_start(out=gfin[:], in_=gfin_d), writes=[B_gfin])
                    B_wo = Buf()
                    wst = [(sbuf(sc, "wst%d" % i, [128, 1024], F32), Buf(), kb.dsem()) for i in range(4)]
                    xr = [(sbuf(sc, "xr%d" % i, [128, 1024], F32), Buf(), kb.dsem()) for i in range(3)]
                    hb = [(sbuf(sc, "hb%d" % i, [128, 1024], F32), Buf()) for i in range(2)]
                    yb = [(sbuf(sc, "yb%d" % i, [128, 1024], F32), Buf(), kb.dsem()) for i in range(2)]
                    yps = [(psum(sc, "yps%d" % i, [128, 1024], F32), PB()) for i in range(2)]
                    B_wo = [(Buf(), Buf()) for _ in range(8)]
                    for kc in range(8):
                        w, Bw, dw = wst[kc % 4]
                        kb.dma(dw, lambda q, w=w, kc=kc: q.dma_start(out=w[:], in_=w_out[kc * 128:(kc + 1) * 128, :]), writes=[Bw])
                        kb.op("act", lambda e, w=w, kc=kc: e.activation(out=w_o[:, kc, 0:512], in_=w[:, 0:512], func=AF.Identity),
                              reads=[Bw], writes=[B_wo[kc][0]])
                        kb.op("dve", lambda e, w=w, kc=kc: e.tensor_copy(out=w_o[:, kc, 512:1024], in_=w[:, 512:1024]),
                              reads=[Bw], writes=[B_wo[kc][1]])
                    blks = [(i, 128) for i in range(NB_OWN)] + [(NB_OWN, DEC)]

                    def load_x(i):
                        b, m = blks[i]
                        x_, Bx_, dx_ = xr[i % 3]
                        src = xall[b * 128:(b + 1) * 128, :] if b < NB_OWN else xs
                        kb.dma(dx_, lambda q: q.dma_start(out=x_[0:m, :], in_=src), writes=[Bx_])

                    load_x(0)
                    load_x(1)
                    fin5 = [None]
                    for i, (b, m) in enumerate(blks):
                        if i + 2 < len(blks):
                            load_x(i + 2)
                        x_, Bx_, dx_ = xr[i % 3]
                        yp, Byp = yps[i % 2]
                        h_, Bh = hb[i % 2]
                        y_, By, dy = yb[i % 2]
                        t, bb = divmod(b, 4)
                        for half in range(2):
                            for kc in range(8):
                                if b < NB_OWN:
                                    lhsT = catT[:, kc, b * 128:(b + 1) * 128]
                                    Bc = B_catT[kc][t]
                                else:
                                    lhsT = catT_s[:, kc, :]
                                    Bc = B_catTs[kc]
                                kb.op("pe", lambda e, lhsT=lhsT, kc=kc, half=half, yp=yp: e.matmul(
                                    yp[0:m, half * 512:(half + 1) * 512], lhsT=lhsT, rhs=w_o[:, kc, half * 512:(half + 1) * 512],
                                    start=(kc == 0), stop=(kc == 7)), reads=[Bc, *B_wo[kc]], writes=[Byp],
                                    mark=(kc == 7 and half == 1))
                        si = ss_idx[0] % 64
                        ss_idx[0] += 1
                        kb.op("dve", lambda e, h_=h_, yp=yp, x_=x_: e.tensor_tensor(out=h_[0:m, :], in0=yp[0:m, :], in1=x_[0:m, :],
                                                                                    op=ALU.add), reads=[Byp, Bx_], writes=[Bh])
                        kb.op("act", lambda e, h_=h_, si=si: e.activation(out=junk[0:m, :], in_=h_[0:m, :], func=AF.Square,
                                                                          accum_out=ss[0:m, si:si + 1]),
                              reads=[Bh], writes=[B_junk, B_ss[si]])
                        kb.op("pool", lambda e, si=si: e.tensor_scalar(out=rstd[0:m, si:si + 1], in0=ss[0:m, si:si + 1],
                                                                       scalar1=1.0 / 1024.0, scalar2=EPS, op0=ALU.mult,
                                                                       op1=ALU.add), reads=[B_ss[si]], writes=[B_ss[si]])
                        kb.op("pool", lambda e, si=si: e.tensor_tensor(out=rstd[0:m, si:si + 1], in0=rstd[0:m, si:si + 1],
                                                                       in1=negh[0:m, 0:1], op=ALU.pow),
                              reads=[B_ss[si]], writes=[B_ss[si]])
                        if fin5[0] is not None:
                            fin5[0]()

                        def fin(y_=y_, h_=h_, si=si, By=By, Bh=Bh, dy=dy, b=b, m=m):
                            kb.op("dve", lambda e: e.scalar_tensor_tensor(
                                out=y_[0:m, :], in0=h_[0:m, :], scalar=rstd[0:m, si:si + 1], in1=gfin[0:m, :], op0=ALU.mult,
                                op1=ALU.mult), reads=[Bh, B_ss[si], B_gfin], writes=[By])
                            dst = y_o[b * 128:(b + 1) * 128, :] if b < NB_OWN else ys_o
                            kb.dma(dy, lambda q: q.dma_start(out=dst, in_=y_[0:m, :]), reads=[By], q="pool")
                        fin5[0] = fin
                    fin5[0]()
                    kb.barrier()

        except StopBuild:
            pass

        for d in kb.dsems:
            if d.n > 0:
                kb._wait("sp", (d.sem, d.n, d.key))
    return nc


_NC = None


def _get_nc():
    global _NC
    if _NC is None:
        _NC = build_nc()
    return _NC


def kernel(x_prompt, x_sample, mem_prompt, cache_fox_k, cache_fox_v, cache_fox_logf, cache_mem_k, cache_mem_v,
           state_pool, g_norm, w_in, b_f, w_pool, pool_scale, g_mem, w_mem_kv, w_out, g_final):
    f = np.float32
    x_prompt = np.asarray(x_prompt, f)
    x_sample = np.asarray(x_sample, f)
    nc = _get_nc()
    wpl = np.zeros((128, 2, 64), f)
    wp = np.asarray(w_pool, f)[0]
    for cc in range(2):
        for e in range(2):
            wpl[e * 64:(e + 1) * 64, cc, :] = wp[2 * cc + e]
    common = {
        "w_in": np.ascontiguousarray(np.asarray(w_in, f)[0]),
        "w_out": np.ascontiguousarray(np.asarray(w_out, f)[0]),
        "w_mkv": np.ascontiguousarray(np.asarray(w_mem_kv, f)[0]),
        "gl": np.ascontiguousarray(np.asarray(g_norm, f)[0].reshape(8, 128).T),
        "gml": np.ascontiguousarray(np.asarray(g_mem, f)[0].reshape(8, 128).T),
        "gfin": np.ascontiguousarray(np.broadcast_to(np.asarray(g_final, f)[None, :], (128, 1024))),
        "bfb": np.ascontiguousarray(np.broadcast_to(np.asarray(b_f, f)[0][None, :], (128, 8))),
        "psc": np.ascontiguousarray(np.asarray(pool_scale, f)[0].reshape(2, 128).T),
        "wpl": wpl,
    }
    wins = (2, 4, 8, 16)
    in_maps = []
    for c in range(8):
        b, p = divmod(c, 2)
        xb = x_prompt[b].reshape(32, 128, 1024)
        own = xb[p::2]
        oth = xb[1 - p::2]
        xall = np.ascontiguousarray(np.concatenate([own, oth], 0).reshape(4096, 1024))
        halo = np.zeros((16, 16, 1024), f)
        for i in range(16):
            g0 = (2 * i + p) * 128
            if g0 >= 16:
                halo[i] = x_prompt[b, g0 - 16:g0]
        rc = np.zeros((128, 2, 16), f)
        for cc in range(2):
            for e in range(2):
                w = wins[2 * cc + e]
                if p == 0:
                    cnt = np.minimum(w, np.arange(16) + 1).astype(f)
                else:
                    cnt = np.full(16, w, f)
                rc[e * 64:(e + 1) * 64, cc, :] = (1.0 / cnt)[None, :]
        pqv = np.zeros((128, 3), f)
        pqv[:, 0] = p
        pqv[:, 1] = 1 - p
        pqv[:, 2] = 0.0 if p == 1 else -30000.0
        sp = np.asarray(state_pool, f)[0, c]
        spT = np.zeros((128, 2, 16), f)
        spT[:, :, 1:16] = sp.T.reshape(2, 128, 15).transpose(1, 0, 2)
        m = dict(common)
        m.update({
            "xall": xall, "xhalo": np.ascontiguousarray(halo.reshape(256, 1024)),
            "xmem": np.ascontiguousarray(np.asarray(mem_prompt, f)[b]),
            "xs": np.ascontiguousarray(x_sample[c]),
            "ck": np.ascontiguousarray(np.asarray(cache_fox_k, f)[0, c].reshape(2048, 512)),
            "cv": np.ascontiguousarray(np.asarray(cache_fox_v, f)[0, c].reshape(2048, 512)),
            "clf": np.ascontiguousarray(np.asarray(cache_fox_logf, f)[0, c]),
            "cmk": np.ascontiguousarray(np.asarray(cache_mem_k, f)[0, c].reshape(256, 256)),
            "cmv": np.ascontiguousarray(np.asarray(cache_mem_v, f)[0, c].reshape(256, 256)),
            "spT": spT, "pq": pqv, "rc": rc,
        })
        in_maps.append(m)
    res = run_bass_kernel_spmd(nc, in_maps, core_ids=list(range(8)))
    R = res.results
    y_p = np.zeros((4, 32, 128, 1024), f)
    k_p = np.zeros((4, 32, 128, 512), f)
    v_p = np.zeros((4, 32, 128, 512), f)
    lf_p = np.zeros((4, 32, 128, 8), f)
    mk_p = np.zeros((1, 4, 256, 4, 64), f)
    mv_p = np.zeros((1, 4, 256, 4, 64), f)
    pool_p = np.zeros((1, 4, 15, 256), f)
    y_s = np.zeros((8, DEC, 1024), f)
    k_s = np.zeros((1, 8, DEC, 8, 64), f)
    v_s = np.zeros((1, 8, DEC, 8, 64), f)
    lf_s = np.zeros((1, 8, DEC, 8), f)
    pool_s = np.zeros((1, 8, 15, 256), f)
    for c in range(8):
        b, p = divmod(c, 2)
        r = R[c]
        y_p[b, p::2] = r["y_o"].reshape(16, 128, 1024)
        k_p[b, p::2] = r["k_o"].reshape(16, 128, 512)
        v_p[b, p::2] = r["v_o"].reshape(16, 128, 512)
        lf_p[b, p::2] = r["lf_o"].reshape(16, 128, 8)
        if p == 1:
            mk_p[0, b] = r["mk_o"].reshape(256, 4, 64)
            mv_p[0, b] = r["mv_o"].reshape(256, 4, 64)
            po = r["pool_o"]
            pool_p[0, b] = po.transpose(1, 0, 2).reshape(256, 16)[:, 1:].T
        y_s[c] = r["ys_o"]
        k_s[0, c] = r["ks_o"].reshape(DEC, 8, 64)
        v_s[0, c] = r["vs_o"].reshape(DEC, 8, 64)
        lf_s[0, c] = r["lfs_o"]
        pool_s[0, c] = r["pools_o"].transpose(1, 0, 2).reshape(256, 16)[:, 1:].T
    return (y_p.reshape(4, 4096, 1024), y_s, k_p.reshape(1, 4, 4096, 8, 64), v_p.reshape(1, 4, 4096, 8, 64),
            lf_p.reshape(1, 4, 4096, 8), mk_p, mv_p, pool_p, k_s, v_s, lf_s, pool_s)
```
